# Optimizing a Trainium2 kernel written in Bass

```python
import jax, jax.numpy as jnp
from jax import lax
import numpy as np

D_MODEL = 2048
BATCH = 4
SEQ = 2048
DEPTH = 1
DEC_BATCH = 32
DEC_SEQ = 1
PAST_LEN = 8192
PAGE_SIZE = 128

HEAD_DIM = 128
N_HEADS_ATTN = 6
N_HEADS_HGRN = 6
N_HEADS_MEM = 4
W_ATTN = N_HEADS_ATTN * HEAD_DIM
W_HGRN = N_HEADS_HGRN * HEAD_DIM
W_MEM = N_HEADS_MEM * HEAD_DIM
MIX_WIDTH = W_ATTN + W_HGRN + W_MEM
HGRN_EXPAND = HEAD_DIM
WINDOWS = (128, 512, 2048)
DILATIONS = (1, 4, 16)
MAX_WINDOW = max(WINDOWS)
N_MEM = 256
ROPE_THETA = 10000.0
HGRN_CHUNK = 64
LN_EPS = 1e-5
RMS_EPS = 1e-6
NEG_INF = -1e30
DEEPNORM_ALPHA = (2 * DEPTH) ** 0.25
DEEPNORM_BETA = (8 * DEPTH) ** -0.25
IN_WIDTHS = (W_ATTN,) * 4 + (W_HGRN,) * 4 + (W_MEM,) * 2
IN_PROJ = sum(IN_WIDTHS)

kernel_name = 'dilated_hgrn2_memory_hybrid_step'

F32 = jnp.float32


def _heads(a):
    return a.reshape(a.shape[:-1] + (-1, HEAD_DIM))


def _split_cols(proj):
    offsets = np.cumsum(IN_WIDTHS)[:-1].tolist()
    return jnp.split(proj, offsets, axis=-1)


def _rope(x, pos):
    half = HEAD_DIM // 2
    inv_freq = 1.0 / (ROPE_THETA ** (jnp.arange(half, dtype=F32) / half))
    ang = pos.astype(F32)[:, None] * inv_freq[None, :]
    cos = jnp.cos(ang)[None, :, None, :]
    sin = jnp.sin(ang)[None, :, None, :]
    xf = x.astype(F32)
    x1, x2 = xf[..., :half], xf[..., half:]
    return jnp.concatenate([x1 * cos - x2 * sin, x2 * cos + x1 * sin], axis=-1).astype(x.dtype)


def _dilated_prompt(q, k, v, window, dil):
    bsz, seq, nh, hd = q.shape
    blk = window // dil
    unit = dil * blk
    s_pad = -(-seq // unit) * unit
    m_len = s_pad // dil
    nb = m_len // blk
    pad = ((0, 0), (0, s_pad - seq), (0, 0), (0, 0))

    def to_blocks(a):
        a = jnp.pad(a.astype(F32), pad).reshape(bsz, m_len, dil, nh, hd)
        return a.transpose(0, 2, 3, 1, 4).reshape(bsz, dil, nh, nb, blk, hd)

    def with_prev(a):
        prev = jnp.pad(a, ((0, 0), (0, 0), (0, 0), (1, 0), (0, 0), (0, 0)))[:, :, :, :-1]
        return jnp.concatenate([prev, a], axis=4)

    qb = to_blocks(q)
    kb = with_prev(to_blocks(k))
    vb = with_prev(to_blocks(v))
    s = jnp.einsum('brhnqd,brhnkd->brhnqk', qb, kb) * hd ** -0.5
    qi = jnp.arange(blk)[:, None]
    ki = jnp.arange(2 * blk)[None, :]
    dist = qi + blk - ki
    band = (dist >= 0) & (dist <= blk)
    valid = band[None] & ((jnp.arange(nb)[:, None, None] > 0) | (ki[None] >= blk))
    s = jnp.where(valid, s, NEG_INF)
    m = jnp.max(s, axis=-1, keepdims=True)
    p = jnp.exp(s - m)
    den = jnp.sum(p, axis=-1)
    o = jnp.einsum('brhnqk,brhnkd->brhnqd', p, vb) / den[..., None]
    lse = m[..., 0] + jnp.log(den)
    o = o.reshape(bsz, dil, nh, m_len, hd).transpose(0, 3, 1, 2, 4).reshape(bsz, s_pad, nh, hd)[:, :seq]
    lse = lse.reshape(bsz, dil, nh, m_len).transpose(0, 3, 1, 2).reshape(bsz, s_pad, nh)[:, :seq]
    return o, lse


def _dilated_sample(q, k_all, v_all, window, dil):
    n_new = q.shape[1]
    past = k_all.shape[1] - n_new
    steps = window // dil
    idx = (past + jnp.arange(n_new))[:, None] - dil * jnp.arange(steps + 1)[None, :]
    valid = idx >= 0
    idx = jnp.maximum(idx, 0)
    kg = k_all[:, idx].astype(F32)
    vg = v_all[:, idx].astype(F32)
    s = jnp.einsum('blhd,blnhd->blhn', q.astype(F32), kg) * q.shape[-1] ** -0.5
    s = jnp.where(valid[None, :, None, :], s, NEG_INF)
    m = jnp.max(s, axis=-1, keepdims=True)
    p = jnp.exp(s - m)
    den = jnp.sum(p, axis=-1)
    o = jnp.einsum('blhn,blnhd->blhd', p, vg) / den[..., None]
    return o, m[..., 0] + jnp.log(den)


def _combine_dilations(outs, lses):
    wts = jax.nn.softmax(jnp.stack(lses), axis=0)
    return jnp.sum(wts[..., None] * jnp.stack(outs), axis=0)


def _hgrn_inputs(qb, fb, ib, lb):
    f = lb + (1.0 - lb) * jax.nn.sigmoid(fb.astype(F32))
    return (_heads(qb.astype(F32)), _heads(1.0 - f), _heads(ib.astype(F32)), _heads(jnp.log(f)))


def _hgrn_chunked(q, k, v, log_f, s0):
    bsz, n_tok, nh, kd = q.shape
    vd = v.shape[-1]
    c = min(HGRN_CHUNK, n_tok)
    n_chunks = -(-n_tok // c)
    l_pad = n_chunks * c
    pad = ((0, 0), (0, l_pad - n_tok), (0, 0), (0, 0))

    def chunks(a):
        a = jnp.pad(a, pad).reshape(bsz, n_chunks, c, nh, a.shape[-1])
        return a.transpose(1, 0, 3, 2, 4)

    tri = jnp.tril(jnp.ones((c, c), dtype=bool))

    def step(state, inp):
        qc, kc, vc, gc = inp
        b = jnp.cumsum(gc, axis=2)
        rel = b[:, :, :, None, :] - b[:, :, None, :, :]
        decay = jnp.where(tri[None, None, :, :, None], jnp.exp(jnp.minimum(rel, 0.0)), 0.0)
        att = jnp.einsum('bhtk,bhsk,bhtsk->bhts', qc, kc, decay)
        o = jnp.einsum('bhts,bhsv->bhtv', att, vc) + jnp.einsum('bhtk,bhkv->bhtv', qc * jnp.exp(b), state)
        b_last = b[:, :, -1, :]
        new_state = jnp.exp(b_last)[..., None] * state + jnp.einsum(
            'bhsk,bhsv->bhkv', kc * jnp.exp(b_last[:, :, None, :] - b), vc)
        return new_state, o

    s_fin, o = lax.scan(step, s0, (chunks(q), chunks(k), chunks(v), chunks(log_f)))
    o = o.transpose(1, 0, 3, 2, 4).reshape(bsz, l_pad, nh, vd)[:, :n_tok]
    return o, s_fin


def _mem_attend(q, mk, mv):
    s = jnp.einsum('blhd,bmhd->bhlm', q.astype(F32), mk.astype(F32)) * HEAD_DIM ** -0.5
    p = jax.nn.softmax(s, axis=-1)
    return jnp.einsum('bhlm,bmhd->blhd', p, mv.astype(F32))


def _merge(x, o_attn, ga, o_hgrn, gb, o_mem, gm, norm_g, w_out, ln_g, ln_b):
    bsz, n_tok = x.shape[:2]
    o_h = o_hgrn * lax.rsqrt(jnp.mean(o_hgrn * o_hgrn, axis=-1, keepdims=True) + RMS_EPS)
    o_h = o_h.reshape(bsz, n_tok, W_HGRN) * norm_g.astype(F32)
    z = jnp.concatenate([
        o_attn.reshape(bsz, n_tok, W_ATTN) * jax.nn.silu(ga.astype(F32)),
        o_h * jax.nn.silu(gb.astype(F32)),
        o_mem.reshape(bsz, n_tok, W_MEM) * jax.nn.silu(gm.astype(F32)),
    ], axis=-1).astype(x.dtype)
    y = z @ w_out
    r = DEEPNORM_ALPHA * x.astype(F32) + y.astype(F32)
    mu = jnp.mean(r, axis=-1, keepdims=True)
    var = jnp.mean((r - mu) ** 2, axis=-1, keepdims=True)
    out = (r - mu) * lax.rsqrt(var + LN_EPS) * ln_g.astype(F32) + ln_b.astype(F32)
    return out.astype(x.dtype)


def _prompt_layer(x, mem, w_in, w_mem_kv, lb, norm_g, w_out, ln_g, ln_b):
    bsz, seq, _ = x.shape
    pos = jnp.arange(seq)
    qa, ka, va, ga, qb, fb, ib, gb, qm, gm = _split_cols(x @ w_in)
    qa, ka, va = _rope(_heads(qa), pos), _rope(_heads(ka), pos), _heads(va)
    branches = [_dilated_prompt(qa, ka, va, w, d) for w, d in zip(WINDOWS, DILATIONS)]
    o_attn = _combine_dilations([br[0] for br in branches], [br[1] for br in branches])
    qh, kh, vh, logf = _hgrn_inputs(qb, fb, ib, lb)
    s0 = jnp.zeros((bsz, N_HEADS_HGRN, HGRN_EXPAND, HEAD_DIM), F32)
    o_hgrn, s_fin = _hgrn_chunked(qh, kh, vh, logf, s0)
    mk, mv = jnp.split(mem @ w_mem_kv, 2, axis=-1)
    mk, mv = _heads(mk), _heads(mv)
    o_mem = _mem_attend(_heads(qm), mk, mv)
    y = _merge(x, o_attn, ga, o_hgrn, gb, o_mem, gm, norm_g, w_out, ln_g, ln_b)
    keep = min(MAX_WINDOW, seq)
    return y, ka[:, seq - keep:], va[:, seq - keep:], s_fin, mk, mv


def _sample_layer(x, win_k, win_v, s_prev, mem_k, mem_v, w_in, lb, norm_g, w_out, ln_g, ln_b):
    n_new = x.shape[1]
    pos = PAST_LEN + jnp.arange(n_new)
    qa, ka, va, ga, qb, fb, ib, gb, qm, gm = _split_cols(x @ w_in)
    qa, ka, va = _rope(_heads(qa), pos), _rope(_heads(ka), pos), _heads(va)
    k_all = jnp.concatenate([win_k.astype(ka.dtype), ka], axis=1)
    v_all = jnp.concatenate([win_v.astype(va.dtype), va], axis=1)
    branches = [_dilated_sample(qa, k_all, v_all, w, d) for w, d in zip(WINDOWS, DILATIONS)]
    o_attn = _combine_dilations([br[0] for br in branches], [br[1] for br in branches])
    qh, kh, vh, logf = _hgrn_inputs(qb, fb, ib, lb)
    o_hgrn, s_new = _hgrn_chunked(qh, kh, vh, logf, s_prev.astype(F32))
    o_mem = _mem_attend(_heads(qm), mem_k, mem_v)
    y = _merge(x, o_attn, ga, o_hgrn, gb, o_mem, gm, norm_g, w_out, ln_g, ln_b)
    return y, ka, va, s_new.astype(s_prev.dtype)


def setup_inputs(seed: int = 0) -> dict:
    key = jax.random.key(seed)
    ks = jax.random.split(key, 16)
    w_buf = min(MAX_WINDOW, PAST_LEN)
    nrm = jax.random.normal
    return {
        'x_prompt': nrm(ks[0], (BATCH, SEQ, D_MODEL), F32),
        'x_sample': nrm(ks[1], (DEC_BATCH, DEC_SEQ, D_MODEL), F32),
        'cache_win_k': nrm(ks[2], (DEPTH, DEC_BATCH, w_buf, N_HEADS_ATTN, HEAD_DIM), F32),
        'cache_win_v': nrm(ks[3], (DEPTH, DEC_BATCH, w_buf, N_HEADS_ATTN, HEAD_DIM), F32),
        'state_hgrn': 0.5 * nrm(ks[4], (DEPTH, DEC_BATCH, N_HEADS_HGRN, HGRN_EXPAND, HEAD_DIM), F32),
        'cache_mem_k': nrm(ks[5], (DEPTH, DEC_BATCH, N_MEM, N_HEADS_MEM, HEAD_DIM), F32),
        'cache_mem_v': nrm(ks[6], (DEPTH, DEC_BATCH, N_MEM, N_HEADS_MEM, HEAD_DIM), F32),
        'mem_prompt': nrm(ks[7], (BATCH, N_MEM, D_MODEL), F32),
        'w_in': nrm(ks[8], (DEPTH, D_MODEL, IN_PROJ), F32) * D_MODEL ** -0.5,
        'w_mem_kv': nrm(ks[9], (DEPTH, D_MODEL, 2 * W_MEM), F32) * D_MODEL ** -0.5,
        'hgrn_lb_raw': 0.1 * nrm(ks[10], (DEPTH + 1, W_HGRN), F32),
        'hgrn_norm_g': 1.0 + 0.02 * nrm(ks[11], (DEPTH, W_HGRN), F32),
        'w_out': nrm(ks[12], (DEPTH, MIX_WIDTH, D_MODEL), F32) * (MIX_WIDTH ** -0.5 * DEEPNORM_BETA),
        'ln_g': 1.0 + 0.02 * nrm(ks[13], (DEPTH, D_MODEL), F32),
        'ln_b': 0.02 * nrm(ks[14], (DEPTH, D_MODEL), F32),
    }


def reference(x_prompt, x_sample, cache_win_k, cache_win_v, state_hgrn, cache_mem_k, cache_mem_v,
              mem_prompt, w_in, w_mem_kv, hgrn_lb_raw, hgrn_norm_g, w_out, ln_g, ln_b):
    lb_all = jnp.cumsum(jax.nn.softmax(hgrn_lb_raw.astype(F32), axis=0), axis=0)
    hp, hs = x_prompt, x_sample
    pk, pv, ps, pmk, pmv, sk, sv, ss = [], [], [], [], [], [], [], []
    for layer in range(DEPTH):
        hp, k1, v1, s1, mk1, mv1 = _prompt_layer(
            hp, mem_prompt, w_in[layer], w_mem_kv[layer], lb_all[layer], hgrn_norm_g[layer],
            w_out[layer], ln_g[layer], ln_b[layer])
        hs, k2, v2, s2 = _sample_layer(
            hs, cache_win_k[layer], cache_win_v[layer], state_hgrn[layer], cache_mem_k[layer],
            cache_mem_v[layer], w_in[layer], lb_all[layer], hgrn_norm_g[layer], w_out[layer],
            ln_g[layer], ln_b[layer])
        pk.append(k1); pv.append(v1); ps.append(s1); pmk.append(mk1); pmv.append(mv1)
        sk.append(k2); sv.append(v2); ss.append(s2)
    return (hp, hs, jnp.stack(pk), jnp.stack(pv), jnp.stack(ps), jnp.stack(pmk), jnp.stack(pmv),
            jnp.stack(sk), jnp.stack(sv), jnp.stack(ss))
```

```python
import numpy as np
from contextlib import ExitStack
import concourse.bass as bass
import concourse.mybir as mybir
from concourse.bass_utils import run_bass_kernel_spmd

F32 = mybir.dt.float32
BF16 = mybir.dt.bfloat16
AF = mybir.ActivationFunctionType
ALU = mybir.AluOpType
AX = mybir.AxisListType

NEG = -30000.0
ALPHA = 2.0 ** 0.25
SCALE = 128.0 ** -0.5


class Res:
    __slots__ = ("name", "writer", "readers", "dsem", "dcount", "excl")

    def __init__(self, name):
        self.name = name
        self.excl = False
        self.writer = None
        self.readers = []
        self.dsem = None
        self.dcount = 0


class Op:
    __slots__ = ("eng", "fn", "deps", "is_dma", "res", "dval", "needed", "mval", "tag", "cost", "idx", "sched", "fin", "crit", "start", "name", "is_bar", "single")

    def __init__(self, eng, fn, tag=""):
        self.eng = eng
        self.fn = fn
        self.cost = 0.3
        self.single = False
        self.is_bar = False
        self.idx = 0
        self.sched = False
        self.fin = 0.0
        self.deps = []
        self.is_dma = False
        self.res = None
        self.dval = 0
        self.needed = False
        self.mval = 0
        self.tag = tag


class Prog:
    ENGS = ["pe", "act", "dve", "pool", "sp"]

    def __init__(self, nc, stack):
        self.nc = nc
        self.stack = stack
        self.ops = []
        self.nres = 0
        self.out_ops = []
        self.bar = []
        self.last = {}
        self.dmas_since = []
        self.last_dma = {}
        self.bar_fn = None
        self.bar_ops = []
        self.pe_lat = 0.5
        self.ctx = ''

    def res(self, name=None):
        self.nres += 1
        return Res(f"{name or 'r'}{self.nres}")

    def _deps(self, op, reads, writes):
        ex = [r for r in reads if r.excl and r not in writes]
        if ex:
            writes = writes + ex
            reads = [r for r in reads if not r.excl]
        deps = list(self.bar)
        for r in reads:
            if r.writer is not None:
                deps.append(r.writer)
        for w in writes:
            if w.writer is not None:
                deps.append(w.writer)
            deps.extend(w.readers)
        seen = set()
        for d in deps:
            if id(d) not in seen and d is not op:
                seen.add(id(d))
                op.deps.append(d)
        for w in writes:
            w.writer = op
            w.readers = []
        for r in reads:
            if r not in writes:
                r.readers.append(op)

    def add(self, eng, fn, reads=(), writes=(), tag="", cost=0.3, single=False):
        op = Op(eng, fn, tag)
        op.cost = cost
        op.single = single
        op.name = self.ctx
        self._deps(op, list(reads), list(writes))
        op.idx = len(self.ops)
        self.ops.append(op)
        self.last[eng] = op
        return op

    def dma(self, issuer, fn, n, primary, reads=(), writes=(), tag="", is_out=False, cost=3.0):
        op = Op(issuer, fn, tag)
        op.is_dma = True
        op.res = primary
        op.cost = cost
        op.name = self.ctx + ":dma:" + primary.name
        self._deps(op, list(reads), list(writes))
        prev = self.last_dma.get(id(primary))
        if prev is not None and prev not in op.deps:
            op.deps.append(prev)
        self.last_dma[id(primary)] = op
        op.idx = len(self.ops)
        primary.dcount += 16 * n
        op.dval = primary.dcount
        op.mval = n
        self.ops.append(op)
        self.dmas_since.append(op)
        if is_out:
            self.out_ops.append(op)
        return op

    def barrier(self):
        deps = [o for o in self.last.values()] + self.dmas_since
        self.dmas_since = []
        self.bar = []
        op = self.add("pool", self.bar_fn, [], [], cost=0.2)
        op.is_bar = True
        for d in deps:
            if d is not op and d not in op.deps:
                op.deps.append(d)
        self.bar = [op]
        self.bar_ops.append(op)

    def schedule(self, W=48, LAT=0.5):
        ops = self.ops
        per = {e: [o for o in ops if o.eng == e] for e in self.ENGS}
        head = {e: 0 for e in self.ENGS}
        free = {e: 0.0 for e in self.ENGS}
        order = {e: [] for e in self.ENGS}
        left = len(ops)
        nsched = 0
        issue = {"sp": 0.06, "act": 0.06, "pool": 0.6}
        while left:
            best = None
            for e in self.ENGS:
                lst = per[e]
                h = head[e]
                while h < len(lst) and lst[h].sched:
                    h += 1
                head[e] = h
                cnt = 0
                i = h
                fe = free[e]
                while i < len(lst) and cnt < W:
                    op = lst[i]
                    i += 1
                    if op.sched:
                        continue
                    cnt += 1
                    if op.is_bar and nsched < op.idx:
                        continue
                    est = fe
                    ok = True
                    cr = None
                    for d in op.deps:
                        if not d.sched:
                            ok = False
                            break
                        t = d.fin + (LAT if (e != "pe" or d.eng == "pe") else self.pe_lat)
                        if t > est:
                            est = t
                            cr = d
                    if not ok:
                        continue
                    op.tag = cr
                    key = (est + 0.002 * (cnt - 1), op.idx)
                    if best is None or key < best[0]:
                        best = (key, e, op, est)
            assert best is not None, "scheduler stuck"
            _, e, op, est = best
            if op.is_bar:
                for e2 in self.ENGS:
                    if order[e2]:
                        d = order[e2][-1]
                        if d is not op and d not in op.deps:
                            op.deps.append(d)
                        if d.fin + LAT > est:
                            est = d.fin + LAT
            nsched += 1
            op.sched = True
            op.mval = op.mval if op.is_dma else 0
            op.crit = op.tag if op.tag is not None else (order[e][-1] if order[e] else None)
            op.start = est
            if op.is_dma:
                free[e] = est + issue[e]
                op.fin = est + issue[e] + op.cost
            else:
                free[e] = est + op.cost
                op.fin = free[e]
            order[e].append(op)
            left -= 1
        self.sim_end = max(o.fin for o in ops)
        return order

    def emit(self):
        nc, stack = self.nc, self.stack
        per = self.schedule()
        for op in self.ops:
            for d in op.deps:
                d.needed = True
        esem = {e: stack.enter_context(nc.semaphore(f"esem_{e}")) for e in self.ENGS}
        for op in self.ops:
            if op.is_dma and op.res.dsem is None:
                op.res.dsem = stack.enter_context(nc.semaphore(f"ds_{op.res.name}"))
        cnt = {e: 0 for e in self.ENGS}
        for e in self.ENGS:
            for op in per[e]:
                if not op.is_dma and op.needed:
                    cnt[e] += 1
                    op.dval = cnt[e]

        def done(op):
            if op.is_dma:
                return op.res.dsem, op.dval
            return esem[op.eng], op.dval

        out_waits = {}
        for op in self.out_ops:
            s, v = done(op)
            if id(s) not in out_waits or out_waits[id(s)][1] < v:
                out_waits[id(s)] = (s, v)
        block = stack.enter_context(nc.Block())

        def make(e):
            def body(engine):
                waited = {}
                for op in per[e]:
                    need = {}
                    for d in op.deps:
                        s, v = done(d)
                        if waited.get(id(s), 0) >= v:
                            continue
                        cur = need.get(id(s))
                        if cur is None or v > cur[1]:
                            need[id(s)] = (s, v, d.fin)
                    pend = []
                    for (s, v, f) in sorted(need.values(), key=lambda x: x[2]):
                        waited[id(s)] = v
                        pend.append((s, v))
                    fused = None
                    if (op.single or op.is_dma) and pend:
                        fused = pend.pop()
                    for (s, v) in pend:
                        engine.wait_ge(s, v)
                    if op.is_dma:
                        insts = op.fn(engine)
                        assert len(insts) == op.mval, (op.tag, len(insts), op.mval)
                        if fused is not None:
                            insts[0]._wait_ge(fused[0], fused[1])
                        for ins in insts:
                            ins.then_inc(op.res.dsem, 16)
                    else:
                        res = op.fn(engine)
                        first, ins = res if isinstance(res, tuple) else (res, res)
                        if fused is not None:
                            first._wait_ge(fused[0], fused[1])
                        if op.needed:
                            ins.then_inc(esem[e], 1)
                if e == "sp":
                    for s, v in out_waits.values():
                        engine.wait_ge(s, v)
            return body

        block.tensor(make("pe"))
        block.scalar(make("act"))
        block.vector(make("dve"))
        block.gpsimd(make("pool"))
        block.sync(make("sp"))
        return cnt


def build_nc():
    nc = bass.Bass("TRN2", target_bir_lowering=False)

    def din(name, shape, dt=F32):
        return nc.dram_tensor(name, list(shape), dt, kind="ExternalInput").ap()

    def dout(name, shape, dt=F32):
        return nc.dram_tensor(name, list(shape), dt, kind="ExternalOutput").ap()

    x_all = din("x_all", [2048, 2048])
    mem = din("mem", [256, 2048])
    x_s = din("x_s", [4, 2048])
    ck = din("ck", [4, 2048, 768])
    cv = din("cv", [4, 2048, 768])
    st_in = din("st_in", [4, 6, 128, 128])
    cmk = din("cmk", [4, 256, 512])
    cmv = din("cmv", [4, 256, 512])
    w_in = din("w_in", [2048, 7168])
    w_mem = din("w_mem", [2048, 1024])
    w_out = din("w_out", [2048, 2048])
    lb_raw = din("lb_raw", [2, 768])
    norm_g = din("norm_g", [1, 768])
    ln_g = din("ln_g", [1, 2048])
    ln_b = din("ln_b", [1, 2048])
    c_ident = din("c_ident", [128, 128])
    c_rope = din("c_rope", [2, 2048, 64])
    c_rope_s = din("c_rope_s", [2, 4, 64])
    c_bias = din("c_bias", [3, 128, 256])
    c_hg = din("c_hg", [4, 128, 128])
    c_sel = din("c_sel", [4, 4, 128])
    c_selc = din("c_selc", [128, 16])

    y = dout("y", [1024, 2048])
    ys = dout("ys", [4, 2048])
    pk = dout("pk", [1024, 768])
    pv = dout("pv", [1024, 768])
    pstate = dout("pstate", [6, 128, 128])
    pmk = dout("pmk", [256, 512])
    pmv = dout("pmv", [256, 512])
    sk = dout("sk", [4, 768])
    sv = dout("sv", [4, 768])
    sstate = dout("sstate", [4, 6, 128, 128])

    v_scr = nc.dram_tensor("v_scr", [2048, 768], BF16, kind="Internal").ap()
    rec_scr = nc.dram_tensor("rec_scr", [1024, 3, 6, 130], F32, kind="Internal").ap()
    smp_scr = nc.dram_tensor("smp_scr", [4, 6, 4, 128], F32, kind="Internal").ap()

    with ExitStack() as st:
        P = Prog(nc, st)

        def sb(name, shape, dt=F32):
            return st.enter_context(nc.sbuf_tensor(name, list(shape), dt))

        def psb(name, shape, dt=F32):
            return st.enter_context(nc.psum_tensor(name, list(shape), dt))

        def fap(base, off, dims):
            return bass.AP(tensor=base.tensor, offset=base.offset + off, ap=[list(base.ap[0])] + [list(d) for d in dims])

        def dap(base, off, dims):
            return bass.AP(tensor=base.tensor, offset=base.offset + off, ap=[list(d) for d in dims])

        def fsz(ap):
            n = 1
            for d in ap.shape[1:]:
                n *= d
            return n

        def ecost(eng, n, slow=1.0):
            if eng == "dve":
                return 0.12 + n * 1.05e-3 * slow
            if eng == "pool":
                return 0.25 + n * 1.8e-3 * slow
            return 0.2 + n * 0.85e-3

        def tt(eng, out, in0, in1, op, R, W):
            return P.add(eng, lambda e: e.tensor_tensor(out=out, in0=in0, in1=in1, op=op), R, W, cost=ecost(eng, fsz(out)), single=True)

        def ts(eng, out, in0, s1, s2, op0, op1, R, W):
            c = ecost(eng, fsz(out))
            if op1 is None:
                return P.add(eng, lambda e: e.tensor_scalar(out=out, in0=in0, scalar1=s1, scalar2=None, op0=op0), R, W, cost=c, single=True)
            return P.add(eng, lambda e: e.tensor_scalar(out=out, in0=in0, scalar1=s1, scalar2=s2, op0=op0, op1=op1), R, W, cost=c, single=True)

        def stt(eng, out, in0, scalar, in1, op0, op1, R, W):
            return P.add(eng, lambda e: e.scalar_tensor_tensor(out=out, in0=in0, scalar=scalar, in1=in1, op0=op0, op1=op1), R, W,
                         cost=ecost(eng, fsz(out)), single=True)

        def act(out, in_, func, R, W, bias=0.0, scale=1.0, accum=None):
            c = ecost("act", fsz(out))
            if accum is None:
                return P.add("act", lambda e: e.activation(out=out, in_=in_, func=func, bias=bias, scale=scale), R, W, cost=c, single=True)
            return P.add("act", lambda e: e.activation(out=out, in_=in_, func=func, bias=bias, scale=scale, accum_out=accum), R, W, cost=c, single=True)

        def cp(eng, out, in_, R, W):
            c = ecost(eng, fsz(out))
            if eng == "act":
                return P.add("act", lambda e: e.activation(out=out, in_=in_, func=AF.Copy), R, W, cost=c, single=True)
            return P.add(eng, lambda e: e.tensor_copy(out=out, in_=in_), R, W, cost=c, single=True)

        def memset(eng, out, val, W):
            return P.add(eng, lambda e: e.memset(out, val), [], W, cost=ecost(eng, fsz(out)) * 0.6, single=True)

        def red(eng, out, in_, op, R, W):
            return P.add(eng, lambda e: e.tensor_reduce(out=out, in_=in_, axis=AX.X, op=op), R, W, cost=ecost(eng, fsz(in_)), single=True)

        def recip(out, in_, R, W):
            return P.add("dve", lambda e: e.reciprocal(out=out, in_=in_), R, W, cost=ecost("dve", fsz(out), 6.0), single=True)

        def mmcost(o, l):
            n = fsz(o)
            c = max(n, 128) / 2000.0
            if l.dtype == F32:
                c *= 4.0
            return c + 0.02

        def mms(lst, R, W):
            def fn(e):
                ins = None
                first = None
                n = len(lst)
                for i, (o, l, r) in enumerate(lst):
                    ins = e.matmul(o, lhsT=l, rhs=r, start=(i == 0), stop=(i == n - 1))
                    if first is None:
                        first = ins
                return (first, ins)
            return P.add("pe", fn, R, W, cost=sum(mmcost(o, l) for (o, l, r) in lst), single=True)

        def mmi(lst, R, W):
            def fn(e):
                ins = None
                first = None
                for (o, l, r) in lst:
                    ins = e.matmul(o, lhsT=l, rhs=r, start=True, stop=True)
                    if first is None:
                        first = ins
                return (first, ins)
            return P.add("pe", fn, R, W, cost=sum(mmcost(o, l) for (o, l, r) in lst), single=True)

        def trs(lst, ident, R, W):
            def fn(e):
                ins = None
                first = None
                for (o, i) in lst:
                    ins = e.transpose(out=o, in_=i, identity=ident)
                    if first is None:
                        first = ins
                return (first, ins)
            return P.add("pe", fn, R, W, cost=0.09 * len(lst), single=True)

        def dma(issuer, pairs, primary, R, W, is_out=False):
            def fn(e):
                return [e.dma_start(out=o, in_=i) for (o, i) in pairs]
            nbytes = 0
            for (o, i) in pairs:
                n = 1
                for d in o.shape:
                    n *= d
                nbytes += n * 4
            return P.dma(issuer, fn, len(pairs), primary, R, W, is_out=is_out, cost=2.0 + nbytes / 250e3)

        xT = sb("xT", [128, 16, 2048], BF16); r_xTt = [P.res("xTt") for _ in range(16)]; r_xTc = r_xTt[0]; r_xTo = r_xTt[8]
        wbuf = sb("wbuf", [128, 2, 16, 512], BF16); r_w = [P.res("w0"), P.res("w1")]
        z = sb("z", [128, 8, 2048], BF16); r_z = [P.res("z") for _ in range(8)]
        zs = sb("zs", [4, 2048], BF16); r_zs = P.res("zs")
        xsT = sb("xsT", [128, 16, 4], BF16); r_xsT = P.res("xsT")
        identf = sb("identf", [128, 128], F32); r_idf = P.res("idf")
        identb = sb("identb", [128, 128], BF16); r_idb = P.res("idb")
        onesf = sb("onesf", [128, 128], F32); r_ones = P.res("ones")
        stage = [sb(f"stage{i}", [128, 512], F32) for i in range(3)]
        r_stage = [P.res("stage") for _ in range(3)]
        sel = sb("sel", [4, 4, 128], F32); r_sel = P.res("sel")
        selc = sb("selc", [128, 16], F32); r_selc = P.res("selc")
        barscr = sb("barscr", [1, 8], F32)
        P.bar_fn = lambda e: e.memset(barscr[0:1, 0:1], 0.0)
        ARENA = 16944
        arena = sb("arena", [128, ARENA], F32)
        apos = [0]

        amax = [0]

        def aalloc(n_f32):
            a = apos[0]
            apos[0] += n_f32
            amax[0] = max(amax[0], apos[0])
            assert apos[0] <= ARENA, apos[0]
            return arena[:, a:a + n_f32]

        def areset():
            print("arena high-water", amax[0])
            amax[0] = 0
            apos[0] = 0
            P.barrier()

        def a_f32(shape):
            n = int(np.prod(shape[1:]))
            v = aalloc(n)[0:shape[0], :]
            if len(shape) == 3:
                v = v.rearrange("p (a b) -> p a b", a=shape[1])
            elif len(shape) == 4:
                v = v.rearrange("p (a b c) -> p a b c", a=shape[1], b=shape[2])
            return v

        def a_bf(shape):
            n = int(np.prod(shape[1:]))
            assert n % 2 == 0
            v = aalloc(n // 2).bitcast(BF16)[0:shape[0], :]
            if len(shape) == 3:
                v = v.rearrange("p (a b) -> p a b", a=shape[1])
            elif len(shape) == 4:
                v = v.rearrange("p (a b c) -> p a b c", a=shape[1], b=shape[2])
            return v

        print('SBUF bytes remaining', nc.sbuf_bytes_remaining)
        psA = [psb(f"psA{i}", [128, 512], F32) for i in range(6)]
        r_psA = [P.res("psA") for _ in range(6)]
        for _r in r_psA:
            _r.excl = True
        psB = [psb(f"psB{i}", [128, 1024], BF16) for i in range(2)]
        r_psB = [P.res("psB") for _ in range(2)]
        for _r in r_psB:
            _r.excl = True
        stage_i = [0]
        proj_i = [0]

        dma("sp", [(identf[:], c_ident)], r_idf, [], [r_idf])
        cp("dve", identb[:], identf[:], [r_idf], [r_idb])
        memset("pool", onesf[:], 1.0, [r_ones])
        dma("sp", [(sel[:], c_sel.rearrange("b t m -> t b m"))], r_sel, [], [r_sel])
        dma("sp", [(selc[:], c_selc)], r_selc, [], [r_selc])

        P.ctx = "phase0"
        dma("pool", [(wbuf[:, 0, :, j * 128:(j + 1) * 128],
                      dap(w_in, c0, [[7168, 128], [128 * 7168, 16], [1, 128]])) for j, c0 in enumerate((768, 1536, 0, 2304))],
            r_w[0], [], [r_w[0]])
        xb = [z[:, i, :] for i in range(4)]
        r_xb = [r_z[i] for i in range(4)]
        for t in range(16):
            s = t % 4
            dma("pool", [(xb[s], x_all[t * 128:(t + 1) * 128, :])], r_xb[s], [], [r_xb[s]])
            for hb in range(2):
                pb = psB[hb]
                trs([(pb[:, j * 128:(j + 1) * 128], xb[s][:, (hb * 8 + j) * 128:(hb * 8 + j + 1) * 128]) for j in range(8)],
                    identb[:], [r_xb[s], r_idb], [r_psB[hb]])
                cp("act" if hb == 0 else "dve", xT[:, hb * 8:(hb + 1) * 8, t * 128:(t + 1) * 128],
                   pb[:].rearrange("p (j c) -> p j c", j=8), [r_psB[hb]], [r_xTt[t]])
        xsb = z[0:4, 4, :]; r_xsb = r_z[4]
        dma("pool", [(xsb, x_s)], r_xsb, [], [r_xsb])
        trs([(psB[0][:, j * 4:(j + 1) * 4], xsb[:, j * 128:(j + 1) * 128]) for j in range(16)], identb[0:4, 0:4],
            [r_xsb, r_idb], [r_psB[0]])
        cp("act", xsT[:], psB[0][:, 0:64].rearrange("p (j c) -> p j c", j=16), [r_psB[0]], [r_xsT])

        def load_w(slot, src, segs):
            pairs = []
            c = 0
            for (c0, n) in segs:
                pairs.append((wbuf[:, slot, :, c:c + n],
                              dap(src, c0, [[src.ap[0][0], 128], [128 * src.ap[0][0], 16], [1, n]])))
                c += n
            dma("pool", pairs, r_w[slot], [], [r_w[slot]])

        def project(slot, lhs_of, M, N, R_extra, ev="act"):
            pi = proj_i[0] % 2
            proj_i[0] += 1
            ps, rps = psA[pi], r_psA[pi]
            lst = [(ps[0:M, 0:N], lhs_of(kc), wbuf[:, slot, kc, 0:N]) for kc in range(16)]
            mms(lst, [r_w[slot]] + R_extra, [rps])
            si = stage_i[0] % 3
            stage_i[0] += 1
            cp(ev, stage[si][0:M, 0:N], ps[0:M, 0:N], [rps], [r_stage[si]])
            return stage[si], r_stage[si]

        def silu_to(eng2, out_bf, g_ap, M, Rg, Wout, tmp, r_tmp):
            act(tmp, g_ap, AF.Exp, Rg, [r_tmp], scale=-1.0)
            act(tmp, tmp, AF.Ln, [r_tmp], [r_tmp], bias=1.0)
            act(tmp, tmp, AF.Exp, [r_tmp], [r_tmp], scale=-1.0)
            tt(eng2, out_bf, g_ap, tmp, ALU.mult, Rg + [r_tmp], Wout)

        apos[0] = 0
        biasm = a_f32([128, 3, 256]); r_biasm = P.res("biasm")
        dma("sp", [(biasm[:], c_bias.rearrange("k p c -> p k c"))], r_biasm, [], [r_biasm])

        def mk_hb():
            d = dict(qT=a_bf([128, 1024]), kT=a_bf([128, 2048]), qT3=a_bf([128, 1024]), v_bf=a_bf([128, 16, 128]))
            for k in list(d.keys()):
                d["r_" + k] = P.res(k)
            return d
        HB1 = mk_hb()
        NU = 4
        vg = [a_bf([128, 2, 128]) for _ in range(NU)]; r_vg = [P.res("vg") for _ in range(NU)]
        rec = [a_f32([128, 130]) for _ in range(NU)]; r_rec = [P.res("rec") for _ in range(NU)]; r_den = [P.res("den") for _ in range(NU)]
        sm = [a_f32([128, 256]) for _ in range(NU)]; r_sm = [P.res("sm") for _ in range(NU)]
        pbf = [a_bf([128, 256]) for _ in range(NU)]; r_pbf = [P.res("pbf") for _ in range(NU)]
        pT = [a_bf([128, 2, 128]) for _ in range(NU)]; r_pT = [P.res("pT") for _ in range(NU)]
        tail_mark = apos[0]
        rope = a_f32([128, 2, 16, 64]); r_rope = P.res("rope")
        dma("sp", [(rope[:, cs], c_rope[cs].rearrange("(t p) c -> p t c", p=128)) for cs in range(2)], r_rope, [], [r_rope])
        rope_s = a_f32([4, 2, 64]); r_rope_s = P.res("rope_s")
        dma("sp", [(rope_s[:, cs], c_rope_s[cs]) for cs in range(2)], r_rope_s, [], [r_rope_s])
        HB = [mk_hb(), HB1]
        k_out = a_f32([128, 8, 128]); r_kout = P.res("kout")
        v_out = a_f32([128, 8, 128]); r_vout = P.res("vout")
        smpT = a_f32([4, 4, 128]); r_smpT = P.res("smpT")
        kq_r = [a_f32([128, 128]) for _ in range(2)]; r_kqr = [P.res("kqr") for _ in range(2)]
        kq_f = [a_f32([128, 2, 128]) for _ in range(2)]; r_kqf = [P.res("kqf") for _ in range(2)]
        kq_b = [a_bf([128, 2, 128]) for _ in range(2)]; r_kqb = [P.res("kqb") for _ in range(2)]
        rt = [a_f32([128, 2, 64]) for _ in range(4)]; r_rtl = [P.res("rt") for _ in range(4)]
        gtmp = a_f32([128, 128]); r_gtmp = P.res("gtmp")
        r_pb0 = [r_psB[0], r_psB[0]]
        r_pb1 = [r_psB[1], r_psB[1]]

        def do_rope(src, nk, cosv, sinv, dsts, Rsrc, Wdst, M):
            for j in range(nk):
                x1 = src[j][:, 0:64]
                x2 = src[j][:, 64:128]
                tt("dve", rt[0][0:M, j], x1, cosv, ALU.mult, Rsrc, [r_rtl[0]])
                tt("dve", rt[1][0:M, j], x2, sinv, ALU.mult, Rsrc, [r_rtl[1]])
                tt("pool", rt[2][0:M, j], x2, cosv, ALU.mult, Rsrc, [r_rtl[2]])
                tt("pool", rt[3][0:M, j], x1, sinv, ALU.mult, Rsrc, [r_rtl[3]])
                tt("dve", dsts[j][:, 0:64], rt[0][0:M, j], rt[1][0:M, j], ALU.subtract, [r_rtl[0], r_rtl[1]], Wdst[j])
                tt("dve", dsts[j][:, 64:128], rt[2][0:M, j], rt[3][0:M, j], ALU.add, [r_rtl[2], r_rtl[3]], Wdst[j])

        def do_rope2(stg, cosv, sinv, dst2, Rsrc, Wdst):
            x1 = fap(stg[:], 0, [[256, 2], [1, 64]])
            x2 = fap(stg[:], 64, [[256, 2], [1, 64]])
            cb = cosv.unsqueeze(1).to_broadcast([128, 2, 64])
            sb_ = sinv.unsqueeze(1).to_broadcast([128, 2, 64])
            tt("dve", rt[0], x1, cb, ALU.mult, Rsrc, [r_rtl[0]])
            tt("dve", rt[1], x2, sb_, ALU.mult, Rsrc, [r_rtl[1]])
            tt("pool", rt[2], x2, cb, ALU.mult, Rsrc, [r_rtl[2]])
            tt("pool", rt[3], x1, sb_, ALU.mult, Rsrc, [r_rtl[3]])
            tt("dve", dst2[:, :, 0:64], rt[0], rt[1], ALU.subtract, [r_rtl[0], r_rtl[1]], Wdst)
            tt("pool", dst2[:, :, 64:128], rt[2], rt[3], ALU.add, [r_rtl[2], r_rtl[3]], Wdst)

        units = []
        for u in range(8):
            kb0 = 1024 + 128 * (u - 1)
            units.append(dict(br=0, q=(0, 128 * u, 1), k=(kb0, [[1, 256]]), bias=(1 if u == 0 else 0),
                              v=[[(kb0, 1, 128)], [(kb0 + 128, 1, 128)]], rows=[(128 * u, 1, 128)]))
        for n in range(2):
            for r in range(4):
                q0 = 512 * n + r
                k0 = 1024 + 512 * (n - 1) + r
                units.append(dict(br=1, q=(0, q0, 4), k=(k0, [[512, 2], [4, 128]]), bias=(1 if n == 0 else 0),
                                  v=[[(k0, 4, 128)], [(k0 + 512, 4, 128)]], rows=[(q0, 4, 128)]))
        for u in range(8):
            q0 = 2 * u
            units.append(dict(br=2, q=(1, 128 * u, 1), k=(q0, [[1024, 2], [1, 2], [16, 64]]), bias=2,
                              v=[[(q0, 16, 64), (q0 + 1, 16, 64)], [(1024 + q0, 16, 64), (1024 + q0 + 1, 16, 64)]],
                              rows=[(q0, 16, 64), (q0 + 1, 16, 64)]))

        def A_proj_tile(h, t):
            P.ctx = "Aproj h%d t%d" % (h, t)
            slot = h % 2
            H = HB[h % 2]
            qT, kT, v_bf = H["qT"], H["kT"], H["v_bf"]
            r_qT, r_kT, r_vbf = H["r_qT"], H["r_kT"], H["r_v_bf"]
            if t == 2 and h + 1 < 6:
                h1 = h + 1
                load_w(h1 % 2, w_in, [(768 + h1 * 128, 128), (1536 + h1 * 128, 128), (h1 * 128, 128), (2304 + h1 * 128, 128)])
            if t < 16:
                own = t >= 8
                par = t % 2
                M, N = 128, (512 if own else 256)
                stg, rs = project(slot, lambda kc, t=t: xT[:, kc, t * 128:(t + 1) * 128], M, N, [r_xTt[t]])
                cosv, sinv = rope[:, 0, t], rope[:, 1, t]
                kr = kq_r[par]; rkr = r_kqr[par]
                kb_, rkb = kq_b[par], r_kqb[par]
                if own:
                    to = t - 8
                    kq2 = kq_f[par]; rkq2 = r_kqf[par]
                    do_rope2(stg, cosv, sinv, kq2, [rs, r_rope], [rkq2])
                    cp("act", kb_[:, 0:2], kq2, [rkq2], [rkb])
                    cp("pool", k_out[:, to], kq2[:, 0], [rkq2], [r_kout])
                    cp("pool", v_out[:, to], stg[:, 128:256], [rs], [r_vout])
                    nk = 2
                else:
                    do_rope([stg[:, 0:128]], 1, cosv, sinv, [kr], [rs, r_rope], [[rkr]], 128)
                    cp("act", kb_[:, 0], kr, [rkr], [rkb])
                    nk = 1
                cp("act", v_bf[:, t], stg[:, 128:256], [rs], [r_vbf])
                pb = psB[0]
                c0 = par * 256
                trs([(pb[:, c0 + j * 128:c0 + (j + 1) * 128], kb_[:, j]) for j in range(nk)], identb[:], [rkb, r_idb], [r_pb0[par]])
                cp("act", kT[:, t * 128:(t + 1) * 128], pb[:, c0:c0 + 128], [r_pb0[par]], [r_kT])
                if own:
                    cp("act", qT[:, to * 128:(to + 1) * 128], pb[:, c0 + 128:c0 + 256], [r_pb0[par]], [r_qT])
                    silu_to("pool", z[:, to, h * 128:(h + 1) * 128], stg[:, 384:512], 128, [rs], [r_z[to]], gtmp, r_gtmp)
            else:
                stg, rs = project(slot, lambda kc: xsT[:, kc, :], 4, 512, [r_xsT])
                do_rope([stg[0:4, 0:128], stg[0:4, 256:384]], 2, rope_s[:, 0], rope_s[:, 1], [smpT[:, 0], smpT[:, 2]],
                        [rs, r_rope_s], [[r_smpT], [r_smpT]], 4)
                cp("pool", smpT[:, 1], stg[0:4, 128:256], [rs], [r_smpT])
                cp("pool", smpT[:, 3], stg[0:4, 384:512], [rs], [r_smpT])
                dma("sp", [(smp_scr[:, h], smpT[:])], r_smpT, [r_smpT], [])

        def A_post(h):
            P.ctx = "Apost h%d" % h
            H = HB[h % 2]
            dma("sp", [(pk[:, h * 128:(h + 1) * 128].rearrange("(t p) d -> p t d", p=128), k_out[:])], r_kout, [r_kout], [], is_out=True)
            dma("sp", [(pv[:, h * 128:(h + 1) * 128].rearrange("(t p) d -> p t d", p=128), v_out[:])], r_vout, [r_vout], [], is_out=True)
            H["r_vscr"] = P.res("vscr")
            dma("sp", [(dap(v_scr, h * 128, [[768, 128], [128 * 768, 16], [1, 128]]), H["v_bf"][:])], H["r_v_bf"], [H["r_v_bf"]], [H["r_vscr"]])
            cp("act", H["qT3"].rearrange("p (r i) -> p r i", r=16), fap(H["qT"], 0, [[1, 16], [16, 64]]), [H["r_qT"]], [H["r_qT3"]])

        def A_unit(h, ui):
            P.ctx = "Aunit h%d u%d" % (h, ui)
            U = units[ui]
            H = HB[h % 2]
            qT, kT, qT3 = H["qT"], H["kT"], H["qT3"]
            s4_ = ui % NU
            pairs = []
            for blk in range(2):
                p0 = 0
                for (row0, step, n) in U["v"][blk]:
                    pairs.append((vg[s4_][p0:p0 + n, blk, :], dap(v_scr, row0 * 768 + h * 128, [[step * 768, n], [1, 128]])))
                    p0 += n
            dma("sp", pairs, r_vg[s4_], [H["r_vscr"]], [r_vg[s4_]])
            psS, rS = psA[2 + s4_], r_psA[2 + s4_]
            qsrc = (qT, qT3)[U["q"][0]]
            q_ap = fap(qsrc, U["q"][1], [[U["q"][2], 128]])
            mms([(psS[:, 0:256], q_ap, fap(kT, U["k"][0], U["k"][1]))], [H["r_qT"], H["r_kT"], H["r_qT3"]], [rS])
            stt("dve", sm[s4_], psS[:, 0:256], -SCALE, biasm[:, U["bias"]], ALU.mult, ALU.subtract, [rS, r_biasm], [r_sm[s4_]])
            rc, rrc = rec[s4_], r_rec[s4_]
            red("dve", rc[:, 128:129], sm[s4_], ALU.min, [r_sm[s4_]], [rrc])
            memset("pool", rc[:, 129:130], 0.0, [r_den[s4_]])
            act(pbf[s4_], sm[s4_], AF.Exp, [r_sm[s4_], rrc], [r_pbf[s4_], r_den[s4_]], bias=rc[:, 128:129], scale=-1.0, accum=rc[:, 129:130])
            pb, rpb = psB[ui % 2], r_psB[ui % 2]
            trs([(pb[:, 512 + j * 128:512 + (j + 1) * 128], pbf[s4_][:, j * 128:(j + 1) * 128]) for j in range(2)], identb[:],
                [r_pbf[s4_], r_idb], [rpb])
            cp("dve", pT[s4_], pb[:, 512:768].rearrange("p (a b) -> p a b", a=2), [rpb], [r_pT[s4_]])
            mms([(psS[:, 256:384], pT[s4_][:, 0], vg[s4_][:, 0]), (psS[:, 256:384], pT[s4_][:, 1], vg[s4_][:, 1])], [r_pT[s4_], r_vg[s4_]], [rS])
            cp("act", rc[:, 0:128], psS[:, 256:384], [rS], [rrc])
            pairs = []
            p0 = 0
            for (row0, step, n) in U["rows"]:
                pairs.append((dap(rec_scr, row0 * 2340 + U["br"] * 780 + h * 130, [[step * 2340, n], [1, 130]]), rc[p0:p0 + n, :]))
                p0 += n
            dma("sp", pairs, rrc, [rrc, r_den[s4_]], [])

        for t in range(17):
            A_proj_tile(0, t)
        for h in range(5):
            A_post(h)
            nt = 17
            ti = 0
            for ui in range(24):
                A_unit(h, ui)
                while ti < nt and ti * 24 <= (ui + 1) * nt:
                    A_proj_tile(h + 1, ti)
                    ti += 1
            while ti < nt:
                A_proj_tile(h + 1, ti)
                ti += 1
        A_post(5)
        P.ctx = "Atail"
        P.barrier()
        apos[0] = tail_mark
        dma("sp", [(sk.rearrange("t (h d) -> t h d", h=6), smp_scr[:, :, 0, :])], P.res("skd"), [], [], is_out=True)
        dma("sp", [(sv.rearrange("t (h d) -> t h d", h=6), smp_scr[:, :, 1, :])], P.res("svd"), [], [], is_out=True)
        gsm = a_f32([4, 6, 128]); r_gsm = P.res("gsm")
        dma("sp", [(gsm, smp_scr[:, :, 3, :])], r_gsm, [], [r_gsm])
        osmp = a_f32([4, 768]); r_osmp = P.res("osmp")
        memset("dve", osmp, 0.0, [r_osmp])
        SA = []
        for _i in range(1):
            d = dict(Kg=[a_f32([128, 768]) for _ in range(2)], Vg=[a_f32([128, 768]) for _ in range(2)], qb=a_f32([128, 768]), kb=a_f32([128, 768]), prod=a_f32([128, 768]),
                     prod2=a_f32([128, 768]), s0=a_f32([128, 6]), sx=a_f32([128, 6]), dsum=a_f32([128, 6]), nsum=a_f32([128, 768]),
                     prodB=a_f32([128, 768]), sxB=a_f32([128, 6]))
            d["r_Kg"] = [P.res("Kg") for _ in range(2)]
            d["r_Vg"] = [P.res("Vg") for _ in range(2)]
            for k in ("qb", "kb", "prod", "prod2", "s0", "sx", "dsum", "nsum", "prodB", "sxB"):
                d["r_" + k] = P.res(k)
            SA.append(d)
        starts = [(1920, 1), (1536, 4), (0, 16)]
        def sampA_batch(b):
            D = SA[0]
            qb, kb, prod, prod2, s0, sx, dsum, nsum = (D[k] for k in ("qb", "kb", "prod", "prod2", "s0", "sx", "dsum", "nsum"))

            def bc(kind):
                return dap(smp_scr, b * 3072 + kind * 128, [[0, 128], [512, 6], [1, 128]])
            dma("sp", [(qb.rearrange("p (h d) -> p h d", h=6), bc(2))], D["r_qb"], [], [D["r_qb"]])
            dma("sp", [(kb.rearrange("p (h d) -> p h d", h=6), bc(0))], D["r_kb"], [], [D["r_kb"]])
            dma("sp", [(nsum.rearrange("p (h d) -> p h d", h=6), bc(1))], D["r_nsum"], [], [D["r_nsum"]])
            ts("pool", nsum, nsum, 3.0, None, ALU.mult, None, [D["r_nsum"]], [D["r_nsum"]])
            tt("dve", prod, kb, qb, ALU.mult, [D["r_qb"], D["r_kb"]], [D["r_prod"]])
            red("dve", s0, prod.rearrange("p (h d) -> p h d", h=6), ALU.add, [D["r_prod"]], [D["r_s0"]])
            memset("pool", dsum, 3.0, [D["r_dsum"]])
            for r in range(3):
                r0, stp = starts[r]
                Kg, Vg, r_Kg, r_Vg = D["Kg"][r % 2], D["Vg"][r % 2], D["r_Kg"][r % 2], D["r_Vg"][r % 2]
                sfx = "B" if r % 2 else ""
                prod, prod2, sx = D["prod" + sfx], D["prod2"], D["sx" + sfx]
                rprod, rprod2, rsx = D["r_prod" + sfx], D["r_prod2"], D["r_sx" + sfx]
                dma("sp", [(Kg, dap(ck, b * 2048 * 768 + r0 * 768, [[stp * 768, 128], [1, 768]]))], r_Kg, [], [r_Kg])
                dma("sp", [(Vg, dap(cv, b * 2048 * 768 + r0 * 768, [[stp * 768, 128], [1, 768]]))], r_Vg, [], [r_Vg])
                tt("pool", prod, Kg, qb, ALU.mult, [r_Kg, D["r_qb"]], [rprod])
                red("dve", sx, prod.rearrange("p (h d) -> p h d", h=6), ALU.add, [rprod], [rsx])
                tt("dve", sx, sx, s0, ALU.subtract, [rsx, D["r_s0"]], [rsx])
                act(sx, sx, AF.Exp, [rsx], [rsx], scale=SCALE)
                tt("pool", prod2.rearrange("p (h d) -> p h d", h=6), Vg.rearrange("p (h d) -> p h d", h=6),
                   sx.unsqueeze(2).to_broadcast([128, 6, 128]), ALU.mult, [r_Vg, rsx], [rprod2])
                P.add("pe", lambda e, sx=sx, r=r: e.matmul(psA[2][:, 0:6], lhsT=onesf[:], rhs=sx, start=(r == 0), stop=(r == 2)),
                      [r_ones, rsx], [r_psA[2]], cost=0.3)
                for hh in range(2):
                    P.add("pe", lambda e, hh=hh, prod2=prod2, r=r: e.matmul(psA[hh][:, 0:384], lhsT=onesf[:], rhs=prod2[:, hh * 384:(hh + 1) * 384],
                                                                      start=(r == 0), stop=(r == 2)), [r_ones, rprod2], [r_psA[hh]], cost=0.8)
            tt("dve", dsum, dsum, psA[2][:, 0:6], ALU.add, [r_psA[2], D["r_dsum"]], [D["r_dsum"]])
            for hh in range(2):
                tt("dve", nsum[:, hh * 384:(hh + 1) * 384], nsum[:, hh * 384:(hh + 1) * 384], psA[hh][:, 0:384], ALU.add,
                   [r_psA[hh], D["r_nsum"]], [D["r_nsum"]])
            recip(dsum, dsum, [D["r_dsum"]], [D["r_dsum"]])
            tt("dve", nsum.rearrange("p (h d) -> p h d", h=6), nsum.rearrange("p (h d) -> p h d", h=6),
               dsum.unsqueeze(2).to_broadcast([128, 6, 128]), ALU.mult, [D["r_dsum"], D["r_nsum"]], [D["r_nsum"]])
            stt("dve", osmp, nsum[0:4, :], sel[:, b, 0:1], osmp, ALU.mult, ALU.add, [D["r_nsum"], r_sel, r_osmp], [r_osmp])
        def sampA_tail():
            gs_t = SA[0]["prod"][0:4, :].rearrange("p (h d) -> p h d", h=6); r_gst = SA[0]["r_prod"]
            act(gs_t, gsm, AF.Exp, [r_gsm], [r_gst], scale=-1.0)
            ts("dve", gs_t, gs_t, 1.0, None, ALU.add, None, [r_gst], [r_gst])
            recip(gs_t, gs_t, [r_gst], [r_gst])
            tt("dve", gs_t, gs_t, gsm, ALU.mult, [r_gst, r_gsm], [r_gst])
            tt("dve", zs[:, 0:768].rearrange("p (h d) -> p h d", h=6), gs_t, osmp.rearrange("p (h d) -> p h d", h=6), ALU.mult,
               [r_gst, r_osmp], [r_zs])

        for ui in range(24):
            A_unit(5, ui)
            if ui % 6 == 1:
                P.ctx = "sampA b%d" % (ui // 6)
                sampA_batch(ui // 6)
        P.ctx = "sampA tail"
        sampA_tail()
        P.ctx = "Bpre"
        load_w(0, w_in, [(3840, 128), (4608, 128), (3072, 128), (5376, 128)])
        areset()
        hg = a_f32([128, 4, 128]); r_hg = P.res("hg")
        dma("sp", [(hg[:], c_hg.rearrange("k p c -> p k c"))], r_hg, [], [r_hg])
        lbr = a_f32([128, 2, 768]); r_lbr = P.res("lbr")
        dma("sp", [(lbr[:, j], dap(lb_raw, j * 768, [[0, 128], [1, 768]])) for j in range(2)], r_lbr, [], [r_lbr])
        lbv = a_f32([128, 768]); oml = a_f32([128, 768]); r_lb = P.res("lb")
        tt("dve", lbv, lbr[:, 1], lbr[:, 0], ALU.subtract, [r_lbr], [r_lb])
        act(lbv, lbv, AF.Exp, [r_lb], [r_lb])
        ts("dve", lbv, lbv, 1.0, None, ALU.add, None, [r_lb], [r_lb])
        recip(lbv, lbv, [r_lb], [r_lb])
        ts("dve", oml, lbv, -1.0, 1.0, ALU.mult, ALU.add, [r_lb], [r_lb])
        ngb = a_f32([128, 768]); r_ngb = P.res("ngb")
        dma("sp", [(ngb, dap(norm_g, 0, [[0, 128], [1, 768]]))], r_ngb, [], [r_ngb])
        stS = a_f32([128, 4, 6, 128]); r_stS = P.res("stS")
        dma("sp", [(stS[:, b], st_in[b].rearrange("h k v -> k h v")) for b in range(4)], r_stS, [], [r_stS])
        smpB = a_f32([4, 4, 128]); r_smpB = P.res("smpB")
        NB = 3
        Bset = []
        for _i in range(NB):
            d = dict(ft=a_f32([128, 128]), logf=a_f32([128, 128]), kk=a_f32([128, 128]), ex=a_f32([128, 4, 128]),
                     prods=a_bf([128, 4, 128]), i_bf=a_bf([128, 128]), trT=a_bf([128, 3, 128]), attm=a_bf([128, 128]),
                     dec=a_f32([128, 2]), osb=a_f32([128, 128]), sq=a_f32([128, 128]), ssum=a_f32([128, 1]),
                     gtmp2=a_f32([128, 128]), sg=a_bf([128, 128]))
            for k in list(d.keys()):
                d["r_" + k] = P.res(k)
            Bset.append(d)
        Sf = a_f32([128, 128]); r_Sf = P.res("Sf")
        Sb = a_bf([128, 128]); r_Sb = P.res("Sb")
        colsT = a_f32([128, 3, 4]); r_colsT = P.res("colsT")
        qmask = [a_f32([128, 4]) for _ in range(2)]; r_qm = [P.res("qm") for _ in range(2)]
        ibc = [a_f32([128, 128]) for _ in range(2)]; r_ibc = [P.res("ibc") for _ in range(2)]
        Snew = [a_f32([128, 128]) for _ in range(2)]; r_Snew = [P.res("Snew") for _ in range(2)]
        smp_o = a_f32([4, 128]); r_smpo = P.res("smpo")
        s4 = a_f32([4, 4, 128]); r_s4 = P.res("s4")

        def gate_f(dst_f, src, M, lb_ap, oml_ap, R, Wr):
            act(dst_f, src, AF.Exp, R, [Wr], scale=-1.0)
            act(dst_f, dst_f, AF.Ln, [Wr], [Wr], bias=1.0)
            act(dst_f, dst_f, AF.Exp, [Wr], [Wr], scale=-1.0)
            tt("pool", dst_f, dst_f, oml_ap, ALU.mult, [Wr, r_lb], [Wr])
            tt("dve", dst_f, dst_f, lb_ap, ALU.add, [Wr, r_lb], [Wr])

        def rms_gate(o_ap, g_ap, M, h, out_bf, R_o, R_g, W_out, B):
            sq, ssum, sg, gtmp2 = B["sq"], B["ssum"], B["sg"], B["gtmp2"]
            r_sq, r_ss, r_sg, r_g2 = B["r_sq"], B["r_ssum"], B["r_sg"], B["r_gtmp2"]
            tt("dve", sq[0:M], o_ap, o_ap, ALU.mult, R_o, [r_sq])
            red("dve", ssum[0:M], sq[0:M], ALU.add, [r_sq], [r_ss])
            act(ssum[0:M], ssum[0:M], AF.Ln, [r_ss], [r_ss], bias=1e-6, scale=1.0 / 128.0)
            act(ssum[0:M], ssum[0:M], AF.Exp, [r_ss], [r_ss], scale=-0.5)
            stt("dve", sq[0:M], o_ap, ssum[0:M, 0:1], ngb[0:M, h * 128:(h + 1) * 128], ALU.mult, ALU.mult, R_o + [r_ss, r_ngb], [r_sq])
            silu_to("pool", sg[0:M], g_ap, M, R_g, [r_sg], gtmp2[0:M], r_g2)
            tt("dve", out_bf, sq[0:M], sg[0:M], ALU.mult, [r_sq, r_sg], W_out)

        Bps = []
        for _i in range(2):
            X, Y = psA[2 + 2 * _i], psA[3 + 2 * _i]
            rX, rY = r_psA[2 + 2 * _i], r_psA[3 + 2 * _i]
            Bps.append(dict(X=X, Y=Y, r_cums=rX, r_decp=rX, r_att=rY, r_o=rY, r_dS=rY))

        def load_wB(h):
            load_w(h % 2, w_in, [(3840 + h * 128, 128), (4608 + h * 128, 128), (3072 + h * 128, 128), (5376 + h * 128, 128)])

        for h in range(6):
            slot = h % 2
            lb_h, oml_h = lbv[:, h * 128:(h + 1) * 128], oml[:, h * 128:(h + 1) * 128]
            memset("dve", Sf, 0.0, [r_Sf])
            memset("pool", Sb, 0.0, [r_Sb])
            for t in range(16):
                own = t >= 8
                P.ctx = "B h%d t%d" % (h, t)
                if t == 2 and h + 1 < 6:
                    load_wB(h + 1)
                B = Bset[t % NB]
                Q = Bps[t % 2]
                ft, logf, kk, ex, prods, i_bf, trT, attm, dec, osb = (B[k] for k in ("ft", "logf", "kk", "ex", "prods", "i_bf", "trT", "attm", "dec", "osb"))
                stg, rs = project(slot, lambda kc, t=t: xT[:, kc, t * 128:(t + 1) * 128], 128, (512 if own else 256), [r_xTt[t]], ev="dve")
                gate_f(ft, stg[:, 0:128], 128, lb_h, oml_h, [rs], B["r_ft"])
                act(logf, ft, AF.Ln, [B["r_ft"]], [B["r_logf"]])
                ts("pool", kk, ft, -1.0, 1.0, ALU.mult, ALU.add, [B["r_ft"]], [B["r_kk"]])
                cp("pool", i_bf, stg[:, 128:256], [rs], [B["r_i_bf"]])
                pc = Q["X"]
                if own:
                    P.add("pe", lambda e, pc=pc, logf=logf: [e.matmul(pc[:, j * 128:(j + 1) * 128], lhsT=hg[:, j], rhs=logf, start=True, stop=True)
                                                             for j in range(2)][-1], [r_hg, B["r_logf"]], [Q["r_cums"]], cost=0.55)
                    mmi([(pc[:, 384:385], logf, onesf[:, 0:1]), (pc[:, 385:386], logf, hg[:, 2, 63:64])], [B["r_logf"], r_ones, r_hg], [Q["r_decp"]])
                    act(ex[:, 0:2], pc[:, 0:256].rearrange("p (a b) -> p a b", a=2), AF.Exp, [Q["r_cums"]], [B["r_ex"]])
                    act(ex[:, 3], pc[:, 0:128], AF.Exp, [Q["r_cums"]], [B["r_ex"]], scale=-1.0)
                    act(dec, pc[:, 384:386], AF.Exp, [Q["r_decp"]], [B["r_dec"]])
                else:
                    mmi([(pc[:, 128:256], hg[:, 1], logf), (pc[:, 384:385], logf, onesf[:, 0:1])], [r_hg, B["r_logf"], r_ones], [Q["r_cums"]])
                    act(ex[:, 1], pc[:, 128:256], AF.Exp, [Q["r_cums"]], [B["r_ex"]])
                    act(dec[:, 0:1], pc[:, 384:385], AF.Exp, [Q["r_decp"]], [B["r_dec"]])
                tt("dve", prods[:, 3], kk, ex[:, 1], ALU.mult, [B["r_kk"], B["r_ex"]], [B["r_prods"]])
                Y = Q["Y"]
                if own:
                    q_ap = stg[:, 256:384]
                    tt("dve", prods[:, 0], q_ap, ex[:, 0], ALU.mult, [rs, B["r_ex"]], [B["r_prods"]])
                    tt("pool", prods[:, 1], kk, ex[:, 3], ALU.mult, [B["r_kk"], B["r_ex"]], [B["r_prods"]])
                    pb, rpb = psB[t % 2], r_psB[t % 2]
                    trs([(pb[:, j * 128:(j + 1) * 128], prods[:, j]) for j in range(2)], identb[:], [B["r_prods"], r_idb], [rpb])
                    cp("dve", trT[:, 0:2], pb[:, 0:256].rearrange("p (a b) -> p a b", a=2), [rpb], [B["r_trT"]])
                    mms([(Y[:, 0:128], trT[:, 1], trT[:, 0])], [B["r_trT"]], [Q["r_att"]])
                    tt("dve", attm, Y[:, 0:128], hg[:, 3], ALU.mult, [Q["r_att"], r_hg], [B["r_attm"]])
                    ts("pool", Sb, Sf, dec[:, 1:2], None, ALU.mult, None, [r_Sf, B["r_dec"]], [r_Sb])
                    mms([(Y[:, 128:256], attm, i_bf), (Y[:, 128:256], trT[:, 0], Sb)], [B["r_attm"], B["r_i_bf"], B["r_trT"], r_Sb], [Q["r_o"]])
                    cp("act", osb, Y[:, 128:256], [Q["r_o"]], [B["r_osb"]])
                mms([(Y[:, 256:384], prods[:, 3], i_bf)], [B["r_prods"], B["r_i_bf"]], [Q["r_dS"]])
                stt("dve", Sf, Sf, dec[:, 0:1], Y[:, 256:384], ALU.mult, ALU.add, [r_Sf, B["r_dec"], Q["r_dS"]], [r_Sf])
                if own:
                    to = t - 8
                    rms_gate(osb, stg[:, 384:512], 128, h, z[:, to, 768 + h * 128:768 + (h + 1) * 128], [B["r_osb"]], [rs], [r_z[to]], B)
            dma("sp", [(pstate[h], Sf)], r_Sf, [r_Sf], [], is_out=True)
            B = Bset[0]
            stg, rs = project(slot, lambda kc: xsT[:, kc, :], 4, 512, [r_xsT])
            cp("pool", smpB, stg[0:4, :].rearrange("p (a b) -> p a b", a=4), [rs], [r_smpB])
            gate_f(s4[:, 0], smpB[:, 0], 4, lbv[0:4, h * 128:(h + 1) * 128], oml[0:4, h * 128:(h + 1) * 128], [r_smpB], r_s4)
            ts("dve", s4[:, 1], s4[:, 0], -1.0, 1.0, ALU.mult, ALU.add, [r_s4], [r_s4])
            cp("dve", s4[:, 2], smpB[:, 2], [r_smpB], [r_s4])
            P.add("pe", lambda e: [e.transpose(out=psA[2][:, j * 4:(j + 1) * 4], in_=s4[:, j], identity=identf[0:4, 0:4]) for j in range(3)][-1],
                  [r_s4, r_idf], [Bps[0]["r_cums"]], cost=0.4)
            cp("act", colsT, psA[2][:, 0:12].rearrange("p (a b) -> p a b", a=3), [Bps[0]["r_cums"]], [r_colsT])
            for b in range(4):
                sn, rsn = Snew[b % 2], r_Snew[b % 2]
                Yb = Bps[b % 2]
                mms([(Yb["Y"][:, 0:128], sel[:, b, :], smpB[:, 1])], [r_sel, r_smpB], [Yb["r_att"]])
                ts("dve", ibc[b % 2], Yb["Y"][:, 0:128], colsT[:, 1, b:b + 1], None, ALU.mult, None, [Yb["r_att"], r_colsT], [r_ibc[b % 2]])
                stt("dve", sn, stS[:, b, h], colsT[:, 0, b:b + 1], ibc[b % 2], ALU.mult, ALU.add, [r_stS, r_colsT, r_ibc[b % 2]], [rsn])
                dma("sp", [(sstate[b, h], sn)], rsn, [rsn], [], is_out=True)
                tt("dve", qmask[b % 2], selc[:, b * 4:(b + 1) * 4], colsT[:, 2, b:b + 1].to_broadcast([128, 4]), ALU.mult, [r_selc, r_colsT], [r_qm[b % 2]])
                P.add("pe", lambda e, b=b, sn=sn: e.matmul(psA[4][0:4, 256:384], lhsT=qmask[b % 2], rhs=sn, start=(b == 0), stop=(b == 3)),
                      [r_qm[b % 2], rsn], [r_psA[4]], cost=0.3)
            cp("act", smp_o, psA[4][0:4, 256:384], [r_psA[4]], [r_smpo])
            rms_gate(smp_o, smpB[:, 3], 4, h, zs[:, 768 + h * 128:768 + (h + 1) * 128], [r_smpo], [r_smpB], [r_zs], B)

        P.ctx = "Mpre"
        load_w(0, w_mem, [(0, 512)])
        load_w(1, w_mem, [(512, 512)])
        areset()
        P.ctx = "M"
        smpM = a_f32([4, 2, 4, 128]); r_smpM = P.res("smpM")
        mark_M = apos[0]
        def P_res_tmp(G):
            if "rt" not in G:
                G["rt"] = P.res("mgt")
            return G["rt"]

        recsL = [a_f32([128, 3, 6, 130]) for _ in range(2)]; r_recsL = [P.res("recs") for _ in range(2)]
        MG = []
        for _i in range(2):
            d = dict(Mx=a_f32([128, 6]), wv=a_f32([128, 3, 6]), wd=a_f32([128, 3, 6]), Dn=a_f32([128, 6]),
                     oacc=a_f32([128, 6, 128]))
            d["r"] = P.res("mg")
            d["r2"] = P.res("mg2")
            MG.append(d)
        def merge_tile(to):
            recs, r_recs = recsL[to % 2], r_recsL[to % 2]
            G = MG[to % 2]
            Mx, wv, wd, Dn, oacc, r_mg, r_mg2 = G["Mx"], G["wv"], G["wd"], G["Dn"], G["oacc"], G["r"], G["r2"]
            dma("sp", [(recs[:], rec_scr[to * 128:(to + 1) * 128])], r_recs, [], [r_recs])
            mvw = recs[:, :, :, 128]
            dvw = recs[:, :, :, 129]
            tt("dve", Mx, mvw[:, 0], mvw[:, 1], ALU.min, [r_recs], [r_mg])
            tt("dve", Mx, Mx, mvw[:, 2], ALU.min, [r_recs, r_mg], [r_mg])
            tt("dve", wv, mvw, Mx.unsqueeze(1).to_broadcast([128, 3, 6]), ALU.subtract, [r_recs, r_mg], [r_mg])
            act(wv, wv, AF.Exp, [r_mg], [r_mg], scale=-1.0)
            tt("dve", wd, wv, dvw, ALU.mult, [r_recs, r_mg], [r_mg])
            tt("dve", Dn, wd[:, 0], wd[:, 1], ALU.add, [r_mg], [r_mg])
            tt("dve", Dn, Dn, wd[:, 2], ALU.add, [r_mg], [r_mg])
            recip(Dn, Dn, [r_mg], [r_mg])
            tt("dve", wv, wv, Dn.unsqueeze(1).to_broadcast([128, 3, 6]), ALU.mult, [r_mg], [r_mg])
            tt("dve", oacc, recs[:, 0, :, 0:128], wv[:, 0].unsqueeze(2).to_broadcast([128, 6, 128]), ALU.mult, [r_recs, r_mg], [r_mg2])
            for r in (1, 2):
                for hh in range(6):
                    stt("dve", oacc[:, hh], recs[:, r, hh, 0:128], wv[:, r, hh:hh + 1], oacc[:, hh], ALU.mult, ALU.add,
                        [r_recs, r_mg, r_mg2], [r_mg2])
            zv = z[:, to, 0:768].rearrange("p (h d) -> p h d", h=6)
            tt("dve", zv, zv, oacc, ALU.mult, [r_mg2, r_z[to]], [r_z[to]])

        memT = a_bf([128, 16, 256]); r_memT = P.res("memT")
        mb = [a_bf([128, 2048]) for _ in range(2)]; r_mb = [P.res("mb") for _ in range(2)]
        for t in range(2):
            dma("pool", [(mb[t], mem[t * 128:(t + 1) * 128, :])], r_mb[t], [], [r_mb[t]])
            for hb in range(2):
                trs([(psB[hb][:, j * 128:(j + 1) * 128], mb[t][:, (hb * 8 + j) * 128:(hb * 8 + j + 1) * 128]) for j in range(8)],
                    identb[:], [r_mb[t], r_idb], [r_psB[hb]])
                cp("act" if hb == 0 else "dve", memT[:, hb * 8:(hb + 1) * 8, t * 128:(t + 1) * 128],
                   psB[hb][:].rearrange("p (j c) -> p j c", j=8), [r_psB[hb]], [r_memT])
        mkv_b = a_bf([128, 2, 2, 512]); r_mkvb = P.res("mkvb")
        mkT = a_bf([128, 4, 256]); r_mkT = P.res("mkT")
        for kv in range(2):
            slot = kv
            for t in range(2):
                stg, rs = project(slot, lambda kc, t=t: memT[:, kc, t * 128:(t + 1) * 128], 128, 512, [r_memT])
                dma("sp", [((pmk if kv == 0 else pmv)[t * 128:(t + 1) * 128, :], stg[:, :])], rs, [rs], [], is_out=True)
                cp("pool", mkv_b[:, kv, t], stg[:, :], [rs], [r_mkvb])
                if kv == 0:
                    trs([(psB[0][:, j * 128:(j + 1) * 128], mkv_b[:, 0, t, j * 128:(j + 1) * 128]) for j in range(4)], identb[:],
                        [r_mkvb, r_idb], [r_psB[0]])
                    cp("act", mkT[:, :, t * 128:(t + 1) * 128], psB[0][:, 0:512].rearrange("p (a b) -> p a b", a=4), [r_psB[0]], [r_mkT])
        NM = 3
        Mset = []
        for _i in range(NM):
            d = dict(qmb=a_bf([128, 128]), qmT=a_bf([128, 128]), mxM=a_f32([128, 2]), pM=a_bf([128, 256]), pMT=a_bf([128, 2, 128]),
                     oM=a_f32([128, 128]), sgM=a_bf([128, 128]), gtmp3=a_f32([128, 128]))
            for k in list(d.keys()):
                d["r_" + k] = P.res(k)
            d["r_denM"] = P.res("denM")
            Mset.append(d)
        mi = 0
        for p in range(2):
            slot = p
            load_w(slot, w_in, [(6144 + p * 256, 256), (6656 + p * 256, 256)])
            if p == 1:
                dma("pool", [(xT[:, :, c * 512:(c + 1) * 512], dap(w_out, c * 512, [[2048, 128], [128 * 2048, 16], [1, 512]])) for c in (0, 1)],
                    r_xTc, [], r_xTt[0:8])
            for t in range(9):
                if t == 8:
                    stg, rs = project(slot, lambda kc: xsT[:, kc, :], 4, 512, [r_xsT])
                    cp("pool", smpM[:, p], stg[0:4, :].rearrange("p (a b) -> p a b", a=4), [rs], [r_smpM])
                    continue
                stg, rs = project(slot, lambda kc, t=t: xT[:, kc, (8 + t) * 128:(9 + t) * 128], 128, 512, [r_xTt[8 + t]])
                for j in range(2):
                    hm = 2 * p + j
                    D = Mset[mi % NM]
                    par = mi % 2
                    mi += 1
                    qmb, qmT, mxM, pM, pMT, oM, sgM, gtmp3 = (D[k] for k in ("qmb", "qmT", "mxM", "pM", "pMT", "oM", "sgM", "gtmp3"))
                    pS, rpS = psA[2 + par], r_psA[2 + par]
                    pO, rpO = psA[4 + par], r_psA[4 + par]
                    pB, rpB = psB[par], r_psB[par]
                    cp("pool", qmb, stg[:, j * 128:(j + 1) * 128], [rs], [D["r_qmb"]])
                    trs([(pB[:, 0:128], qmb)], identb[:], [D["r_qmb"], r_idb], [rpB])
                    cp("act", qmT, pB[:, 0:128], [rpB], [D["r_qmT"]])
                    mms([(pS[:, 0:256], qmT, mkT[:, hm])], [D["r_qmT"], r_mkT], [rpS])
                    red("dve", mxM[:, 0:1], pS[:, 0:256], ALU.max, [rpS], [D["r_mxM"]])
                    ts("dve", mxM[:, 0:1], mxM[:, 0:1], -SCALE, None, ALU.mult, None, [D["r_mxM"]], [D["r_mxM"]])
                    memset("pool", mxM[:, 1:2], 0.0, [D["r_denM"]])
                    act(pM, pS[:, 0:256], AF.Exp, [rpS, D["r_mxM"]], [D["r_pM"], D["r_denM"]], bias=mxM[:, 0:1], scale=SCALE, accum=mxM[:, 1:2])
                    trs([(pB[:, 256 + jj * 128:256 + (jj + 1) * 128], pM[:, jj * 128:(jj + 1) * 128]) for jj in range(2)], identb[:],
                        [D["r_pM"], r_idb], [rpB])
                    cp("dve", pMT, pB[:, 256:512].rearrange("p (a b) -> p a b", a=2), [rpB], [D["r_pMT"]])
                    mms([(pO[:, 0:128], pMT[:, 0], mkv_b[:, 1, 0, hm * 128:(hm + 1) * 128]),
                         (pO[:, 0:128], pMT[:, 1], mkv_b[:, 1, 1, hm * 128:(hm + 1) * 128])], [D["r_pMT"], r_mkvb], [rpO])
                    recip(mxM[:, 1:2], mxM[:, 1:2], [D["r_denM"]], [D["r_denM"]])
                    ts("dve", oM, pO[:, 0:128], mxM[:, 1:2], None, ALU.mult, None, [rpO, D["r_denM"]], [D["r_oM"]])
                    silu_to("pool", sgM, stg[:, 256 + j * 128:256 + (j + 1) * 128], 128, [rs], [D["r_sgM"]], gtmp3, D["r_gtmp3"])
                    tt("dve", z[:, t, 1536 + hm * 128:1536 + (hm + 1) * 128], oM, sgM, ALU.mult, [D["r_oM"], D["r_sgM"]], [r_z[t]])
                if p == 0:
                    merge_tile(t)
        def sampM_alloc():
            return (a_f32([128, 2, 512]), a_f32([128, 2, 512]), a_f32([128, 512]), a_f32([128, 2, 512]), a_f32([128, 2, 4]), a_f32([128, 8]),
                    a_f32([128, 4]), a_f32([128, 512]), a_f32([4, 512]), a_f32([4, 2, 2, 128]))
        r_Km = P.res("Km"); r_Vm = P.res("Vm"); r_qbm = P.res("qbm"); r_prm = P.res("prm"); r_sxm = P.res("sxm"); r_accm = P.res("accm"); r_osm = P.res("osm")
        SMB = {}
        def sampM_batch(b):
            Km, Vm, qbm, prm, sxm, srf, dsm, nsm, osm, gm_t = SMB["bufs"]
            dma("sp", [(Km[:], cmk[b].rearrange("(t p) c -> p t c", p=128))], r_Km, [], [r_Km])
            dma("sp", [(Vm[:], cmv[b].rearrange("(t p) c -> p t c", p=128))], r_Vm, [], [r_Vm])
            for p in range(2):
                mms([(psA[p][:, 0:256], sel[:, b, :], smpM[:, p, 0:2, :])], [r_sel, r_smpM], [r_psA[p]])
                cp("act", qbm[:, p * 256:(p + 1) * 256], psA[p][:, 0:256], [r_psA[p]], [r_qbm])
            tt("dve", prm, Km, qbm.unsqueeze(1).to_broadcast([128, 2, 512]), ALU.mult, [r_Km, r_qbm], [r_prm])
            red("dve", sxm, prm.rearrange("p t (h d) -> p t h d", h=4), ALU.add, [r_prm], [r_sxm])
            mms([(psA[2][:, 0:8], onesf[:], sxm.rearrange("p a b -> p (a b)"))], [r_ones, r_sxm], [r_psA[2]])
            ts("dve", srf, psA[2][:, 0:8], 1.0 / 128.0, None, ALU.mult, None, [r_psA[2]], [r_sxm])
            tt("dve", sxm, sxm, srf[:, 0:4].unsqueeze(1).to_broadcast([128, 2, 4]), ALU.subtract, [r_sxm], [r_sxm])
            act(sxm, sxm, AF.Exp, [r_sxm], [r_sxm], scale=SCALE)
            tt("dve", prm.rearrange("p t (h d) -> p t h d", h=4), Vm.rearrange("p t (h d) -> p t h d", h=4),
               sxm.unsqueeze(3).to_broadcast([128, 2, 4, 128]), ALU.mult, [r_Vm, r_sxm], [r_prm])
            mms([(psA[2][:, 0:4], onesf[:], sxm[:, 0]), (psA[2][:, 0:4], onesf[:], sxm[:, 1])], [r_ones, r_sxm], [r_psA[2]])
            cp("dve", dsm, psA[2][:, 0:4], [r_psA[2]], [r_accm])
            mms([(psA[3][:, 0:512], onesf[:], prm[:, 0]), (psA[3][:, 0:512], onesf[:], prm[:, 1])], [r_ones, r_prm], [r_psA[3]])
            recip(dsm, dsm, [r_accm], [r_accm])
            tt("dve", nsm.rearrange("p (h d) -> p h d", h=4), psA[3][:, 0:512].rearrange("p (h d) -> p h d", h=4),
               dsm.unsqueeze(2).to_broadcast([128, 4, 128]), ALU.mult, [r_psA[3], r_accm], [r_accm])
            stt("dve", osm, nsm[0:4, :], sel[:, b, 0:1], osm, ALU.mult, ALU.add, [r_accm, r_sel, r_osm], [r_osm])
        def sampM_tail():
            Km, Vm, qbm, prm, sxm, srf, dsm, nsm, osm, gm_t = SMB["bufs"]
            r_gmt = P.res("gmt")
            act(gm_t, smpM[:, :, 2:4, :], AF.Exp, [r_smpM], [r_gmt], scale=-1.0)
            ts("dve", gm_t, gm_t, 1.0, None, ALU.add, None, [r_gmt], [r_gmt])
            recip(gm_t, gm_t, [r_gmt], [r_gmt])
            tt("dve", gm_t, gm_t, smpM[:, :, 2:4, :], ALU.mult, [r_gmt, r_smpM], [r_gmt])
            tt("dve", zs[:, 1536:2048].rearrange("p (a b d) -> p a b d", a=2, b=2), gm_t,
               osm.rearrange("p (a b d) -> p a b d", a=2, b=2), ALU.mult, [r_gmt, r_osm], [r_zs])

        areset()
        apos[0] = mark_M
        SMB["bufs"] = sampM_alloc()
        memset("dve", SMB["bufs"][8], 0.0, [r_osm])
        wo = xT
        dma("pool", [(wo[:, :, c * 512:(c + 1) * 512], dap(w_out, c * 512, [[2048, 128], [128 * 2048, 16], [1, 512]])) for c in (2, 3)],
            r_xTo, [], r_xTt[8:16])
        zT = wbuf[:].rearrange("p s k c -> p (s k c)").rearrange("p (k t) -> p k t", k=16)
        r_zT = P.res("zT")
        zsT = a_bf([128, 16, 4]); r_zsT = P.res("zsT")
        for t in range(8):
            for hb in range(2):
                trs([(psB[hb][:, j * 128:(j + 1) * 128], z[:, t, (hb * 8 + j) * 128:(hb * 8 + j + 1) * 128]) for j in range(8)],
                    identb[:], [r_z[t], r_idb], [r_psB[hb]])
                cp("act" if hb == 0 else "dve", zT[:, hb * 8:(hb + 1) * 8, t * 128:(t + 1) * 128],
                   psB[hb][:].rearrange("p (j c) -> p j c", j=8), [r_psB[hb]], [r_zT] + r_w)
        gbc = a_f32([128, 2048]); bbc = a_f32([128, 2048]); r_gb = P.res("gb")
        dma("sp", [(gbc, dap(ln_g, 0, [[0, 128], [1, 2048]])), (bbc, dap(ln_b, 0, [[0, 128], [1, 2048]]))], r_gb, [], [r_gb])
        rr = [a_f32([128, 2048]) for _ in range(2)]; r_rr = [P.res("rr") for _ in range(2)]
        sqo = a_f32([128, 2048]); r_sqo = P.res("sqo")
        stat = a_f32([128, 4]); r_stat = P.res("stat")
        for t in range(9):
            M = 128 if t < 8 else 4
            s = t % 2
            rv, rrv = rr[s], r_rr[s]
            xr, rxr = rv, rrv
            P.ctx = "O t%d" % t
            if t < 8:
                dma("sp", [(xr, x_all[1024 + t * 128:1024 + (t + 1) * 128, :])], rxr, [], [rxr])
                if t % 2 == 0:
                    sampM_batch(t // 2)
                    P.ctx = "O t%d" % t
            else:
                sampM_tail()
                trs([(psB[0][:, j * 4:(j + 1) * 4], zs[:, j * 128:(j + 1) * 128]) for j in range(16)], identb[0:4, 0:4], [r_zs, r_idb], [r_psB[0]])
                cp("act", zsT[:], psB[0][:, 0:64].rearrange("p (j c) -> p j c", j=16), [r_psB[0]], [r_zsT])
                dma("sp", [(xr[0:4], x_s)], rxr, [], [rxr])
            for c in range(4):
                ps, rps = psA[c], r_psA[c]
                if t < 8:
                    lst = [(ps[0:M, :], zT[:, kc, t * 128:(t + 1) * 128], wo[:, kc, c * 512:(c + 1) * 512]) for kc in range(16)]
                    mms(lst, [r_zT] + (r_xTt[0:8] if c < 2 else r_xTt[8:16]), [rps])
                else:
                    lst = [(ps[0:M, :], zsT[:, kc, :], wo[:, kc, c * 512:(c + 1) * 512]) for kc in range(16)]
                    mms(lst, [r_zsT] + (r_xTt[0:8] if c < 2 else r_xTt[8:16]), [rps])
                stt("dve", rv[0:M, c * 512:(c + 1) * 512], xr[0:M, c * 512:(c + 1) * 512], ALPHA, ps[0:M, :], ALU.mult, ALU.add,
                    [rxr, rps], [rrv])
            red("dve", stat[0:M, 0:1], rv[0:M], ALU.add, [rrv], [r_stat])
            ts("dve", stat[0:M, 0:1], stat[0:M, 0:1], 1.0 / 2048.0, None, ALU.mult, None, [r_stat], [r_stat])
            ts("dve", rv[0:M], rv[0:M], stat[0:M, 0:1], None, ALU.subtract, None, [rrv, r_stat], [rrv])
            tt("pool", sqo[0:M], rv[0:M], rv[0:M], ALU.mult, [rrv], [r_sqo])
            red("dve", stat[0:M, 1:2], sqo[0:M], ALU.add, [r_sqo], [r_stat])
            act(stat[0:M, 1:2], stat[0:M, 1:2], AF.Ln, [r_stat], [r_stat], bias=1e-5, scale=1.0 / 2048.0)
            act(stat[0:M, 1:2], stat[0:M, 1:2], AF.Exp, [r_stat], [r_stat], scale=-0.5)
            stt("dve", rv[0:M], rv[0:M], stat[0:M, 1:2], gbc[0:M], ALU.mult, ALU.mult, [rrv, r_stat, r_gb], [rrv])
            tt("pool", rv[0:M], rv[0:M], bbc[0:M], ALU.add, [rrv, r_gb], [rrv])
            if t < 8:
                dma("sp", [(y[t * 128:(t + 1) * 128, :], rv)], rrv, [rrv], [], is_out=True)
            else:
                dma("sp", [(ys, rv[0:4])], rrv, [rrv], [], is_out=True)
        cnt = P.emit()
        _bi = [o.idx for o in P.bar_ops] + [len(P.ops)]
        _prev = 0
        for _k, _b in enumerate(_bi):
            _tot = {e: 0.0 for e in P.ENGS}
            for o in P.ops[_prev:_b]:
                _tot[o.eng] += (0.06 if o.is_dma and o.eng != "pool" else (0.6 if o.is_dma else o.cost))
            print('phase', _k, 'ops', _b - _prev, {e: round(v) for e, v in _tot.items()})
            _prev = _b
        import os
        if os.environ.get("CRIT"):
            k = int(os.environ["CRIT"])
            o = P.bar_ops[k] if k < len(P.bar_ops) else max(P.ops, key=lambda x: x.fin)
            agg = {}
            chain = []
            while o is not None and (k == 0 or o.idx > P.bar_ops[k - 1].idx):
                key = (o.eng, "dma" if o.is_dma else "op")
                agg[key] = agg.get(key, 0.0) + (o.fin - o.start)
                chain.append(o)
                o = o.crit
            print("CRIT chain len", len(chain), {kk: round(v) for kk, v in agg.items()})
            for o in chain[-120:][::-1][:120]:
                print("   %8.1f %8.1f %s %s idx=%d" % (o.start, o.fin, o.eng, "dma" if o.is_dma else "op", o.idx))
        if os.environ.get("GAPS"):
            lo, hi = [float(x) for x in os.environ["GAPS"].split(",")]
            pe_ops = sorted([o for o in P.ops if o.eng == "pe"], key=lambda o: o.start)
            prev = None
            for o in pe_ops:
                if prev is not None and lo <= o.start <= hi and o.start - prev.fin > 1.0:
                    c = o.crit
                    print("  PE gap %.1f at %.1f before [%s] crit=(%s %s %s fin %.1f)" % (o.start - prev.fin, o.start, o.name, c.eng if c else None,
                          "dma" if (c is not None and c.is_dma) else "op", c.name if c else None, c.fin if c else 0))
                prev = o
        print('SCHED ops', len(P.ops), 'sim_end_us %.1f' % P.sim_end, cnt, 'barriers', ['%.0f' % o.fin for o in P.bar_ops])
    return nc


_NC = None


def _consts(half):
    ident = np.eye(128, dtype=np.float32)
    inv = 1.0 / (10000.0 ** (np.arange(64, dtype=np.float32) / 64.0))
    L = np.arange(2048)
    pos = (L if half == 1 else np.maximum(L - 1024, 0)).astype(np.float32)
    ang = pos[:, None] * inv[None, :].astype(np.float32)
    rope = np.stack([np.cos(ang), np.sin(ang)]).astype(np.float32)
    angs = np.full((4, 1), 8192.0, np.float32) * inv[None, :]
    rope_s = np.stack([np.cos(angs), np.sin(angs)]).astype(np.float32)
    qi = np.arange(128)[:, None]
    ki = np.arange(128)[None, :]
    band_prev = np.where(ki >= qi, 0.0, NEG)
    band_cur = np.where(ki <= qi, 0.0, NEG)
    ctx_ok = 0.0 if half == 1 else NEG
    b0 = np.concatenate([band_prev, band_cur], 1)
    b1 = np.concatenate([band_prev + ctx_ok, band_cur], 1)
    cq = qi // 64
    ck_ = ki // 64
    iq = qi % 64
    ik = ki % 64
    cls_prev = np.where(cq == ck_, 0.0, NEG) + ctx_ok
    cls_cur = np.where((cq == ck_) & (ik <= iq), 0.0, NEG)
    b2 = np.concatenate([cls_prev, cls_cur], 1)
    bias = np.maximum(np.stack([b0, b1, b2]), NEG).astype(np.float32)
    s = np.arange(128)[:, None]
    t = np.arange(128)[None, :]
    tri = (s <= t).astype(np.float32)
    mid = tri - (s <= 63).astype(np.float32)
    upper = (s > t).astype(np.float32)
    hg = np.stack([mid, upper, tri, tri]).astype(np.float32)
    sel = np.zeros((4, 4, 128), np.float32)
    selc = np.zeros((128, 16), np.float32)
    for b in range(4):
        sel[b, b, :] = 1.0
        selc[:, b * 4 + b] = 1.0
    return dict(c_ident=ident, c_rope=rope, c_rope_s=rope_s, c_bias=bias, c_hg=hg, c_sel=sel, c_selc=selc)


def kernel(x_prompt, x_sample, cache_win_k, cache_win_v, state_hgrn, cache_mem_k, cache_mem_v, mem_prompt,
           w_in, w_mem_kv, hgrn_lb_raw, hgrn_norm_g, w_out, ln_g, ln_b):
    global _NC
    if _NC is None:
        _NC = build_nc()
    f = lambda a: np.ascontiguousarray(np.asarray(a, dtype=np.float32))
    x_prompt, x_sample = f(x_prompt), f(x_sample)
    in_maps = []
    for c in range(8):
        b, half = c // 2, c % 2
        xa = np.zeros((2048, 2048), np.float32)
        xa[1024:] = x_prompt[b, half * 1024:(half + 1) * 1024]
        if half == 1:
            xa[:1024] = x_prompt[b, :1024]
        m = dict(
            x_all=xa, mem=f(mem_prompt[b]), x_s=f(x_sample[4 * c:4 * c + 4, 0]),
            ck=f(np.asarray(cache_win_k)[0, 4 * c:4 * c + 4]).reshape(4, 2048, 768),
            cv=f(np.asarray(cache_win_v)[0, 4 * c:4 * c + 4]).reshape(4, 2048, 768),
            st_in=f(np.asarray(state_hgrn)[0, 4 * c:4 * c + 4]),
            cmk=f(np.asarray(cache_mem_k)[0, 4 * c:4 * c + 4]).reshape(4, 256, 512),
            cmv=f(np.asarray(cache_mem_v)[0, 4 * c:4 * c + 4]).reshape(4, 256, 512),
            w_in=f(np.asarray(w_in)[0]), w_mem=f(np.asarray(w_mem_kv)[0]), w_out=f(np.asarray(w_out)[0]),
            lb_raw=f(hgrn_lb_raw), norm_g=f(hgrn_norm_g), ln_g=f(ln_g), ln_b=f(ln_b),
        )
        m.update(_consts(half))
        in_maps.append(m)
    res = run_bass_kernel_spmd(_NC, in_maps, core_ids=list(range(8)))
    R = res.results
    y_p = np.zeros((4, 2048, 2048), np.float32)
    y_s = np.zeros((32, 1, 2048), np.float32)
    pk = np.zeros((1, 4, 2048, 6, 128), np.float32)
    pv = np.zeros((1, 4, 2048, 6, 128), np.float32)
    pst = np.zeros((1, 4, 6, 128, 128), np.float32)
    pmk = np.zeros((1, 4, 256, 4, 128), np.float32)
    pmv = np.zeros((1, 4, 256, 4, 128), np.float32)
    sk = np.zeros((1, 32, 1, 6, 128), np.float32)
    sv = np.zeros((1, 32, 1, 6, 128), np.float32)
    sst = np.zeros((1, 32, 6, 128, 128), np.float32)
    for c in range(8):
        b, half = c // 2, c % 2
        r = R[c]
        sl = slice(half * 1024, (half + 1) * 1024)
        y_p[b, sl] = r["y"]
        y_s[4 * c:4 * c + 4, 0] = r["ys"]
        pk[0, b, sl] = np.asarray(r["pk"]).reshape(1024, 6, 128)
        pv[0, b, sl] = np.asarray(r["pv"]).reshape(1024, 6, 128)
        if half == 1:
            pst[0, b] = r["pstate"]
        else:
            pmk[0, b] = np.asarray(r["pmk"]).reshape(256, 4, 128)
            pmv[0, b] = np.asarray(r["pmv"]).reshape(256, 4, 128)
        sk[0, 4 * c:4 * c + 4, 0] = np.asarray(r["sk"]).reshape(4, 6, 128)
        sv[0, 4 * c:4 * c + 4, 0] = np.asarray(r["sv"]).reshape(4, 6, 128)
        sst[0, 4 * c:4 * c + 4] = r["sstate"]
    return (y_p, y_s, pk, pv, pst, pmk, pmv, sk, sv, sst)
```

```python
import numpy as np
from contextlib import ExitStack
import concourse.bass as bass
import concourse.mybir as mybir
from concourse.bass_utils import run_bass_kernel_spmd

F32 = mybir.dt.float32
BF16 = mybir.dt.bfloat16
AF = mybir.ActivationFunctionType
ALU = mybir.AluOpType
AX = mybir.AxisListType

NEG = -30000.0
ALPHA = 2.0 ** 0.25
SCALE = 128.0 ** -0.5


class Res:
    __slots__ = ("name", "writer", "readers", "dsem", "dcount", "excl")

    def __init__(self, name):
        self.name = name
        self.excl = False
        self.writer = None
        self.readers = []
        self.dsem = None
        self.dcount = 0


class Op:
    __slots__ = ("eng", "fn", "deps", "is_dma", "res", "dval", "needed", "mval", "tag", "cost", "idx", "sched", "fin", "crit", "start", "name", "is_bar", "single")

    def __init__(self, eng, fn, tag=""):
        self.eng = eng
        self.fn = fn
        self.cost = 0.3
        self.single = False
        self.is_bar = False
        self.idx = 0
        self.sched = False
        self.fin = 0.0
        self.deps = []
        self.is_dma = False
        self.res = None
        self.dval = 0
        self.needed = False
        self.mval = 0
        self.tag = tag


class Prog:
    ENGS = ["pe", "act", "dve", "pool", "sp"]

    def __init__(self, nc, stack):
        self.nc = nc
        self.stack = stack
        self.ops = []
        self.nres = 0
        self.out_ops = []
        self.bar = []
        self.last = {}
        self.dmas_since = []
        self.last_dma = {}
        self.bar_fn = None
        self.bar_ops = []
        self.pe_lat = 0.5
        self.ctx = ''

    def res(self, name=None):
        self.nres += 1
        return Res(f"{name or 'r'}{self.nres}")

    def _deps(self, op, reads, writes):
        ex = [r for r in reads if r.excl and r not in writes]
        if ex:
            writes = writes + ex
            reads = [r for r in reads if not r.excl]
        deps = list(self.bar)
        for r in reads:
            if r.writer is not None:
                deps.append(r.writer)
        for w in writes:
            if w.writer is not None:
                deps.append(w.writer)
            deps.extend(w.readers)
        seen = set()
        for d in deps:
            if id(d) not in seen and d is not op:
                seen.add(id(d))
                op.deps.append(d)
        for w in writes:
            w.writer = op
            w.readers = []
        for r in reads:
            if r not in writes:
                r.readers.append(op)

    def add(self, eng, fn, reads=(), writes=(), tag="", cost=0.3, single=False):
        op = Op(eng, fn, tag)
        op.cost = cost
        op.single = single
        op.name = self.ctx
        self._deps(op, list(reads), list(writes))
        op.idx = len(self.ops)
        self.ops.append(op)
        self.last[eng] = op
        return op

    def dma(self, issuer, fn, n, primary, reads=(), writes=(), tag="", is_out=False, cost=3.0):
        op = Op(issuer, fn, tag)
        op.is_dma = True
        op.res = primary
        op.cost = cost
        op.name = self.ctx + ":dma:" + primary.name
        self._deps(op, list(reads), list(writes))
        prev = self.last_dma.get(id(primary))
        if prev is not None and prev not in op.deps:
            op.deps.append(prev)
        self.last_dma[id(primary)] = op
        op.idx = len(self.ops)
        primary.dcount += 16 * n
        op.dval = primary.dcount
        op.mval = n
        self.ops.append(op)
        self.dmas_since.append(op)
        if is_out:
            self.out_ops.append(op)
        return op

    def barrier(self):
        deps = [o for o in self.last.values()] + self.dmas_since
        self.dmas_since = []
        self.bar = []
        op = self.add("pool", self.bar_fn, [], [], cost=0.2)
        op.is_bar = True
        for d in deps:
            if d is not op and d not in op.deps:
                op.deps.append(d)
        self.bar = [op]
        self.bar_ops.append(op)

    def schedule(self, W=48, LAT=0.5):
        ops = self.ops
        per = {e: [o for o in ops if o.eng == e] for e in self.ENGS}
        head = {e: 0 for e in self.ENGS}
        free = {e: 0.0 for e in self.ENGS}
        order = {e: [] for e in self.ENGS}
        left = len(ops)
        nsched = 0
        issue = {"sp": 0.06, "act": 0.06, "pool": 0.6}
        while left:
            best = None
            for e in self.ENGS:
                lst = per[e]
                h = head[e]
                while h < len(lst) and lst[h].sched:
                    h += 1
                head[e] = h
                cnt = 0
                i = h
                fe = free[e]
                while i < len(lst) and cnt < W:
                    op = lst[i]
                    i += 1
                    if op.sched:
                        continue
                    cnt += 1
                    if op.is_bar and nsched < op.idx:
                        continue
                    est = fe
                    ok = True
                    cr = None
                    for d in op.deps:
                        if not d.sched:
                            ok = False
                            break
                        t = d.fin + (LAT if (e != "pe" or d.eng == "pe") else self.pe_lat)
                        if t > est:
                            est = t
                            cr = d
                    if not ok:
                        continue
                    op.tag = cr
                    key = (est + 0.002 * (cnt - 1), op.idx)
                    if best is None or key < best[0]:
                        best = (key, e, op, est)
            assert best is not None, "scheduler stuck"
            _, e, op, est = best
            if op.is_bar:
                for e2 in self.ENGS:
                    if order[e2]:
                        d = order[e2][-1]
                        if d is not op and d not in op.deps:
                            op.deps.append(d)
                        if d.fin + LAT > est:
                            est = d.fin + LAT
            nsched += 1
            op.sched = True
            op.mval = op.mval if op.is_dma else 0
            op.crit = op.tag if op.tag is not None else (order[e][-1] if order[e] else None)
            op.start = est
            if op.is_dma:
                free[e] = est + issue[e]
                op.fin = est + issue[e] + op.cost
            else:
                free[e] = est + op.cost
                op.fin = free[e]
            order[e].append(op)
            left -= 1
        self.sim_end = max(o.fin for o in ops)
        return order

    def emit(self):
        nc, stack = self.nc, self.stack
        per = self.schedule()
        for op in self.ops:
            for d in op.deps:
                d.needed = True
        esem = {e: stack.enter_context(nc.semaphore(f"esem_{e}")) for e in self.ENGS}
        for op in self.ops:
            if op.is_dma and op.res.dsem is None:
                op.res.dsem = stack.enter_context(nc.semaphore(f"ds_{op.res.name}"))
        cnt = {e: 0 for e in self.ENGS}
        for e in self.ENGS:
            for op in per[e]:
                if not op.is_dma and op.needed:
                    cnt[e] += 1
                    op.dval = cnt[e]

        def done(op):
            if op.is_dma:
                return op.res.dsem, op.dval
            return esem[op.eng], op.dval

        out_waits = {}
        for op in self.out_ops:
            s, v = done(op)
            if id(s) not in out_waits or out_waits[id(s)][1] < v:
                out_waits[id(s)] = (s, v)
        block = stack.enter_context(nc.Block())

        def make(e):
            def body(engine):
                waited = {}
                for op in per[e]:
                    need = {}
                    for d in op.deps:
                        s, v = done(d)
                        if waited.get(id(s), 0) >= v:
                            continue
                        cur = need.get(id(s))
                        if cur is None or v > cur[1]:
                            need[id(s)] = (s, v, d.fin)
                    pend = []
                    for (s, v, f) in sorted(need.values(), key=lambda x: x[2]):
                        waited[id(s)] = v
                        pend.append((s, v))
                    fused = None
                    if (op.single or op.is_dma) and pend:
                        fused = pend.pop()
                    for (s, v) in pend:
                        engine.wait_ge(s, v)
                    if op.is_dma:
                        insts = op.fn(engine)
                        assert len(insts) == op.mval, (op.tag, len(insts), op.mval)
                        if fused is not None:
                            insts[0]._wait_ge(fused[0], fused[1])
                        for ins in insts:
                            ins.then_inc(op.res.dsem, 16)
                    else:
                        res = op.fn(engine)
                        first, ins = res if isinstance(res, tuple) else (res, res)
                        if fused is not None:
                            first._wait_ge(fused[0], fused[1])
                        if op.needed:
                            ins.then_inc(esem[e], 1)
                if e == "sp":
                    for s, v in out_waits.values():
                        engine.wait_ge(s, v)
            return body

        block.tensor(make("pe"))
        block.scalar(make("act"))
        block.vector(make("dve"))
        block.gpsimd(make("pool"))
        block.sync(make("sp"))
        return cnt


def build_nc():
    nc = bass.Bass("TRN2", target_bir_lowering=False)

    def din(name, shape, dt=F32):
        return nc.dram_tensor(name, list(shape), dt, kind="ExternalInput").ap()

    def dout(name, shape, dt=F32):
        return nc.dram_tensor(name, list(shape), dt, kind="ExternalOutput").ap()

    x_all = din("x_all", [2048, 2048])
    mem = din("mem", [256, 2048])
    x_s = din("x_s", [4, 2048])
    ck = din("ck", [4, 2048, 768])
    cv = din("cv", [4, 2048, 768])
    st_in = din("st_in", [4, 6, 128, 128])
    cmk = din("cmk", [4, 256, 512])
    cmv = din("cmv", [4, 256, 512])
    w_in = din("w_in", [2048, 7168])
    w_mem = din("w_mem", [2048, 1024])
    w_out = din("w_out", [2048, 2048])
    lb_raw = din("lb_raw", [2, 768])
    norm_g = din("norm_g", [1, 768])
    ln_g = din("ln_g", [1, 2048])
    ln_b = din("ln_b", [1, 2048])
    c_ident = din("c_ident", [128, 128])
    c_rope = din("c_rope", [2, 2048, 64])
    c_rope_s = din("c_rope_s", [2, 4, 64])
    c_bias = din("c_bias", [3, 128, 256])
    c_hg = din("c_hg", [4, 128, 128])
    c_sel = din("c_sel", [4, 4, 128])
    c_selc = din("c_selc", [128, 16])

    y = dout("y", [1024, 2048])
    ys = dout("ys", [4, 2048])
    pk = dout("pk", [1024, 768])
    pv = dout("pv", [1024, 768])
    pstate = dout("pstate", [6, 128, 128])
    pmk = dout("pmk", [256, 512])
    pmv = dout("pmv", [256, 512])
    sk = dout("sk", [4, 768])
    sv = dout("sv", [4, 768])
    sstate = dout("sstate", [4, 6, 128, 128])

    v_scr = nc.dram_tensor("v_scr", [2048, 768], BF16, kind="Internal").ap()
    rec_scr = nc.dram_tensor("rec_scr", [1024, 3, 6, 130], F32, kind="Internal").ap()
    smp_scr = nc.dram_tensor("smp_scr", [4, 6, 4, 128], F32, kind="Internal").ap()

    with ExitStack() as st:
        P = Prog(nc, st)

        def sb(name, shape, dt=F32):
            return st.enter_context(nc.sbuf_tensor(name, list(shape), dt))

        def psb(name, shape, dt=F32):
            return st.enter_context(nc.psum_tensor(name, list(shape), dt))

        def fap(base, off, dims):
            return bass.AP(tensor=base.tensor, offset=base.offset + off, ap=[list(base.ap[0])] + [list(d) for d in dims])

        def dap(base, off, dims):
            return bass.AP(tensor=base.tensor, offset=base.offset + off, ap=[list(d) for d in dims])

        def fsz(ap):
            n = 1
            for d in ap.shape[1:]:
                n *= d
            return n

        def ecost(eng, n, slow=1.0):
            if eng == "dve":
                return 0.12 + n * 1.05e-3 * slow
            if eng == "pool":
                return 0.25 + n * 1.8e-3 * slow
            return 0.2 + n * 0.85e-3

        def tt(eng, out, in0, in1, op, R, W):
            return P.add(eng, lambda e: e.tensor_tensor(out=out, in0=in0, in1=in1, op=op), R, W, cost=ecost(eng, fsz(out)), single=True)

        def ts(eng, out, in0, s1, s2, op0, op1, R, W):
            c = ecost(eng, fsz(out))
            if op1 is None:
                return P.add(eng, lambda e: e.tensor_scalar(out=out, in0=in0, scalar1=s1, scalar2=None, op0=op0), R, W, cost=c, single=True)
            return P.add(eng, lambda e: e.tensor_scalar(out=out, in0=in0, scalar1=s1, scalar2=s2, op0=op0, op1=op1), R, W, cost=c, single=True)

        def stt(eng, out, in0, scalar, in1, op0, op1, R, W):
            return P.add(eng, lambda e: e.scalar_tensor_tensor(out=out, in0=in0, scalar=scalar, in1=in1, op0=op0, op1=op1), R, W,
                         cost=ecost(eng, fsz(out)), single=True)

        def act(out, in_, func, R, W, bias=0.0, scale=1.0, accum=None):
            c = ecost("act", fsz(out))
            if accum is None:
                return P.add("act", lambda e: e.activation(out=out, in_=in_, func=func, bias=bias, scale=scale), R, W, cost=c, single=True)
            return P.add("act", lambda e: e.activation(out=out, in_=in_, func=func, bias=bias, scale=scale, accum_out=accum), R, W, cost=c, single=True)

        def cp(eng, out, in_, R, W):
            c = ecost(eng, fsz(out))
            if eng == "act":
                return P.add("act", lambda e: e.activation(out=out, in_=in_, func=AF.Copy), R, W, cost=c, single=True)
            return P.add(eng, lambda e: e.tensor_copy(out=out, in_=in_), R, W, cost=c, single=True)

        def memset(eng, out, val, W):
            return P.add(eng, lambda e: e.memset(out, val), [], W, cost=ecost(eng, fsz(out)) * 0.6, single=True)

        def red(eng, out, in_, op, R, W):
            return P.add(eng, lambda e: e.tensor_reduce(out=out, in_=in_, axis=AX.X, op=op), R, W, cost=ecost(eng, fsz(in_)), single=True)

        def recip(out, in_, R, W):
            return P.add("dve", lambda e: e.reciprocal(out=out, in_=in_), R, W, cost=ecost("dve", fsz(out), 6.0), single=True)

        def mmcost(o, l):
            n = fsz(o)
            c = max(n, 128) / 2000.0
            if l.dtype == F32:
                c *= 4.0
            return c + 0.02

        def mms(lst, R, W):
            def fn(e):
                ins = None
                first = None
                n = len(lst)
                for i, (o, l, r) in enumerate(lst):
                    ins = e.matmul(o, lhsT=l, rhs=r, start=(i == 0), stop=(i == n - 1))
                    if first is None:
                        first = ins
                return (first, ins)
            return P.add("pe", fn, R, W, cost=sum(mmcost(o, l) for (o, l, r) in lst), single=True)

        def mmi(lst, R, W):
            def fn(e):
                ins = None
                first = None
                for (o, l, r) in lst:
                    ins = e.matmul(o, lhsT=l, rhs=r, start=True, stop=True)
                    if first is None:
                        first = ins
                return (first, ins)
            return P.add("pe", fn, R, W, cost=sum(mmcost(o, l) for (o, l, r) in lst), single=True)

        def trs(lst, ident, R, W):
            def fn(e):
                ins = None
                first = None
                for (o, i) in lst:
                    ins = e.transpose(out=o, in_=i, identity=ident)
                    if first is None:
                        first = ins
                return (first, ins)
            return P.add("pe", fn, R, W, cost=0.09 * len(lst), single=True)

        def dma(issuer, pairs, primary, R, W, is_out=False):
            def fn(e):
                return [e.dma_start(out=o, in_=i) for (o, i) in pairs]
            nbytes = 0
            for (o, i) in pairs:
                n = 1
                for d in o.shape:
                    n *= d
                nbytes += n * 4
            return P.dma(issuer, fn, len(pairs), primary, R, W, is_out=is_out, cost=2.0 + nbytes / 250e3)

        xT = sb("xT", [128, 16, 2048], BF16); r_xTt = [P.res("xTt") for _ in range(16)]; r_xTc = r_xTt[0]; r_xTo = r_xTt[8]
        wbuf = sb("wbuf", [128, 2, 16, 512], BF16); r_w = [P.res("w0"), P.res("w1")]
        z = sb("z", [128, 8, 2048], BF16); r_z = [P.res("z") for _ in range(8)]
        zs = sb("zs", [4, 2048], BF16); r_zs = P.res("zs")
        xsT = sb("xsT", [128, 16, 4], BF16); r_xsT = P.res("xsT")
        identf = sb("identf", [128, 128], F32); r_idf = P.res("idf")
        identb = sb("identb", [128, 128], BF16); r_idb = P.res("idb")
        onesf = sb("onesf", [128, 128], F32); r_ones = P.res("ones")
        stage = [sb(f"stage{i}", [128, 512], F32) for i in range(3)]
        r_stage = [P.res("stage") for _ in range(3)]
        sel = sb("sel", [4, 4, 128], F32); r_sel = P.res("sel")
        selc = sb("selc", [128, 16], F32); r_selc = P.res("selc")
        barscr = sb("barscr", [1, 8], F32)
        P.bar_fn = lambda e: e.memset(barscr[0:1, 0:1], 0.0)
        ARENA = 16944
        arena = sb("arena", [128, ARENA], F32)
        apos = [0]

        amax = [0]

        def aalloc(n_f32):
            a = apos[0]
            apos[0] += n_f32
            amax[0] = max(amax[0], apos[0])
            assert apos[0] <= ARENA, apos[0]
            return arena[:, a:a + n_f32]

        def areset():
            print("arena high-water", amax[0])
            amax[0] = 0
            apos[0] = 0
            P.barrier()

        def a_f32(shape):
            n = int(np.prod(shape[1:]))
            v = aalloc(n)[0:shape[0], :]
            if len(shape) == 3:
                v = v.rearrange("p (a b) -> p a b", a=shape[1])
            elif len(shape) == 4:
                v = v.rearrange("p (a b c) -> p a b c", a=shape[1], b=shape[2])
            return v

        def a_bf(shape):
            n = int(np.prod(shape[1:]))
            assert n % 2 == 0
            v = aalloc(n // 2).bitcast(BF16)[0:shape[0], :]
            if len(shape) == 3:
                v = v.rearrange("p (a b) -> p a b", a=shape[1])
            elif len(shape) == 4:
                v = v.rearrange("p (a b c) -> p a b c", a=shape[1], b=shape[2])
            return v

        print('SBUF bytes remaining', nc.sbuf_bytes_remaining)
        psA = [psb(f"psA{i}", [128, 512], F32) for i in range(6)]
        r_psA = [P.res("psA") for _ in range(6)]
        for _r in r_psA:
            _r.excl = True
        psB = [psb(f"psB{i}", [128, 1024], BF16) for i in range(2)]
        r_psB = [P.res("psB") for _ in range(2)]
        for _r in r_psB:
            _r.excl = True
        stage_i = [0]
        proj_i = [0]

        dma("sp", [(identf[:], c_ident)], r_idf, [], [r_idf])
        cp("dve", identb[:], identf[:], [r_idf], [r_idb])
        memset("pool", onesf[:], 1.0, [r_ones])
        dma("sp", [(sel[:], c_sel.rearrange("b t m -> t b m"))], r_sel, [], [r_sel])
        dma("sp", [(selc[:], c_selc)], r_selc, [], [r_selc])

        P.ctx = "phase0"
        dma("pool", [(wbuf[:, 0, :, j * 128:(j + 1) * 128],
                      dap(w_in, c0, [[7168, 128], [128 * 7168, 16], [1, 128]])) for j, c0 in enumerate((768, 1536, 0, 2304))],
            r_w[0], [], [r_w[0]])
        xb = [z[:, i, :] for i in range(4)]
        r_xb = [r_z[i] for i in range(4)]
        for t in range(16):
            s = t % 4
            dma("pool", [(xb[s], x_all[t * 128:(t + 1) * 128, :])], r_xb[s], [], [r_xb[s]])
            for hb in range(2):
                pb = psB[hb]
                trs([(pb[:, j * 128:(j + 1) * 128], xb[s][:, (hb * 8 + j) * 128:(hb * 8 + j + 1) * 128]) for j in range(8)],
                    identb[:], [r_xb[s], r_idb], [r_psB[hb]])
                cp("act" if hb == 0 else "dve", xT[:, hb * 8:(hb + 1) * 8, t * 128:(t + 1) * 128],
                   pb[:].rearrange("p (j c) -> p j c", j=8), [r_psB[hb]], [r_xTt[t]])
        xsb = z[0:4, 4, :]; r_xsb = r_z[4]
        dma("pool", [(xsb, x_s)], r_xsb, [], [r_xsb])
        trs([(psB[0][:, j * 4:(j + 1) * 4], xsb[:, j * 128:(j + 1) * 128]) for j in range(16)], identb[0:4, 0:4],
            [r_xsb, r_idb], [r_psB[0]])
        cp("act", xsT[:], psB[0][:, 0:64].rearrange("p (j c) -> p j c", j=16), [r_psB[0]], [r_xsT])

        def load_w(slot, src, segs):
            pairs = []
            c = 0
            for (c0, n) in segs:
                pairs.append((wbuf[:, slot, :, c:c + n],
                              dap(src, c0, [[src.ap[0][0], 128], [128 * src.ap[0][0], 16], [1, n]])))
                c += n
            dma("pool", pairs, r_w[slot], [], [r_w[slot]])

        def project(slot, lhs_of, M, N, R_extra, ev="act"):
            pi = proj_i[0] % 2
            proj_i[0] += 1
            ps, rps = psA[pi], r_psA[pi]
            lst = [(ps[0:M, 0:N], lhs_of(kc), wbuf[:, slot, kc, 0:N]) for kc in range(16)]
            mms(lst, [r_w[slot]] + R_extra, [rps])
            si = stage_i[0] % 3
            stage_i[0] += 1
            cp(ev, stage[si][0:M, 0:N], ps[0:M, 0:N], [rps], [r_stage[si]])
            return stage[si], r_stage[si]

        def silu_to(eng2, out_bf, g_ap, M, Rg, Wout, tmp, r_tmp):
            act(tmp, g_ap, AF.Exp, Rg, [r_tmp], scale=-1.0)
            act(tmp, tmp, AF.Ln, [r_tmp], [r_tmp], bias=1.0)
            act(tmp, tmp, AF.Exp, [r_tmp], [r_tmp], scale=-1.0)
            tt(eng2, out_bf, g_ap, tmp, ALU.mult, Rg + [r_tmp], Wout)

        apos[0] = 0
        biasm = a_f32([128, 3, 256]); r_biasm = P.res("biasm")
        dma("sp", [(biasm[:], c_bias.rearrange("k p c -> p k c"))], r_biasm, [], [r_biasm])

        def mk_hb():
            d = dict(qT=a_bf([128, 1024]), kT=a_bf([128, 2048]), qT3=a_bf([128, 1024]), v_bf=a_bf([128, 16, 128]))
            for k in list(d.keys()):
                d["r_" + k] = P.res(k)
            return d
        HB1 = mk_hb()
        NU = 4
        vg = [a_bf([128, 2, 128]) for _ in range(NU)]; r_vg = [P.res("vg") for _ in range(NU)]
        rec = [a_f32([128, 130]) for _ in range(NU)]; r_rec = [P.res("rec") for _ in range(NU)]; r_den = [P.res("den") for _ in range(NU)]
        sm = [a_f32([128, 256]) for _ in range(NU)]; r_sm = [P.res("sm") for _ in range(NU)]
        pbf = [a_bf([128, 256]) for _ in range(NU)]; r_pbf = [P.res("pbf") for _ in range(NU)]
        pT = [a_bf([128, 2, 128]) for _ in range(NU)]; r_pT = [P.res("pT") for _ in range(NU)]
        tail_mark = apos[0]
        rope = a_f32([128, 2, 16, 64]); r_rope = P.res("rope")
        dma("sp", [(rope[:, cs], c_rope[cs].rearrange("(t p) c -> p t c", p=128)) for cs in range(2)], r_rope, [], [r_rope])
        rope_s = a_f32([4, 2, 64]); r_rope_s = P.res("rope_s")
        dma("sp", [(rope_s[:, cs], c_rope_s[cs]) for cs in range(2)], r_rope_s, [], [r_rope_s])
        HB = [mk_hb(), HB1]
        k_out = a_f32([128, 8, 128]); r_kout = P.res("kout")
        v_out = a_f32([128, 8, 128]); r_vout = P.res("vout")
        smpT = a_f32([4, 4, 128]); r_smpT = P.res("smpT")
        kq_r = [a_f32([128, 128]) for _ in range(2)]; r_kqr = [P.res("kqr") for _ in range(2)]
        kq_f = [a_f32([128, 2, 128]) for _ in range(2)]; r_kqf = [P.res("kqf") for _ in range(2)]
        kq_b = [a_bf([128, 2, 128]) for _ in range(2)]; r_kqb = [P.res("kqb") for _ in range(2)]
        rt = [a_f32([128, 2, 64]) for _ in range(4)]; r_rtl = [P.res("rt") for _ in range(4)]
        gtmp = a_f32([128, 128]); r_gtmp = P.res("gtmp")
        r_pb0 = [r_psB[0], r_psB[0]]
        r_pb1 = [r_psB[1], r_psB[1]]

        def do_rope(src, nk, cosv, sinv, dsts, Rsrc, Wdst, M):
            for j in range(nk):
                x1 = src[j][:, 0:64]
                x2 = src[j][:, 64:128]
                tt("dve", rt[0][0:M, j], x1, cosv, ALU.mult, Rsrc, [r_rtl[0]])
                tt("dve", rt[1][0:M, j], x2, sinv, ALU.mult, Rsrc, [r_rtl[1]])
                tt("pool", rt[2][0:M, j], x2, cosv, ALU.mult, Rsrc, [r_rtl[2]])
                tt("pool", rt[3][0:M, j], x1, sinv, ALU.mult, Rsrc, [r_rtl[3]])
                tt("dve", dsts[j][:, 0:64], rt[0][0:M, j], rt[1][0:M, j], ALU.subtract, [r_rtl[0], r_rtl[1]], Wdst[j])
                tt("dve", dsts[j][:, 64:128], rt[2][0:M, j], rt[3][0:M, j], ALU.add, [r_rtl[2], r_rtl[3]], Wdst[j])

        def do_rope2(stg, cosv, sinv, dst2, Rsrc, Wdst):
            x1 = fap(stg[:], 0, [[256, 2], [1, 64]])
            x2 = fap(stg[:], 64, [[256, 2], [1, 64]])
            cb = cosv.unsqueeze(1).to_broadcast([128, 2, 64])
            sb_ = sinv.unsqueeze(1).to_broadcast([128, 2, 64])
            tt("dve", rt[0], x1, cb, ALU.mult, Rsrc, [r_rtl[0]])
            tt("dve", rt[1], x2, sb_, ALU.mult, Rsrc, [r_rtl[1]])
            tt("pool", rt[2], x2, cb, ALU.mult, Rsrc, [r_rtl[2]])
            tt("pool", rt[3], x1, sb_, ALU.mult, Rsrc, [r_rtl[3]])
            tt("dve", dst2[:, :, 0:64], rt[0], rt[1], ALU.subtract, [r_rtl[0], r_rtl[1]], Wdst)
            tt("pool", dst2[:, :, 64:128], rt[2], rt[3], ALU.add, [r_rtl[2], r_rtl[3]], Wdst)

        units = []
        for u in range(8):
            kb0 = 1024 + 128 * (u - 1)
            units.append(dict(br=0, q=(0, 128 * u, 1), k=(kb0, [[1, 256]]), bias=(1 if u == 0 else 0),
                              v=[[(kb0, 1, 128)], [(kb0 + 128, 1, 128)]], rows=[(128 * u, 1, 128)]))
        for n in range(2):
            for r in range(4):
                q0 = 512 * n + r
                k0 = 1024 + 512 * (n - 1) + r
                units.append(dict(br=1, q=(0, q0, 4), k=(k0, [[512, 2], [4, 128]]), bias=(1 if n == 0 else 0),
                                  v=[[(k0, 4, 128)], [(k0 + 512, 4, 128)]], rows=[(q0, 4, 128)]))
        for u in range(8):
            q0 = 2 * u
            units.append(dict(br=2, q=(1, 128 * u, 1), k=(q0, [[1024, 2], [1, 2], [16, 64]]), bias=2,
                              v=[[(q0, 16, 64), (q0 + 1, 16, 64)], [(1024 + q0, 16, 64), (1024 + q0 + 1, 16, 64)]],
                              rows=[(q0, 16, 64), (q0 + 1, 16, 64)]))

        def A_proj_tile(h, t):
            P.ctx = "Aproj h%d t%d" % (h, t)
            slot = h % 2
            H = HB[h % 2]
            qT, kT, v_bf = H["qT"], H["kT"], H["v_bf"]
            r_qT, r_kT, r_vbf = H["r_qT"], H["r_kT"], H["r_v_bf"]
            if t == 2 and h + 1 < 6:
                h1 = h + 1
                load_w(h1 % 2, w_in, [(768 + h1 * 128, 128), (1536 + h1 * 128, 128), (h1 * 128, 128), (2304 + h1 * 128, 128)])
            if t < 16:
                own = t >= 8
                par = t % 2
                M, N = 128, (512 if own else 256)
                stg, rs = project(slot, lambda kc, t=t: xT[:, kc, t * 128:(t + 1) * 128], M, N, [r_xTt[t]])
                cosv, sinv = rope[:, 0, t], rope[:, 1, t]
                kr = kq_r[par]; rkr = r_kqr[par]
                kb_, rkb = kq_b[par], r_kqb[par]
                if own:
                    to = t - 8
                    kq2 = kq_f[par]; rkq2 = r_kqf[par]
                    do_rope2(stg, cosv, sinv, kq2, [rs, r_rope], [rkq2])
                    cp("act", kb_[:, 0:2], kq2, [rkq2], [rkb])
                    cp("pool", k_out[:, to], kq2[:, 0], [rkq2], [r_kout])
                    cp("pool", v_out[:, to], stg[:, 128:256], [rs], [r_vout])
                    nk = 2
                else:
                    do_rope([stg[:, 0:128]], 1, cosv, sinv, [kr], [rs, r_rope], [[rkr]], 128)
                    cp("act", kb_[:, 0], kr, [rkr], [rkb])
                    nk = 1
                cp("act", v_bf[:, t], stg[:, 128:256], [rs], [r_vbf])
                pb = psB[0]
                c0 = par * 256
                trs([(pb[:, c0 + j * 128:c0 + (j + 1) * 128], kb_[:, j]) for j in range(nk)], identb[:], [rkb, r_idb], [r_pb0[par]])
                cp("act", kT[:, t * 128:(t + 1) * 128], pb[:, c0:c0 + 128], [r_pb0[par]], [r_kT])
                if own:
                    cp("act", qT[:, to * 128:(to + 1) * 128], pb[:, c0 + 128:c0 + 256], [r_pb0[par]], [r_qT])
                    silu_to("pool", z[:, to, h * 128:(h + 1) * 128], stg[:, 384:512], 128, [rs], [r_z[to]], gtmp, r_gtmp)
            else:
                stg, rs = project(slot, lambda kc: xsT[:, kc, :], 4, 512, [r_xsT])
                do_rope([stg[0:4, 0:128], stg[0:4, 256:384]], 2, rope_s[:, 0], rope_s[:, 1], [smpT[:, 0], smpT[:, 2]],
                        [rs, r_rope_s], [[r_smpT], [r_smpT]], 4)
                cp("pool", smpT[:, 1], stg[0:4, 128:256], [rs], [r_smpT])
                cp("pool", smpT[:, 3], stg[0:4, 384:512], [rs], [r_smpT])
                dma("sp", [(smp_scr[:, h], smpT[:])], r_smpT, [r_smpT], [])

        def A_post(h):
            P.ctx = "Apost h%d" % h
            H = HB[h % 2]
            dma("sp", [(pk[:, h * 128:(h + 1) * 128].rearrange("(t p) d -> p t d", p=128), k_out[:])], r_kout, [r_kout], [], is_out=True)
            dma("sp", [(pv[:, h * 128:(h + 1) * 128].rearrange("(t p) d -> p t d", p=128), v_out[:])], r_vout, [r_vout], [], is_out=True)
            H["r_vscr"] = P.res("vscr")
            dma("sp", [(dap(v_scr, h * 128, [[768, 128], [128 * 768, 16], [1, 128]]), H["v_bf"][:])], H["r_v_bf"], [H["r_v_bf"]], [H["r_vscr"]])
            cp("act", H["qT3"].rearrange("p (r i) -> p r i", r=16), fap(H["qT"], 0, [[1, 16], [16, 64]]), [H["r_qT"]], [H["r_qT3"]])

        def A_unit(h, ui):
            P.ctx = "Aunit h%d u%d" % (h, ui)
            U = units[ui]
            H = HB[h % 2]
            qT, kT, qT3 = H["qT"], H["kT"], H["qT3"]
            s4_ = ui % NU
            pairs = []
            for blk in range(2):
                p0 = 0
                for (row0, step, n) in U["v"][blk]:
                    pairs.append((vg[s4_][p0:p0 + n, blk, :], dap(v_scr, row0 * 768 + h * 128, [[step * 768, n], [1, 128]])))
                    p0 += n
            dma("sp", pairs, r_vg[s4_], [H["r_vscr"]], [r_vg[s4_]])
            psS, rS = psA[2 + s4_], r_psA[2 + s4_]
            qsrc = (qT, qT3)[U["q"][0]]
            q_ap = fap(qsrc, U["q"][1], [[U["q"][2], 128]])
            mms([(psS[:, 0:256], q_ap, fap(kT, U["k"][0], U["k"][1]))], [H["r_qT"], H["r_kT"], H["r_qT3"]], [rS])
            stt("dve", sm[s4_], psS[:, 0:256], -SCALE, biasm[:, U["bias"]], ALU.mult, ALU.subtract, [rS, r_biasm], [r_sm[s4_]])
            rc, rrc = rec[s4_], r_rec[s4_]
            red("dve", rc[:, 128:129], sm[s4_], ALU.min, [r_sm[s4_]], [rrc])
            memset("pool", rc[:, 129:130], 0.0, [r_den[s4_]])
            act(pbf[s4_], sm[s4_], AF.Exp, [r_sm[s4_], rrc], [r_pbf[s4_], r_den[s4_]], bias=rc[:, 128:129], scale=-1.0, accum=rc[:, 129:130])
            pb, rpb = psB[ui % 2], r_psB[ui % 2]
            trs([(pb[:, 512 + j * 128:512 + (j + 1) * 128], pbf[s4_][:, j * 128:(j + 1) * 128]) for j in range(2)], identb[:],
                [r_pbf[s4_], r_idb], [rpb])
            cp("act", pT[s4_], pb[:, 512:768].rearrange("p (a b) -> p a b", a=2), [rpb], [r_pT[s4_]])
            mms([(psS[:, 256:384], pT[s4_][:, 0], vg[s4_][:, 0]), (psS[:, 256:384], pT[s4_][:, 1], vg[s4_][:, 1])], [r_pT[s4_], r_vg[s4_]], [rS])
            cp("dve", rc[:, 0:128], psS[:, 256:384], [rS], [rrc])
            pairs = []
            p0 = 0
            for (row0, step, n) in U["rows"]:
                pairs.append((dap(rec_scr, row0 * 2340 + U["br"] * 780 + h * 130, [[step * 2340, n], [1, 130]]), rc[p0:p0 + n, :]))
                p0 += n
            dma("sp", pairs, rrc, [rrc, r_den[s4_]], [])

        for t in range(17):
            A_proj_tile(0, t)
        for h in range(5):
            A_post(h)
            nt = 17
            ti = 0
            for ui in range(24):
                A_unit(h, ui)
                while ti < nt and ti * 24 <= (ui + 1) * nt:
                    A_proj_tile(h + 1, ti)
                    ti += 1
            while ti < nt:
                A_proj_tile(h + 1, ti)
                ti += 1
        A_post(5)
        P.ctx = "Atail"
        P.barrier()
        apos[0] = tail_mark
        dma("sp", [(sk.rearrange("t (h d) -> t h d", h=6), smp_scr[:, :, 0, :])], P.res("skd"), [], [], is_out=True)
        dma("sp", [(sv.rearrange("t (h d) -> t h d", h=6), smp_scr[:, :, 1, :])], P.res("svd"), [], [], is_out=True)
        gsm = a_f32([4, 6, 128]); r_gsm = P.res("gsm")
        dma("sp", [(gsm, smp_scr[:, :, 3, :])], r_gsm, [], [r_gsm])
        osmp = a_f32([4, 768]); r_osmp = P.res("osmp")
        memset("dve", osmp, 0.0, [r_osmp])
        SA = []
        for _i in range(1):
            d = dict(Kg=[a_f32([128, 768]) for _ in range(2)], Vg=[a_f32([128, 768]) for _ in range(2)], qb=a_f32([128, 768]), kb=a_f32([128, 768]), prod=a_f32([128, 768]),
                     prod2=a_f32([128, 768]), s0=a_f32([128, 6]), sx=a_f32([128, 6]), dsum=a_f32([128, 6]), nsum=a_f32([128, 768]),
                     prodB=a_f32([128, 768]), sxB=a_f32([128, 6]))
            d["r_Kg"] = [P.res("Kg") for _ in range(2)]
            d["r_Vg"] = [P.res("Vg") for _ in range(2)]
            for k in ("qb", "kb", "prod", "prod2", "s0", "sx", "dsum", "nsum", "prodB", "sxB"):
                d["r_" + k] = P.res(k)
            SA.append(d)
        starts = [(1920, 1), (1536, 4), (0, 16)]
        def sampA_batch(b):
            D = SA[0]
            qb, kb, prod, prod2, s0, sx, dsum, nsum = (D[k] for k in ("qb", "kb", "prod", "prod2", "s0", "sx", "dsum", "nsum"))

            def bc(kind):
                return dap(smp_scr, b * 3072 + kind * 128, [[0, 128], [512, 6], [1, 128]])
            dma("sp", [(qb.rearrange("p (h d) -> p h d", h=6), bc(2))], D["r_qb"], [], [D["r_qb"]])
            dma("sp", [(kb.rearrange("p (h d) -> p h d", h=6), bc(0))], D["r_kb"], [], [D["r_kb"]])
            dma("sp", [(nsum.rearrange("p (h d) -> p h d", h=6), bc(1))], D["r_nsum"], [], [D["r_nsum"]])
            ts("pool", nsum, nsum, 3.0, None, ALU.mult, None, [D["r_nsum"]], [D["r_nsum"]])
            tt("dve", prod, kb, qb, ALU.mult, [D["r_qb"], D["r_kb"]], [D["r_prod"]])
            red("dve", s0, prod.rearrange("p (h d) -> p h d", h=6), ALU.add, [D["r_prod"]], [D["r_s0"]])
            memset("pool", dsum, 3.0, [D["r_dsum"]])
            for r in range(3):
                r0, stp = starts[r]
                Kg, Vg, r_Kg, r_Vg = D["Kg"][r % 2], D["Vg"][r % 2], D["r_Kg"][r % 2], D["r_Vg"][r % 2]
                sfx = "B" if r % 2 else ""
                prod, prod2, sx = D["prod" + sfx], D["prod2"], D["sx" + sfx]
                rprod, rprod2, rsx = D["r_prod" + sfx], D["r_prod2"], D["r_sx" + sfx]
                dma("sp", [(Kg, dap(ck, b * 2048 * 768 + r0 * 768, [[stp * 768, 128], [1, 768]]))], r_Kg, [], [r_Kg])
                dma("sp", [(Vg, dap(cv, b * 2048 * 768 + r0 * 768, [[stp * 768, 128], [1, 768]]))], r_Vg, [], [r_Vg])
                tt("pool", prod, Kg, qb, ALU.mult, [r_Kg, D["r_qb"]], [rprod])
                red("dve", sx, prod.rearrange("p (h d) -> p h d", h=6), ALU.add, [rprod], [rsx])
                tt("dve", sx, sx, s0, ALU.subtract, [rsx, D["r_s0"]], [rsx])
                act(sx, sx, AF.Exp, [rsx], [rsx], scale=SCALE)
                tt("pool", prod2.rearrange("p (h d) -> p h d", h=6), Vg.rearrange("p (h d) -> p h d", h=6),
                   sx.unsqueeze(2).to_broadcast([128, 6, 128]), ALU.mult, [r_Vg, rsx], [rprod2])
                P.add("pe", lambda e, sx=sx, r=r: e.matmul(psA[2][:, 0:6], lhsT=onesf[:], rhs=sx, start=(r == 0), stop=(r == 2)),
                      [r_ones, rsx], [r_psA[2]], cost=0.3)
                for hh in range(2):
                    P.add("pe", lambda e, hh=hh, prod2=prod2, r=r: e.matmul(psA[hh][:, 0:384], lhsT=onesf[:], rhs=prod2[:, hh * 384:(hh + 1) * 384],
                                                                      start=(r == 0), stop=(r == 2)), [r_ones, rprod2], [r_psA[hh]], cost=0.8)
            tt("dve", dsum, dsum, psA[2][:, 0:6], ALU.add, [r_psA[2], D["r_dsum"]], [D["r_dsum"]])
            for hh in range(2):
                tt("dve", nsum[:, hh * 384:(hh + 1) * 384], nsum[:, hh * 384:(hh + 1) * 384], psA[hh][:, 0:384], ALU.add,
                   [r_psA[hh], D["r_nsum"]], [D["r_nsum"]])
            recip(dsum, dsum, [D["r_dsum"]], [D["r_dsum"]])
            tt("dve", nsum.rearrange("p (h d) -> p h d", h=6), nsum.rearrange("p (h d) -> p h d", h=6),
               dsum.unsqueeze(2).to_broadcast([128, 6, 128]), ALU.mult, [D["r_dsum"], D["r_nsum"]], [D["r_nsum"]])
            stt("dve", osmp, nsum[0:4, :], sel[:, b, 0:1], osmp, ALU.mult, ALU.add, [D["r_nsum"], r_sel, r_osmp], [r_osmp])
        def sampA_tail():
            gs_t = SA[0]["prod"][0:4, :].rearrange("p (h d) -> p h d", h=6); r_gst = SA[0]["r_prod"]
            act(gs_t, gsm, AF.Exp, [r_gsm], [r_gst], scale=-1.0)
            ts("dve", gs_t, gs_t, 1.0, None, ALU.add, None, [r_gst], [r_gst])
            recip(gs_t, gs_t, [r_gst], [r_gst])
            tt("dve", gs_t, gs_t, gsm, ALU.mult, [r_gst, r_gsm], [r_gst])
            tt("dve", zs[:, 0:768].rearrange("p (h d) -> p h d", h=6), gs_t, osmp.rearrange("p (h d) -> p h d", h=6), ALU.mult,
               [r_gst, r_osmp], [r_zs])

        for ui in range(24):
            A_unit(5, ui)
            if ui % 6 == 1:
                P.ctx = "sampA b%d" % (ui // 6)
                sampA_batch(ui // 6)
        P.ctx = "sampA tail"
        sampA_tail()
        P.ctx = "Bpre"
        load_w(0, w_in, [(3840, 128), (4608, 128), (3072, 128), (5376, 128)])
        areset()
        hg = a_f32([128, 4, 128]); r_hg = P.res("hg")
        dma("sp", [(hg[:], c_hg.rearrange("k p c -> p k c"))], r_hg, [], [r_hg])
        lbr = a_f32([128, 2, 768]); r_lbr = P.res("lbr")
        dma("sp", [(lbr[:, j], dap(lb_raw, j * 768, [[0, 128], [1, 768]])) for j in range(2)], r_lbr, [], [r_lbr])
        lbv = a_f32([128, 768]); oml = a_f32([128, 768]); r_lb = P.res("lb")
        tt("dve", lbv, lbr[:, 1], lbr[:, 0], ALU.subtract, [r_lbr], [r_lb])
        act(lbv, lbv, AF.Exp, [r_lb], [r_lb])
        ts("dve", lbv, lbv, 1.0, None, ALU.add, None, [r_lb], [r_lb])
        recip(lbv, lbv, [r_lb], [r_lb])
        ts("dve", oml, lbv, -1.0, 1.0, ALU.mult, ALU.add, [r_lb], [r_lb])
        ngb = a_f32([128, 768]); r_ngb = P.res("ngb")
        dma("sp", [(ngb, dap(norm_g, 0, [[0, 128], [1, 768]]))], r_ngb, [], [r_ngb])
        stS = a_f32([128, 4, 6, 128]); r_stS = P.res("stS")
        dma("sp", [(stS[:, b], st_in[b].rearrange("h k v -> k h v")) for b in range(4)], r_stS, [], [r_stS])
        smpB = a_f32([4, 4, 128]); r_smpB = P.res("smpB")
        NB = 3
        Bset = []
        for _i in range(NB):
            d = dict(ft=a_f32([128, 128]), logf=a_f32([128, 128]), kk=a_f32([128, 128]), ex=a_f32([128, 4, 128]),
                     prods=a_bf([128, 4, 128]), i_bf=a_bf([128, 128]), trT=a_bf([128, 3, 128]), attm=a_bf([128, 128]),
                     dec=a_f32([128, 2]), osb=a_f32([128, 128]), sq=a_f32([128, 128]), ssum=a_f32([128, 1]),
                     gtmp2=a_f32([128, 128]), sg=a_bf([128, 128]))
            for k in list(d.keys()):
                d["r_" + k] = P.res(k)
            Bset.append(d)
        Sf = a_f32([128, 128]); r_Sf = P.res("Sf")
        Sb = a_bf([128, 128]); r_Sb = P.res("Sb")
        colsT = a_f32([128, 3, 4]); r_colsT = P.res("colsT")
        qmask = [a_f32([128, 4]) for _ in range(2)]; r_qm = [P.res("qm") for _ in range(2)]
        ibc = [a_f32([128, 128]) for _ in range(2)]; r_ibc = [P.res("ibc") for _ in range(2)]
        Snew = [a_f32([128, 128]) for _ in range(2)]; r_Snew = [P.res("Snew") for _ in range(2)]
        smp_o = a_f32([4, 128]); r_smpo = P.res("smpo")
        s4 = a_f32([4, 4, 128]); r_s4 = P.res("s4")

        def gate_f(dst_f, src, M, lb_ap, oml_ap, R, Wr):
            act(dst_f, src, AF.Exp, R, [Wr], scale=-1.0)
            act(dst_f, dst_f, AF.Ln, [Wr], [Wr], bias=1.0)
            act(dst_f, dst_f, AF.Exp, [Wr], [Wr], scale=-1.0)
            tt("pool", dst_f, dst_f, oml_ap, ALU.mult, [Wr, r_lb], [Wr])
            tt("dve", dst_f, dst_f, lb_ap, ALU.add, [Wr, r_lb], [Wr])

        def rms_gate(o_ap, g_ap, M, h, out_bf, R_o, R_g, W_out, B):
            sq, ssum, sg, gtmp2 = B["sq"], B["ssum"], B["sg"], B["gtmp2"]
            r_sq, r_ss, r_sg, r_g2 = B["r_sq"], B["r_ssum"], B["r_sg"], B["r_gtmp2"]
            tt("dve", sq[0:M], o_ap, o_ap, ALU.mult, R_o, [r_sq])
            red("dve", ssum[0:M], sq[0:M], ALU.add, [r_sq], [r_ss])
            act(ssum[0:M], ssum[0:M], AF.Ln, [r_ss], [r_ss], bias=1e-6, scale=1.0 / 128.0)
            act(ssum[0:M], ssum[0:M], AF.Exp, [r_ss], [r_ss], scale=-0.5)
            stt("dve", sq[0:M], o_ap, ssum[0:M, 0:1], ngb[0:M, h * 128:(h + 1) * 128], ALU.mult, ALU.mult, R_o + [r_ss, r_ngb], [r_sq])
            silu_to("pool", sg[0:M], g_ap, M, R_g, [r_sg], gtmp2[0:M], r_g2)
            tt("dve", out_bf, sq[0:M], sg[0:M], ALU.mult, [r_sq, r_sg], W_out)

        Bps = []
        for _i in range(2):
            X, Y = psA[2 + 2 * _i], psA[3 + 2 * _i]
            rX, rY = r_psA[2 + 2 * _i], r_psA[3 + 2 * _i]
            Bps.append(dict(X=X, Y=Y, r_cums=rX, r_decp=rX, r_att=rY, r_o=rY, r_dS=rY))

        def load_wB(h):
            load_w(h % 2, w_in, [(3840 + h * 128, 128), (4608 + h * 128, 128), (3072 + h * 128, 128), (5376 + h * 128, 128)])

        for h in range(6):
            slot = h % 2
            lb_h, oml_h = lbv[:, h * 128:(h + 1) * 128], oml[:, h * 128:(h + 1) * 128]
            memset("dve", Sf, 0.0, [r_Sf])
            memset("pool", Sb, 0.0, [r_Sb])
            for t in range(16):
                own = t >= 8
                P.ctx = "B h%d t%d" % (h, t)
                if t == 2 and h + 1 < 6:
                    load_wB(h + 1)
                B = Bset[t % NB]
                Q = Bps[t % 2]
                ft, logf, kk, ex, prods, i_bf, trT, attm, dec, osb = (B[k] for k in ("ft", "logf", "kk", "ex", "prods", "i_bf", "trT", "attm", "dec", "osb"))
                stg, rs = project(slot, lambda kc, t=t: xT[:, kc, t * 128:(t + 1) * 128], 128, (512 if own else 256), [r_xTt[t]], ev="dve")
                gate_f(ft, stg[:, 0:128], 128, lb_h, oml_h, [rs], B["r_ft"])
                act(logf, ft, AF.Ln, [B["r_ft"]], [B["r_logf"]])
                ts("pool", kk, ft, -1.0, 1.0, ALU.mult, ALU.add, [B["r_ft"]], [B["r_kk"]])
                cp("pool", i_bf, stg[:, 128:256], [rs], [B["r_i_bf"]])
                pc = Q["X"]
                if own:
                    P.add("pe", lambda e, pc=pc, logf=logf: [e.matmul(pc[:, j * 128:(j + 1) * 128], lhsT=hg[:, j], rhs=logf, start=True, stop=True)
                                                             for j in range(2)][-1], [r_hg, B["r_logf"]], [Q["r_cums"]], cost=0.55)
                    mmi([(pc[:, 384:385], logf, onesf[:, 0:1]), (pc[:, 385:386], logf, hg[:, 2, 63:64])], [B["r_logf"], r_ones, r_hg], [Q["r_decp"]])
                    act(ex[:, 0:2], pc[:, 0:256].rearrange("p (a b) -> p a b", a=2), AF.Exp, [Q["r_cums"]], [B["r_ex"]])
                    act(ex[:, 3], pc[:, 0:128], AF.Exp, [Q["r_cums"]], [B["r_ex"]], scale=-1.0)
                    act(dec, pc[:, 384:386], AF.Exp, [Q["r_decp"]], [B["r_dec"]])
                else:
                    mmi([(pc[:, 128:256], hg[:, 1], logf), (pc[:, 384:385], logf, onesf[:, 0:1])], [r_hg, B["r_logf"], r_ones], [Q["r_cums"]])
                    act(ex[:, 1], pc[:, 128:256], AF.Exp, [Q["r_cums"]], [B["r_ex"]])
                    act(dec[:, 0:1], pc[:, 384:385], AF.Exp, [Q["r_decp"]], [B["r_dec"]])
                tt("dve", prods[:, 3], kk, ex[:, 1], ALU.mult, [B["r_kk"], B["r_ex"]], [B["r_prods"]])
                Y = Q["Y"]
                if own:
                    q_ap = stg[:, 256:384]
                    tt("dve", prods[:, 0], q_ap, ex[:, 0], ALU.mult, [rs, B["r_ex"]], [B["r_prods"]])
                    tt("pool", prods[:, 1], kk, ex[:, 3], ALU.mult, [B["r_kk"], B["r_ex"]], [B["r_prods"]])
                    pb, rpb = psB[t % 2], r_psB[t % 2]
                    trs([(pb[:, j * 128:(j + 1) * 128], prods[:, j]) for j in range(2)], identb[:], [B["r_prods"], r_idb], [rpb])
                    cp("dve", trT[:, 0:2], pb[:, 0:256].rearrange("p (a b) -> p a b", a=2), [rpb], [B["r_trT"]])
                    mms([(Y[:, 0:128], trT[:, 1], trT[:, 0])], [B["r_trT"]], [Q["r_att"]])
                    tt("dve", attm, Y[:, 0:128], hg[:, 3], ALU.mult, [Q["r_att"], r_hg], [B["r_attm"]])
                    ts("pool", Sb, Sf, dec[:, 1:2], None, ALU.mult, None, [r_Sf, B["r_dec"]], [r_Sb])
                    mms([(Y[:, 128:256], attm, i_bf), (Y[:, 128:256], trT[:, 0], Sb)], [B["r_attm"], B["r_i_bf"], B["r_trT"], r_Sb], [Q["r_o"]])
                    cp("act", osb, Y[:, 128:256], [Q["r_o"]], [B["r_osb"]])
                mms([(Y[:, 256:384], prods[:, 3], i_bf)], [B["r_prods"], B["r_i_bf"]], [Q["r_dS"]])
                stt("dve", Sf, Sf, dec[:, 0:1], Y[:, 256:384], ALU.mult, ALU.add, [r_Sf, B["r_dec"], Q["r_dS"]], [r_Sf])
                if own:
                    to = t - 8
                    rms_gate(osb, stg[:, 384:512], 128, h, z[:, to, 768 + h * 128:768 + (h + 1) * 128], [B["r_osb"]], [rs], [r_z[to]], B)
            dma("sp", [(pstate[h], Sf)], r_Sf, [r_Sf], [], is_out=True)
            B = Bset[0]
            stg, rs = project(slot, lambda kc: xsT[:, kc, :], 4, 512, [r_xsT])
            cp("pool", smpB, stg[0:4, :].rearrange("p (a b) -> p a b", a=4), [rs], [r_smpB])
            gate_f(s4[:, 0], smpB[:, 0], 4, lbv[0:4, h * 128:(h + 1) * 128], oml[0:4, h * 128:(h + 1) * 128], [r_smpB], r_s4)
            ts("dve", s4[:, 1], s4[:, 0], -1.0, 1.0, ALU.mult, ALU.add, [r_s4], [r_s4])
            cp("dve", s4[:, 2], smpB[:, 2], [r_smpB], [r_s4])
            P.add("pe", lambda e: [e.transpose(out=psA[2][:, j * 4:(j + 1) * 4], in_=s4[:, j], identity=identf[0:4, 0:4]) for j in range(3)][-1],
                  [r_s4, r_idf], [Bps[0]["r_cums"]], cost=0.4)
            cp("act", colsT, psA[2][:, 0:12].rearrange("p (a b) -> p a b", a=3), [Bps[0]["r_cums"]], [r_colsT])
            for b in range(4):
                sn, rsn = Snew[b % 2], r_Snew[b % 2]
                Yb = Bps[b % 2]
                mms([(Yb["Y"][:, 0:128], sel[:, b, :], smpB[:, 1])], [r_sel, r_smpB], [Yb["r_att"]])
                ts("dve", ibc[b % 2], Yb["Y"][:, 0:128], colsT[:, 1, b:b + 1], None, ALU.mult, None, [Yb["r_att"], r_colsT], [r_ibc[b % 2]])
                stt("dve", sn, stS[:, b, h], colsT[:, 0, b:b + 1], ibc[b % 2], ALU.mult, ALU.add, [r_stS, r_colsT, r_ibc[b % 2]], [rsn])
                dma("sp", [(sstate[b, h], sn)], rsn, [rsn], [], is_out=True)
                tt("dve", qmask[b % 2], selc[:, b * 4:(b + 1) * 4], colsT[:, 2, b:b + 1].to_broadcast([128, 4]), ALU.mult, [r_selc, r_colsT], [r_qm[b % 2]])
                P.add("pe", lambda e, b=b, sn=sn: e.matmul(psA[4][0:4, 256:384], lhsT=qmask[b % 2], rhs=sn, start=(b == 0), stop=(b == 3)),
                      [r_qm[b % 2], rsn], [r_psA[4]], cost=0.3)
            cp("act", smp_o, psA[4][0:4, 256:384], [r_psA[4]], [r_smpo])
            rms_gate(smp_o, smpB[:, 3], 4, h, zs[:, 768 + h * 128:768 + (h + 1) * 128], [r_smpo], [r_smpB], [r_zs], B)

        P.ctx = "Mpre"
        load_w(0, w_mem, [(0, 512)])
        load_w(1, w_mem, [(512, 512)])
        areset()
        P.ctx = "M"
        smpM = a_f32([4, 2, 4, 128]); r_smpM = P.res("smpM")
        mark_M = apos[0]
        def P_res_tmp(G):
            if "rt" not in G:
                G["rt"] = P.res("mgt")
            return G["rt"]

        recsL = [a_f32([128, 3, 6, 130]) for _ in range(2)]; r_recsL = [P.res("recs") for _ in range(2)]
        MG = []
        for _i in range(2):
            d = dict(Mx=a_f32([128, 6]), wv=a_f32([128, 3, 6]), wd=a_f32([128, 3, 6]), Dn=a_f32([128, 6]),
                     oacc=a_f32([128, 6, 128]), otmp=a_f32([128, 6, 128]))
            d["r"] = P.res("mg")
            d["r2"] = P.res("mg2")
            MG.append(d)
        def merge_tile(to):
            recs, r_recs = recsL[to % 2], r_recsL[to % 2]
            G = MG[to % 2]
            Mx, wv, wd, Dn, oacc, otmp, r_mg, r_mg2 = G["Mx"], G["wv"], G["wd"], G["Dn"], G["oacc"], G["otmp"], G["r"], G["r2"]
            dma("sp", [(recs[:], rec_scr[to * 128:(to + 1) * 128])], r_recs, [], [r_recs])
            mvw = recs[:, :, :, 128]
            dvw = recs[:, :, :, 129]
            tt("dve", Mx, mvw[:, 0], mvw[:, 1], ALU.min, [r_recs], [r_mg])
            tt("dve", Mx, Mx, mvw[:, 2], ALU.min, [r_recs, r_mg], [r_mg])
            tt("dve", wv, mvw, Mx.unsqueeze(1).to_broadcast([128, 3, 6]), ALU.subtract, [r_recs, r_mg], [r_mg])
            act(wv, wv, AF.Exp, [r_mg], [r_mg], scale=-1.0)
            tt("dve", wd, wv, dvw, ALU.mult, [r_recs, r_mg], [r_mg])
            tt("dve", Dn, wd[:, 0], wd[:, 1], ALU.add, [r_mg], [r_mg])
            tt("dve", Dn, Dn, wd[:, 2], ALU.add, [r_mg], [r_mg])
            recip(Dn, Dn, [r_mg], [r_mg])
            tt("dve", wv, wv, Dn.unsqueeze(1).to_broadcast([128, 3, 6]), ALU.mult, [r_mg], [r_mg])
            tt("dve", oacc, recs[:, 0, :, 0:128], wv[:, 0].unsqueeze(2).to_broadcast([128, 6, 128]), ALU.mult, [r_recs, r_mg], [r_mg2])
            tt("pool", otmp, recs[:, 1, :, 0:128], wv[:, 1].unsqueeze(2).to_broadcast([128, 6, 128]), ALU.mult, [r_recs, r_mg], [P_res_tmp(G)])
            tt("dve", oacc, oacc, otmp, ALU.add, [r_mg2, G["rt"]], [r_mg2])
            tt("pool", otmp, recs[:, 2, :, 0:128], wv[:, 2].unsqueeze(2).to_broadcast([128, 6, 128]), ALU.mult, [r_recs, r_mg], [G["rt"]])
            tt("dve", oacc, oacc, otmp, ALU.add, [r_mg2, G["rt"]], [r_mg2])
            zv = z[:, to, 0:768].rearrange("p (h d) -> p h d", h=6)
            tt("dve", zv, zv, oacc, ALU.mult, [r_mg2, r_z[to]], [r_z[to]])

        memT = a_bf([128, 16, 256]); r_memT = P.res("memT")
        mb = [a_bf([128, 2048]) for _ in range(2)]; r_mb = [P.res("mb") for _ in range(2)]
        for t in range(2):
            dma("pool", [(mb[t], mem[t * 128:(t + 1) * 128, :])], r_mb[t], [], [r_mb[t]])
            for hb in range(2):
                trs([(psB[hb][:, j * 128:(j + 1) * 128], mb[t][:, (hb * 8 + j) * 128:(hb * 8 + j + 1) * 128]) for j in range(8)],
                    identb[:], [r_mb[t], r_idb], [r_psB[hb]])
                cp("act" if hb == 0 else "dve", memT[:, hb * 8:(hb + 1) * 8, t * 128:(t + 1) * 128],
                   psB[hb][:].rearrange("p (j c) -> p j c", j=8), [r_psB[hb]], [r_memT])
        mkv_b = a_bf([128, 2, 2, 512]); r_mkvb = P.res("mkvb")
        mkT = a_bf([128, 4, 256]); r_mkT = P.res("mkT")
        for kv in range(2):
            slot = kv
            for t in range(2):
                stg, rs = project(slot, lambda kc, t=t: memT[:, kc, t * 128:(t + 1) * 128], 128, 512, [r_memT])
                dma("sp", [((pmk if kv == 0 else pmv)[t * 128:(t + 1) * 128, :], stg[:, :])], rs, [rs], [], is_out=True)
                cp("pool", mkv_b[:, kv, t], stg[:, :], [rs], [r_mkvb])
                if kv == 0:
                    trs([(psB[0][:, j * 128:(j + 1) * 128], mkv_b[:, 0, t, j * 128:(j + 1) * 128]) for j in range(4)], identb[:],
                        [r_mkvb, r_idb], [r_psB[0]])
                    cp("act", mkT[:, :, t * 128:(t + 1) * 128], psB[0][:, 0:512].rearrange("p (a b) -> p a b", a=4), [r_psB[0]], [r_mkT])
        NM = 3
        Mset = []
        for _i in range(NM):
            d = dict(qmb=a_bf([128, 128]), qmT=a_bf([128, 128]), mxM=a_f32([128, 2]), pM=a_bf([128, 256]), pMT=a_bf([128, 2, 128]),
                     oM=a_f32([128, 128]), sgM=a_bf([128, 128]), gtmp3=a_f32([128, 128]))
            for k in list(d.keys()):
                d["r_" + k] = P.res(k)
            d["r_denM"] = P.res("denM")
            Mset.append(d)
        mi = 0
        for p in range(2):
            slot = p
            load_w(slot, w_in, [(6144 + p * 256, 256), (6656 + p * 256, 256)])
            if p == 1:
                dma("pool", [(xT[:, :, c * 512:(c + 1) * 512], dap(w_out, c * 512, [[2048, 128], [128 * 2048, 16], [1, 512]])) for c in (0, 1)],
                    r_xTc, [], r_xTt[0:8])
            for t in range(9):
                if t == 8:
                    stg, rs = project(slot, lambda kc: xsT[:, kc, :], 4, 512, [r_xsT])
                    cp("pool", smpM[:, p], stg[0:4, :].rearrange("p (a b) -> p a b", a=4), [rs], [r_smpM])
                    continue
                stg, rs = project(slot, lambda kc, t=t: xT[:, kc, (8 + t) * 128:(9 + t) * 128], 128, 512, [r_xTt[8 + t]])
                for j in range(2):
                    hm = 2 * p + j
                    D = Mset[mi % NM]
                    par = mi % 2
                    mi += 1
                    qmb, qmT, mxM, pM, pMT, oM, sgM, gtmp3 = (D[k] for k in ("qmb", "qmT", "mxM", "pM", "pMT", "oM", "sgM", "gtmp3"))
                    pS, rpS = psA[2 + par], r_psA[2 + par]
                    pO, rpO = psA[4 + par], r_psA[4 + par]
                    pB, rpB = psB[par], r_psB[par]
                    cp("pool", qmb, stg[:, j * 128:(j + 1) * 128], [rs], [D["r_qmb"]])
                    trs([(pB[:, 0:128], qmb)], identb[:], [D["r_qmb"], r_idb], [rpB])
                    cp("act", qmT, pB[:, 0:128], [rpB], [D["r_qmT"]])
                    mms([(pS[:, 0:256], qmT, mkT[:, hm])], [D["r_qmT"], r_mkT], [rpS])
                    red("dve", mxM[:, 0:1], pS[:, 0:256], ALU.max, [rpS], [D["r_mxM"]])
                    ts("dve", mxM[:, 0:1], mxM[:, 0:1], -SCALE, None, ALU.mult, None, [D["r_mxM"]], [D["r_mxM"]])
                    memset("pool", mxM[:, 1:2], 0.0, [D["r_denM"]])
                    act(pM, pS[:, 0:256], AF.Exp, [rpS, D["r_mxM"]], [D["r_pM"], D["r_denM"]], bias=mxM[:, 0:1], scale=SCALE, accum=mxM[:, 1:2])
                    trs([(pB[:, 256 + jj * 128:256 + (jj + 1) * 128], pM[:, jj * 128:(jj + 1) * 128]) for jj in range(2)], identb[:],
                        [D["r_pM"], r_idb], [rpB])
                    cp("dve", pMT, pB[:, 256:512].rearrange("p (a b) -> p a b", a=2), [rpB], [D["r_pMT"]])
                    mms([(pO[:, 0:128], pMT[:, 0], mkv_b[:, 1, 0, hm * 128:(hm + 1) * 128]),
                         (pO[:, 0:128], pMT[:, 1], mkv_b[:, 1, 1, hm * 128:(hm + 1) * 128])], [D["r_pMT"], r_mkvb], [rpO])
                    recip(mxM[:, 1:2], mxM[:, 1:2], [D["r_denM"]], [D["r_denM"]])
                    ts("dve", oM, pO[:, 0:128], mxM[:, 1:2], None, ALU.mult, None, [rpO, D["r_denM"]], [D["r_oM"]])
                    silu_to("pool", sgM, stg[:, 256 + j * 128:256 + (j + 1) * 128], 128, [rs], [D["r_sgM"]], gtmp3, D["r_gtmp3"])
                    tt("dve", z[:, t, 1536 + hm * 128:1536 + (hm + 1) * 128], oM, sgM, ALU.mult, [D["r_oM"], D["r_sgM"]], [r_z[t]])
                if p == 0:
                    merge_tile(t)
        def sampM_alloc():
            return (a_f32([128, 2, 512]), a_f32([128, 2, 512]), a_f32([128, 512]), a_f32([128, 2, 512]), a_f32([128, 2, 4]), a_f32([128, 8]),
                    a_f32([128, 4]), a_f32([128, 512]), a_f32([4, 512]), a_f32([4, 2, 2, 128]))
        r_Km = P.res("Km"); r_Vm = P.res("Vm"); r_qbm = P.res("qbm"); r_prm = P.res("prm"); r_sxm = P.res("sxm"); r_accm = P.res("accm"); r_osm = P.res("osm")
        SMB = {}
        def sampM_batch(b):
            Km, Vm, qbm, prm, sxm, srf, dsm, nsm, osm, gm_t = SMB["bufs"]
            dma("sp", [(Km[:], cmk[b].rearrange("(t p) c -> p t c", p=128))], r_Km, [], [r_Km])
            dma("sp", [(Vm[:], cmv[b].rearrange("(t p) c -> p t c", p=128))], r_Vm, [], [r_Vm])
            for p in range(2):
                mms([(psA[p][:, 0:256], sel[:, b, :], smpM[:, p, 0:2, :])], [r_sel, r_smpM], [r_psA[p]])
                cp("act", qbm[:, p * 256:(p + 1) * 256], psA[p][:, 0:256], [r_psA[p]], [r_qbm])
            tt("dve", prm, Km, qbm.unsqueeze(1).to_broadcast([128, 2, 512]), ALU.mult, [r_Km, r_qbm], [r_prm])
            red("dve", sxm, prm.rearrange("p t (h d) -> p t h d", h=4), ALU.add, [r_prm], [r_sxm])
            mms([(psA[2][:, 0:8], onesf[:], sxm.rearrange("p a b -> p (a b)"))], [r_ones, r_sxm], [r_psA[2]])
            ts("dve", srf, psA[2][:, 0:8], 1.0 / 128.0, None, ALU.mult, None, [r_psA[2]], [r_sxm])
            tt("dve", sxm, sxm, srf[:, 0:4].unsqueeze(1).to_broadcast([128, 2, 4]), ALU.subtract, [r_sxm], [r_sxm])
            act(sxm, sxm, AF.Exp, [r_sxm], [r_sxm], scale=SCALE)
            tt("dve", prm.rearrange("p t (h d) -> p t h d", h=4), Vm.rearrange("p t (h d) -> p t h d", h=4),
               sxm.unsqueeze(3).to_broadcast([128, 2, 4, 128]), ALU.mult, [r_Vm, r_sxm], [r_prm])
            mms([(psA[2][:, 0:4], onesf[:], sxm[:, 0]), (psA[2][:, 0:4], onesf[:], sxm[:, 1])], [r_ones, r_sxm], [r_psA[2]])
            cp("dve", dsm, psA[2][:, 0:4], [r_psA[2]], [r_accm])
            mms([(psA[3][:, 0:512], onesf[:], prm[:, 0]), (psA[3][:, 0:512], onesf[:], prm[:, 1])], [r_ones, r_prm], [r_psA[3]])
            recip(dsm, dsm, [r_accm], [r_accm])
            tt("dve", nsm.rearrange("p (h d) -> p h d", h=4), psA[3][:, 0:512].rearrange("p (h d) -> p h d", h=4),
               dsm.unsqueeze(2).to_broadcast([128, 4, 128]), ALU.mult, [r_psA[3], r_accm], [r_accm])
            stt("dve", osm, nsm[0:4, :], sel[:, b, 0:1], osm, ALU.mult, ALU.add, [r_accm, r_sel, r_osm], [r_osm])
        def sampM_tail():
            Km, Vm, qbm, prm, sxm, srf, dsm, nsm, osm, gm_t = SMB["bufs"]
            r_gmt = P.res("gmt")
            act(gm_t, smpM[:, :, 2:4, :], AF.Exp, [r_smpM], [r_gmt], scale=-1.0)
            ts("dve", gm_t, gm_t, 1.0, None, ALU.add, None, [r_gmt], [r_gmt])
            recip(gm_t, gm_t, [r_gmt], [r_gmt])
            tt("dve", gm_t, gm_t, smpM[:, :, 2:4, :], ALU.mult, [r_gmt, r_smpM], [r_gmt])
            tt("dve", zs[:, 1536:2048].rearrange("p (a b d) -> p a b d", a=2, b=2), gm_t,
               osm.rearrange("p (a b d) -> p a b d", a=2, b=2), ALU.mult, [r_gmt, r_osm], [r_zs])

        areset()
        apos[0] = mark_M
        SMB["bufs"] = sampM_alloc()
        memset("dve", SMB["bufs"][8], 0.0, [r_osm])
        wo = xT
        dma("pool", [(wo[:, :, c * 512:(c + 1) * 512], dap(w_out, c * 512, [[2048, 128], [128 * 2048, 16], [1, 512]])) for c in (2, 3)],
            r_xTo, [], r_xTt[8:16])
        zT = wbuf[:].rearrange("p s k c -> p (s k c)").rearrange("p (k t) -> p k t", k=16)
        r_zT = P.res("zT")
        zsT = a_bf([128, 16, 4]); r_zsT = P.res("zsT")
        for t in range(8):
            for hb in range(2):
                trs([(psB[hb][:, j * 128:(j + 1) * 128], z[:, t, (hb * 8 + j) * 128:(hb * 8 + j + 1) * 128]) for j in range(8)],
                    identb[:], [r_z[t], r_idb], [r_psB[hb]])
                cp("act" if hb == 0 else "dve", zT[:, hb * 8:(hb + 1) * 8, t * 128:(t + 1) * 128],
                   psB[hb][:].rearrange("p (j c) -> p j c", j=8), [r_psB[hb]], [r_zT] + r_w)
        gbc = a_f32([128, 2048]); bbc = a_f32([128, 2048]); r_gb = P.res("gb")
        dma("sp", [(gbc, dap(ln_g, 0, [[0, 128], [1, 2048]])), (bbc, dap(ln_b, 0, [[0, 128], [1, 2048]]))], r_gb, [], [r_gb])
        rr = [a_f32([128, 2048]) for _ in range(2)]; r_rr = [P.res("rr") for _ in range(2)]
        sqo = a_f32([128, 2048]); r_sqo = P.res("sqo")
        stat = a_f32([128, 4]); r_stat = P.res("stat")
        for t in range(9):
            M = 128 if t < 8 else 4
            s = t % 2
            rv, rrv = rr[s], r_rr[s]
            xr, rxr = rv, rrv
            P.ctx = "O t%d" % t
            if t < 8:
                dma("sp", [(xr, x_all[1024 + t * 128:1024 + (t + 1) * 128, :])], rxr, [], [rxr])
                if t % 2 == 0:
                    sampM_batch(t // 2)
                    P.ctx = "O t%d" % t
            else:
                sampM_tail()
                trs([(psB[0][:, j * 4:(j + 1) * 4], zs[:, j * 128:(j + 1) * 128]) for j in range(16)], identb[0:4, 0:4], [r_zs, r_idb], [r_psB[0]])
                cp("act", zsT[:], psB[0][:, 0:64].rearrange("p (j c) -> p j c", j=16), [r_psB[0]], [r_zsT])
                dma("sp", [(xr[0:4], x_s)], rxr, [], [rxr])
            for c in range(4):
                ps, rps = psA[c], r_psA[c]
                if t < 8:
                    lst = [(ps[0:M, :], zT[:, kc, t * 128:(t + 1) * 128], wo[:, kc, c * 512:(c + 1) * 512]) for kc in range(16)]
                    mms(lst, [r_zT] + (r_xTt[0:8] if c < 2 else r_xTt[8:16]), [rps])
                else:
                    lst = [(ps[0:M, :], zsT[:, kc, :], wo[:, kc, c * 512:(c + 1) * 512]) for kc in range(16)]
                    mms(lst, [r_zsT] + (r_xTt[0:8] if c < 2 else r_xTt[8:16]), [rps])
                stt("dve", rv[0:M, c * 512:(c + 1) * 512], xr[0:M, c * 512:(c + 1) * 512], ALPHA, ps[0:M, :], ALU.mult, ALU.add,
                    [rxr, rps], [rrv])
            red("dve", stat[0:M, 0:1], rv[0:M], ALU.add, [rrv], [r_stat])
            ts("dve", stat[0:M, 0:1], stat[0:M, 0:1], 1.0 / 2048.0, None, ALU.mult, None, [r_stat], [r_stat])
            ts("dve", rv[0:M], rv[0:M], stat[0:M, 0:1], None, ALU.subtract, None, [rrv, r_stat], [rrv])
            tt("pool", sqo[0:M], rv[0:M], rv[0:M], ALU.mult, [rrv], [r_sqo])
            red("dve", stat[0:M, 1:2], sqo[0:M], ALU.add, [r_sqo], [r_stat])
            act(stat[0:M, 1:2], stat[0:M, 1:2], AF.Ln, [r_stat], [r_stat], bias=1e-5, scale=1.0 / 2048.0)
            act(stat[0:M, 1:2], stat[0:M, 1:2], AF.Exp, [r_stat], [r_stat], scale=-0.5)
            stt("dve", rv[0:M], rv[0:M], stat[0:M, 1:2], gbc[0:M], ALU.mult, ALU.mult, [rrv, r_stat, r_gb], [rrv])
            tt("pool", rv[0:M], rv[0:M], bbc[0:M], ALU.add, [rrv, r_gb], [rrv])
            if t < 8:
                dma("sp", [(y[t * 128:(t + 1) * 128, :], rv)], rrv, [rrv], [], is_out=True)
            else:
                dma("sp", [(ys, rv[0:4])], rrv, [rrv], [], is_out=True)
        cnt = P.emit()
        _bi = [o.idx for o in P.bar_ops] + [len(P.ops)]
        _prev = 0
        for _k, _b in enumerate(_bi):
            _tot = {e: 0.0 for e in P.ENGS}
            for o in P.ops[_prev:_b]:
                _tot[o.eng] += (0.06 if o.is_dma and o.eng != "pool" else (0.6 if o.is_dma else o.cost))
            print('phase', _k, 'ops', _b - _prev, {e: round(v) for e, v in _tot.items()})
            _prev = _b
        import os
        if os.environ.get("CRIT"):
            k = int(os.environ["CRIT"])
            o = P.bar_ops[k] if k < len(P.bar_ops) else max(P.ops, key=lambda x: x.fin)
            agg = {}
            chain = []
            while o is not None and (k == 0 or o.idx > P.bar_ops[k - 1].idx):
                key = (o.eng, "dma" if o.is_dma else "op")
                agg[key] = agg.get(key, 0.0) + (o.fin - o.start)
                chain.append(o)
                o = o.crit
            print("CRIT chain len", len(chain), {kk: round(v) for kk, v in agg.items()})
            for o in chain[-120:][::-1][:120]:
                print("   %8.1f %8.1f %s %s idx=%d" % (o.start, o.fin, o.eng, "dma" if o.is_dma else "op", o.idx))
        if os.environ.get("GAPS"):
            lo, hi = [float(x) for x in os.environ["GAPS"].split(",")]
            pe_ops = sorted([o for o in P.ops if o.eng == "pe"], key=lambda o: o.start)
            prev = None
            for o in pe_ops:
                if prev is not None and lo <= o.start <= hi and o.start - prev.fin > 1.0:
                    c = o.crit
                    print("  PE gap %.1f at %.1f before [%s] crit=(%s %s %s fin %.1f)" % (o.start - prev.fin, o.start, o.name, c.eng if c else None,
                          "dma" if (c is not None and c.is_dma) else "op", c.name if c else None, c.fin if c else 0))
                prev = o
        print('SCHED ops', len(P.ops), 'sim_end_us %.1f' % P.sim_end, cnt, 'barriers', ['%.0f' % o.fin for o in P.bar_ops])
    return nc


_NC = None


def _consts(half):
    ident = np.eye(128, dtype=np.float32)
    inv = 1.0 / (10000.0 ** (np.arange(64, dtype=np.float32) / 64.0))
    L = np.arange(2048)
    pos = (L if half == 1 else np.maximum(L - 1024, 0)).astype(np.float32)
    ang = pos[:, None] * inv[None, :].astype(np.float32)
    rope = np.stack([np.cos(ang), np.sin(ang)]).astype(np.float32)
    angs = np.full((4, 1), 8192.0, np.float32) * inv[None, :]
    rope_s = np.stack([np.cos(angs), np.sin(angs)]).astype(np.float32)
    qi = np.arange(128)[:, None]
    ki = np.arange(128)[None, :]
    band_prev = np.where(ki >= qi, 0.0, NEG)
    band_cur = np.where(ki <= qi, 0.0, NEG)
    ctx_ok = 0.0 if half == 1 else NEG
    b0 = np.concatenate([band_prev, band_cur], 1)
    b1 = np.concatenate([band_prev + ctx_ok, band_cur], 1)
    cq = qi // 64
    ck_ = ki // 64
    iq = qi % 64
    ik = ki % 64
    cls_prev = np.where(cq == ck_, 0.0, NEG) + ctx_ok
    cls_cur = np.where((cq == ck_) & (ik <= iq), 0.0, NEG)
    b2 = np.concatenate([cls_prev, cls_cur], 1)
    bias = np.maximum(np.stack([b0, b1, b2]), NEG).astype(np.float32)
    s = np.arange(128)[:, None]
    t = np.arange(128)[None, :]
    tri = (s <= t).astype(np.float32)
    mid = tri - (s <= 63).astype(np.float32)
    upper = (s > t).astype(np.float32)
    hg = np.stack([mid, upper, tri, tri]).astype(np.float32)
    sel = np.zeros((4, 4, 128), np.float32)
    selc = np.zeros((128, 16), np.float32)
    for b in range(4):
        sel[b, b, :] = 1.0
        selc[:, b * 4 + b] = 1.0
    return dict(c_ident=ident, c_rope=rope, c_rope_s=rope_s, c_bias=bias, c_hg=hg, c_sel=sel, c_selc=selc)


def kernel(x_prompt, x_sample, cache_win_k, cache_win_v, state_hgrn, cache_mem_k, cache_mem_v, mem_prompt,
           w_in, w_mem_kv, hgrn_lb_raw, hgrn_norm_g, w_out, ln_g, ln_b):
    global _NC
    if _NC is None:
        _NC = build_nc()
    f = lambda a: np.ascontiguousarray(np.asarray(a, dtype=np.float32))
    x_prompt, x_sample = f(x_prompt), f(x_sample)
    in_maps = []
    for c in range(8):
        b, half = c // 2, c % 2
        xa = np.zeros((2048, 2048), np.float32)
        xa[1024:] = x_prompt[b, half * 1024:(half + 1) * 1024]
        if half == 1:
            xa[:1024] = x_prompt[b, :1024]
        m = dict(
            x_all=xa, mem=f(mem_prompt[b]), x_s=f(x_sample[4 * c:4 * c + 4, 0]),
            ck=f(np.asarray(cache_win_k)[0, 4 * c:4 * c + 4]).reshape(4, 2048, 768),
            cv=f(np.asarray(cache_win_v)[0, 4 * c:4 * c + 4]).reshape(4, 2048, 768),
            st_in=f(np.asarray(state_hgrn)[0, 4 * c:4 * c + 4]),
            cmk=f(np.asarray(cache_mem_k)[0, 4 * c:4 * c + 4]).reshape(4, 256, 512),
            cmv=f(np.asarray(cache_mem_v)[0, 4 * c:4 * c + 4]).reshape(4, 256, 512),
            w_in=f(np.asarray(w_in)[0]), w_mem=f(np.asarray(w_mem_kv)[0]), w_out=f(np.asarray(w_out)[0]),
            lb_raw=f(hgrn_lb_raw), norm_g=f(hgrn_norm_g), ln_g=f(ln_g), ln_b=f(ln_b),
        )
        m.update(_consts(half))
        in_maps.append(m)
    res = run_bass_kernel_spmd(_NC, in_maps, core_ids=list(range(8)))
    R = res.results
    y_p = np.zeros((4, 2048, 2048), np.float32)
    y_s = np.zeros((32, 1, 2048), np.float32)
    pk = np.zeros((1, 4, 2048, 6, 128), np.float32)
    pv = np.zeros((1, 4, 2048, 6, 128), np.float32)
    pst = np.zeros((1, 4, 6, 128, 128), np.float32)
    pmk = np.zeros((1, 4, 256, 4, 128), np.float32)
    pmv = np.zeros((1, 4, 256, 4, 128), np.float32)
    sk = np.zeros((1, 32, 1, 6, 128), np.float32)
    sv = np.zeros((1, 32, 1, 6, 128), np.float32)
    sst = np.zeros((1, 32, 6, 128, 128), np.float32)
    for c in range(8):
        b, half = c // 2, c % 2
        r = R[c]
        sl = slice(half * 1024, (half + 1) * 1024)
        y_p[b, sl] = r["y"]
        y_s[4 * c:4 * c + 4, 0] = r["ys"]
        pk[0, b, sl] = np.asarray(r["pk"]).reshape(1024, 6, 128)
        pv[0, b, sl] = np.asarray(r["pv"]).reshape(1024, 6, 128)
        if half == 1:
            pst[0, b] = r["pstate"]
        else:
            pmk[0, b] = np.asarray(r["pmk"]).reshape(256, 4, 128)
            pmv[0, b] = np.asarray(r["pmv"]).reshape(256, 4, 128)
        sk[0, 4 * c:4 * c + 4, 0] = np.asarray(r["sk"]).reshape(4, 6, 128)
        sv[0, 4 * c:4 * c + 4, 0] = np.asarray(r["sv"]).reshape(4, 6, 128)
        sst[0, 4 * c:4 * c + 4] = r["sstate"]
    return (y_p, y_s, pk, pv, pst, pmk, pmv, sk, sv, sst)
```

```python
import numpy as np
from contextlib import ExitStack
import concourse.bass as bass
import concourse.mybir as mybir
from concourse.bass_utils import run_bass_kernel_spmd

F32 = mybir.dt.float32
BF16 = mybir.dt.bfloat16
AF = mybir.ActivationFunctionType
ALU = mybir.AluOpType
AX = mybir.AxisListType

NEG = -30000.0
ALPHA = 2.0 ** 0.25
SCALE = 128.0 ** -0.5


class Res:
    __slots__ = ("name", "writer", "readers", "dsem", "dcount", "excl")

    def __init__(self, name):
        self.name = name
        self.excl = False
        self.writer = None
        self.readers = []
        self.dsem = None
        self.dcount = 0


class Op:
    __slots__ = ("eng", "fn", "deps", "is_dma", "res", "dval", "needed", "mval", "tag", "cost", "idx", "sched", "fin", "crit", "start", "name", "is_bar", "single")

    def __init__(self, eng, fn, tag=""):
        self.eng = eng
        self.fn = fn
        self.cost = 0.3
        self.single = False
        self.is_bar = False
        self.idx = 0
        self.sched = False
        self.fin = 0.0
        self.deps = []
        self.is_dma = False
        self.res = None
        self.dval = 0
        self.needed = False
        self.mval = 0
        self.tag = tag


class Prog:
    ENGS = ["pe", "act", "dve", "pool", "sp"]

    def __init__(self, nc, stack):
        self.nc = nc
        self.stack = stack
        self.ops = []
        self.nres = 0
        self.out_ops = []
        self.bar = []
        self.last = {}
        self.dmas_since = []
        self.last_dma = {}
        self.bar_fn = None
        self.bar_ops = []
        self.pe_lat = 0.5
        self.ctx = ''

    def res(self, name=None):
        self.nres += 1
        return Res(f"{name or 'r'}{self.nres}")

    def _deps(self, op, reads, writes):
        ex = [r for r in reads if r.excl and r not in writes]
        if ex:
            writes = writes + ex
            reads = [r for r in reads if not r.excl]
        deps = list(self.bar)
        for r in reads:
            if r.writer is not None:
                deps.append(r.writer)
        for w in writes:
            if w.writer is not None:
                deps.append(w.writer)
            deps.extend(w.readers)
        seen = set()
        for d in deps:
            if id(d) not in seen and d is not op:
                seen.add(id(d))
                op.deps.append(d)
        for w in writes:
            w.writer = op
            w.readers = []
        for r in reads:
            if r not in writes:
                r.readers.append(op)

    def add(self, eng, fn, reads=(), writes=(), tag="", cost=0.3, single=False):
        op = Op(eng, fn, tag)
        op.cost = cost
        op.single = single
        op.name = self.ctx
        self._deps(op, list(reads), list(writes))
        op.idx = len(self.ops)
        self.ops.append(op)
        self.last[eng] = op
        return op

    def dma(self, issuer, fn, n, primary, reads=(), writes=(), tag="", is_out=False, cost=3.0):
        op = Op(issuer, fn, tag)
        op.is_dma = True
        op.res = primary
        op.cost = cost
        op.name = self.ctx + ":dma:" + primary.name
        self._deps(op, list(reads), list(writes))
        prev = self.last_dma.get(id(primary))
        if prev is not None and prev not in op.deps:
            op.deps.append(prev)
        self.last_dma[id(primary)] = op
        op.idx = len(self.ops)
        primary.dcount += 16 * n
        op.dval = primary.dcount
        op.mval = n
        self.ops.append(op)
        self.dmas_since.append(op)
        if is_out:
            self.out_ops.append(op)
        return op

    def barrier(self):
        deps = [o for o in self.last.values()] + self.dmas_since
        self.dmas_since = []
        self.bar = []
        op = self.add("pool", self.bar_fn, [], [], cost=0.2)
        op.is_bar = True
        for d in deps:
            if d is not op and d not in op.deps:
                op.deps.append(d)
        self.bar = [op]
        self.bar_ops.append(op)

    def schedule(self, W=48, LAT=0.5):
        ops = self.ops
        per = {e: [o for o in ops if o.eng == e] for e in self.ENGS}
        head = {e: 0 for e in self.ENGS}
        free = {e: 0.0 for e in self.ENGS}
        order = {e: [] for e in self.ENGS}
        left = len(ops)
        nsched = 0
        issue = {"sp": 0.06, "act": 0.06, "pool": 0.6}
        while left:
            best = None
            for e in self.ENGS:
                lst = per[e]
                h = head[e]
                while h < len(lst) and lst[h].sched:
                    h += 1
                head[e] = h
                cnt = 0
                i = h
                fe = free[e]
                while i < len(lst) and cnt < W:
                    op = lst[i]
                    i += 1
                    if op.sched:
                        continue
                    cnt += 1
                    if op.is_bar and nsched < op.idx:
                        continue
                    est = fe
                    ok = True
                    cr = None
                    for d in op.deps:
                        if not d.sched:
                            ok = False
                            break
                        t = d.fin + (LAT if (e != "pe" or d.eng == "pe") else self.pe_lat)
                        if t > est:
                            est = t
                            cr = d
                    if not ok:
                        continue
                    op.tag = cr
                    key = (est + 0.03 * (cnt - 1), op.idx)
                    if best is None or key < best[0]:
                        best = (key, e, op, est)
            assert best is not None, "scheduler stuck"
            _, e, op, est = best
            if op.is_bar:
                for e2 in self.ENGS:
                    if order[e2]:
                        d = order[e2][-1]
                        if d is not op and d not in op.deps:
                            op.deps.append(d)
                        if d.fin + LAT > est:
                            est = d.fin + LAT
            nsched += 1
            op.sched = True
            op.mval = op.mval if op.is_dma else 0
            op.crit = op.tag if op.tag is not None else (order[e][-1] if order[e] else None)
            op.start = est
            if op.is_dma:
                free[e] = est + issue[e]
                op.fin = est + issue[e] + op.cost
            else:
                free[e] = est + op.cost
                op.fin = free[e]
            order[e].append(op)
            left -= 1
        self.sim_end = max(o.fin for o in ops)
        return order

    def emit(self):
        nc, stack = self.nc, self.stack
        per = self.schedule()
        for op in self.ops:
            for d in op.deps:
                d.needed = True
        esem = {e: stack.enter_context(nc.semaphore(f"esem_{e}")) for e in self.ENGS}
        for op in self.ops:
            if op.is_dma and op.res.dsem is None:
                op.res.dsem = stack.enter_context(nc.semaphore(f"ds_{op.res.name}"))
        cnt = {e: 0 for e in self.ENGS}
        for e in self.ENGS:
            for op in per[e]:
                if not op.is_dma and op.needed:
                    cnt[e] += 1
                    op.dval = cnt[e]

        def done(op):
            if op.is_dma:
                return op.res.dsem, op.dval
            return esem[op.eng], op.dval

        out_waits = {}
        for op in self.out_ops:
            s, v = done(op)
            if id(s) not in out_waits or out_waits[id(s)][1] < v:
                out_waits[id(s)] = (s, v)
        block = stack.enter_context(nc.Block())

        def make(e):
            def body(engine):
                waited = {}
                for op in per[e]:
                    need = {}
                    for d in op.deps:
                        s, v = done(d)
                        if waited.get(id(s), 0) >= v:
                            continue
                        cur = need.get(id(s))
                        if cur is None or v > cur[1]:
                            need[id(s)] = (s, v, d.fin)
                    pend = []
                    for (s, v, f) in sorted(need.values(), key=lambda x: x[2]):
                        waited[id(s)] = v
                        pend.append((s, v))
                    fused = None
                    if (op.single or op.is_dma) and pend:
                        fused = pend.pop()
                    for (s, v) in pend:
                        engine.wait_ge(s, v)
                    if op.is_dma:
                        insts = op.fn(engine)
                        assert len(insts) == op.mval, (op.tag, len(insts), op.mval)
                        if fused is not None:
                            insts[0]._wait_ge(fused[0], fused[1])
                        for ins in insts:
                            ins.then_inc(op.res.dsem, 16)
                    else:
                        res = op.fn(engine)
                        first, ins = res if isinstance(res, tuple) else (res, res)
                        if fused is not None:
                            first._wait_ge(fused[0], fused[1])
                        if op.needed:
                            ins.then_inc(esem[e], 1)
                if e == "sp":
                    for s, v in out_waits.values():
                        engine.wait_ge(s, v)
            return body

        block.tensor(make("pe"))
        block.scalar(make("act"))
        block.vector(make("dve"))
        block.gpsimd(make("pool"))
        block.sync(make("sp"))
        return cnt


def build_nc():
    nc = bass.Bass("TRN2", target_bir_lowering=False)

    def din(name, shape, dt=F32):
        return nc.dram_tensor(name, list(shape), dt, kind="ExternalInput").ap()

    def dout(name, shape, dt=F32):
        return nc.dram_tensor(name, list(shape), dt, kind="ExternalOutput").ap()

    x_all = din("x_all", [2048, 2048])
    mem = din("mem", [256, 2048])
    x_s = din("x_s", [4, 2048])
    ck = din("ck", [4, 2048, 768])
    cv = din("cv", [4, 2048, 768])
    st_in = din("st_in", [4, 6, 128, 128])
    cmk = din("cmk", [4, 256, 512])
    cmv = din("cmv", [4, 256, 512])
    w_in = din("w_in", [2048, 7168])
    w_mem = din("w_mem", [2048, 1024])
    w_out = din("w_out", [2048, 2048])
    lb_raw = din("lb_raw", [2, 768])
    norm_g = din("norm_g", [1, 768])
    ln_g = din("ln_g", [1, 2048])
    ln_b = din("ln_b", [1, 2048])
    c_ident = din("c_ident", [128, 128])
    c_rope = din("c_rope", [2, 2048, 64])
    c_rope_s = din("c_rope_s", [2, 4, 64])
    c_bias = din("c_bias", [3, 128, 256])
    c_hg = din("c_hg", [4, 128, 128])
    c_sel = din("c_sel", [4, 4, 128])
    c_selc = din("c_selc", [128, 16])

    y = dout("y", [1024, 2048])
    ys = dout("ys", [4, 2048])
    pk = dout("pk", [1024, 768])
    pv = dout("pv", [1024, 768])
    pstate = dout("pstate", [6, 128, 128])
    pmk = dout("pmk", [256, 512])
    pmv = dout("pmv", [256, 512])
    sk = dout("sk", [4, 768])
    sv = dout("sv", [4, 768])
    sstate = dout("sstate", [4, 6, 128, 128])

    v_scr = nc.dram_tensor("v_scr", [2048, 768], BF16, kind="Internal").ap()
    rec_scr = nc.dram_tensor("rec_scr", [1024, 3, 6, 130], F32, kind="Internal").ap()
    smp_scr = nc.dram_tensor("smp_scr", [4, 6, 4, 128], F32, kind="Internal").ap()

    with ExitStack() as st:
        P = Prog(nc, st)

        def sb(name, shape, dt=F32):
            return st.enter_context(nc.sbuf_tensor(name, list(shape), dt))

        def psb(name, shape, dt=F32):
            return st.enter_context(nc.psum_tensor(name, list(shape), dt))

        def fap(base, off, dims):
            return bass.AP(tensor=base.tensor, offset=base.offset + off, ap=[list(base.ap[0])] + [list(d) for d in dims])

        def dap(base, off, dims):
            return bass.AP(tensor=base.tensor, offset=base.offset + off, ap=[list(d) for d in dims])

        def fsz(ap):
            n = 1
            for d in ap.shape[1:]:
                n *= d
            return n

        def ecost(eng, n, slow=1.0):
            if eng == "dve":
                return 0.12 + n * 1.05e-3 * slow
            if eng == "pool":
                return 0.25 + n * 1.8e-3 * slow
            return 0.2 + n * 0.85e-3

        def tt(eng, out, in0, in1, op, R, W):
            return P.add(eng, lambda e: e.tensor_tensor(out=out, in0=in0, in1=in1, op=op), R, W, cost=ecost(eng, fsz(out)), single=True)

        def ts(eng, out, in0, s1, s2, op0, op1, R, W):
            c = ecost(eng, fsz(out))
            if op1 is None:
                return P.add(eng, lambda e: e.tensor_scalar(out=out, in0=in0, scalar1=s1, scalar2=None, op0=op0), R, W, cost=c, single=True)
            return P.add(eng, lambda e: e.tensor_scalar(out=out, in0=in0, scalar1=s1, scalar2=s2, op0=op0, op1=op1), R, W, cost=c, single=True)

        def stt(eng, out, in0, scalar, in1, op0, op1, R, W):
            return P.add(eng, lambda e: e.scalar_tensor_tensor(out=out, in0=in0, scalar=scalar, in1=in1, op0=op0, op1=op1), R, W,
                         cost=ecost(eng, fsz(out)), single=True)

        def act(out, in_, func, R, W, bias=0.0, scale=1.0, accum=None):
            c = ecost("act", fsz(out))
            if accum is None:
                return P.add("act", lambda e: e.activation(out=out, in_=in_, func=func, bias=bias, scale=scale), R, W, cost=c, single=True)
            return P.add("act", lambda e: e.activation(out=out, in_=in_, func=func, bias=bias, scale=scale, accum_out=accum), R, W, cost=c, single=True)

        def cp(eng, out, in_, R, W):
            c = ecost(eng, fsz(out))
            if eng == "act":
                return P.add("act", lambda e: e.activation(out=out, in_=in_, func=AF.Copy), R, W, cost=c, single=True)
            return P.add(eng, lambda e: e.tensor_copy(out=out, in_=in_), R, W, cost=c, single=True)

        def memset(eng, out, val, W):
            return P.add(eng, lambda e: e.memset(out, val), [], W, cost=ecost(eng, fsz(out)) * 0.6, single=True)

        def red(eng, out, in_, op, R, W):
            return P.add(eng, lambda e: e.tensor_reduce(out=out, in_=in_, axis=AX.X, op=op), R, W, cost=ecost(eng, fsz(in_)), single=True)

        def recip(out, in_, R, W):
            return P.add("dve", lambda e: e.reciprocal(out=out, in_=in_), R, W, cost=ecost("dve", fsz(out), 6.0), single=True)

        def mmcost(o, l):
            n = fsz(o)
            c = max(n, 128) / 2000.0
            if l.dtype == F32:
                c *= 4.0
            return c + 0.02

        def mms(lst, R, W):
            def fn(e):
                ins = None
                first = None
                n = len(lst)
                for i, (o, l, r) in enumerate(lst):
                    ins = e.matmul(o, lhsT=l, rhs=r, start=(i == 0), stop=(i == n - 1))
                    if first is None:
                        first = ins
                return (first, ins)
            return P.add("pe", fn, R, W, cost=sum(mmcost(o, l) for (o, l, r) in lst), single=True)

        def mmi(lst, R, W):
            def fn(e):
                ins = None
                first = None
                for (o, l, r) in lst:
                    ins = e.matmul(o, lhsT=l, rhs=r, start=True, stop=True)
                    if first is None:
                        first = ins
                return (first, ins)
            return P.add("pe", fn, R, W, cost=sum(mmcost(o, l) for (o, l, r) in lst), single=True)

        def trs(lst, ident, R, W):
            def fn(e):
                ins = None
                first = None
                for (o, i) in lst:
                    ins = e.transpose(out=o, in_=i, identity=ident)
                    if first is None:
                        first = ins
                return (first, ins)
            return P.add("pe", fn, R, W, cost=0.09 * len(lst), single=True)

        def dma(issuer, pairs, primary, R, W, is_out=False):
            def fn(e):
                return [e.dma_start(out=o, in_=i) for (o, i) in pairs]
            nbytes = 0
            for (o, i) in pairs:
                n = 1
                for d in o.shape:
                    n *= d
                nbytes += n * 4
            return P.dma(issuer, fn, len(pairs), primary, R, W, is_out=is_out, cost=2.0 + nbytes / 250e3)

        xT = sb("xT", [128, 16, 2048], BF16); r_xTt = [P.res("xTt") for _ in range(16)]; r_xTc = r_xTt[0]; r_xTo = r_xTt[8]
        wbuf = sb("wbuf", [128, 2, 16, 512], BF16); r_w = [P.res("w0"), P.res("w1")]
        z = sb("z", [128, 8, 2048], BF16); r_z = [P.res("z") for _ in range(8)]
        zs = sb("zs", [4, 2048], BF16); r_zs = P.res("zs")
        xsT = sb("xsT", [128, 16, 4], BF16); r_xsT = P.res("xsT")
        identf = sb("identf", [128, 128], F32); r_idf = P.res("idf")
        identb = sb("identb", [128, 128], BF16); r_idb = P.res("idb")
        onesf = sb("onesf", [128, 128], F32); r_ones = P.res("ones")
        stage = [sb(f"stage{i}", [128, 512], F32) for i in range(3)]
        r_stage = [P.res("stage") for _ in range(3)]
        sel = sb("sel", [4, 4, 128], F32); r_sel = P.res("sel")
        selc = sb("selc", [128, 16], F32); r_selc = P.res("selc")
        barscr = sb("barscr", [1, 8], F32)
        P.bar_fn = lambda e: e.memset(barscr[0:1, 0:1], 0.0)
        ARENA = 16944
        arena = sb("arena", [128, ARENA], F32)
        apos = [0]

        amax = [0]

        def aalloc(n_f32):
            a = apos[0]
            apos[0] += n_f32
            amax[0] = max(amax[0], apos[0])
            assert apos[0] <= ARENA, apos[0]
            return arena[:, a:a + n_f32]

        def areset():
            print("arena high-water", amax[0])
            amax[0] = 0
            apos[0] = 0
            P.barrier()

        def a_f32(shape):
            n = int(np.prod(shape[1:]))
            v = aalloc(n)[0:shape[0], :]
            if len(shape) == 3:
                v = v.rearrange("p (a b) -> p a b", a=shape[1])
            elif len(shape) == 4:
                v = v.rearrange("p (a b c) -> p a b c", a=shape[1], b=shape[2])
            return v

        def a_bf(shape):
            n = int(np.prod(shape[1:]))
            assert n % 2 == 0
            v = aalloc(n // 2).bitcast(BF16)[0:shape[0], :]
            if len(shape) == 3:
                v = v.rearrange("p (a b) -> p a b", a=shape[1])
            elif len(shape) == 4:
                v = v.rearrange("p (a b c) -> p a b c", a=shape[1], b=shape[2])
            return v

        print('SBUF bytes remaining', nc.sbuf_bytes_remaining)
        psA = [psb(f"psA{i}", [128, 512], F32) for i in range(6)]
        r_psA = [P.res("psA") for _ in range(6)]
        for _r in r_psA:
            _r.excl = True
        psB = [psb(f"psB{i}", [128, 1024], BF16) for i in range(2)]
        r_psB = [P.res("psB") for _ in range(2)]
        for _r in r_psB:
            _r.excl = True
        stage_i = [0]
        proj_i = [0]

        dma("sp", [(identf[:], c_ident)], r_idf, [], [r_idf])
        cp("dve", identb[:], identf[:], [r_idf], [r_idb])
        memset("pool", onesf[:], 1.0, [r_ones])
        dma("sp", [(sel[:], c_sel.rearrange("b t m -> t b m"))], r_sel, [], [r_sel])
        dma("sp", [(selc[:], c_selc)], r_selc, [], [r_selc])

        P.ctx = "phase0"
        dma("pool", [(wbuf[:, 0, :, j * 128:(j + 1) * 128],
                      dap(w_in, c0, [[7168, 128], [128 * 7168, 16], [1, 128]])) for j, c0 in enumerate((768, 1536, 0, 2304))],
            r_w[0], [], [r_w[0]])
        xb = [z[:, i, :] for i in range(4)]
        r_xb = [r_z[i] for i in range(4)]
        for t in range(16):
            s = t % 4
            dma("pool", [(xb[s], x_all[t * 128:(t + 1) * 128, :])], r_xb[s], [], [r_xb[s]])
            for hb in range(2):
                pb = psB[hb]
                trs([(pb[:, j * 128:(j + 1) * 128], xb[s][:, (hb * 8 + j) * 128:(hb * 8 + j + 1) * 128]) for j in range(8)],
                    identb[:], [r_xb[s], r_idb], [r_psB[hb]])
                cp("act" if hb == 0 else "dve", xT[:, hb * 8:(hb + 1) * 8, t * 128:(t + 1) * 128],
                   pb[:].rearrange("p (j c) -> p j c", j=8), [r_psB[hb]], [r_xTt[t]])
        xsb = z[0:4, 4, :]; r_xsb = r_z[4]
        dma("pool", [(xsb, x_s)], r_xsb, [], [r_xsb])
        trs([(psB[0][:, j * 4:(j + 1) * 4], xsb[:, j * 128:(j + 1) * 128]) for j in range(16)], identb[0:4, 0:4],
            [r_xsb, r_idb], [r_psB[0]])
        cp("act", xsT[:], psB[0][:, 0:64].rearrange("p (j c) -> p j c", j=16), [r_psB[0]], [r_xsT])

        def load_w(slot, src, segs):
            pairs = []
            c = 0
            for (c0, n) in segs:
                pairs.append((wbuf[:, slot, :, c:c + n],
                              dap(src, c0, [[src.ap[0][0], 128], [128 * src.ap[0][0], 16], [1, n]])))
                c += n
            dma("pool", pairs, r_w[slot], [], [r_w[slot]])

        def project(slot, lhs_of, M, N, R_extra, ev="act"):
            pi = proj_i[0] % 2
            proj_i[0] += 1
            ps, rps = psA[pi], r_psA[pi]
            lst = [(ps[0:M, 0:N], lhs_of(kc), wbuf[:, slot, kc, 0:N]) for kc in range(16)]
            mms(lst, [r_w[slot]] + R_extra, [rps])
            si = stage_i[0] % 3
            stage_i[0] += 1
            cp(ev, stage[si][0:M, 0:N], ps[0:M, 0:N], [rps], [r_stage[si]])
            return stage[si], r_stage[si]

        def silu_to(eng2, out_bf, g_ap, M, Rg, Wout, tmp, r_tmp):
            act(tmp, g_ap, AF.Exp, Rg, [r_tmp], scale=-1.0)
            act(tmp, tmp, AF.Ln, [r_tmp], [r_tmp], bias=1.0)
            act(tmp, tmp, AF.Exp, [r_tmp], [r_tmp], scale=-1.0)
            tt(eng2, out_bf, g_ap, tmp, ALU.mult, Rg + [r_tmp], Wout)

        apos[0] = 0
        biasm = a_f32([128, 3, 256]); r_biasm = P.res("biasm")
        dma("sp", [(biasm[:], c_bias.rearrange("k p c -> p k c"))], r_biasm, [], [r_biasm])

        def mk_hb():
            d = dict(qT=a_bf([128, 1024]), kT=a_bf([128, 2048]), qT3=a_bf([128, 1024]), v_bf=a_bf([128, 16, 128]))
            for k in list(d.keys()):
                d["r_" + k] = P.res(k)
            return d
        HB1 = mk_hb()
        NU = 4
        vg = [a_bf([128, 2, 128]) for _ in range(NU)]; r_vg = [P.res("vg") for _ in range(NU)]
        rec = [a_f32([128, 130]) for _ in range(NU)]; r_rec = [P.res("rec") for _ in range(NU)]; r_den = [P.res("den") for _ in range(NU)]
        sm = [a_f32([128, 256]) for _ in range(NU)]; r_sm = [P.res("sm") for _ in range(NU)]
        pbf = [a_bf([128, 256]) for _ in range(NU)]; r_pbf = [P.res("pbf") for _ in range(NU)]
        pT = [a_bf([128, 2, 128]) for _ in range(NU)]; r_pT = [P.res("pT") for _ in range(NU)]
        tail_mark = apos[0]
        rope = a_f32([128, 2, 16, 64]); r_rope = P.res("rope")
        dma("sp", [(rope[:, cs], c_rope[cs].rearrange("(t p) c -> p t c", p=128)) for cs in range(2)], r_rope, [], [r_rope])
        rope_s = a_f32([4, 2, 64]); r_rope_s = P.res("rope_s")
        dma("sp", [(rope_s[:, cs], c_rope_s[cs]) for cs in range(2)], r_rope_s, [], [r_rope_s])
        HB = [mk_hb(), HB1]
        k_out = a_f32([128, 8, 128]); r_kout = P.res("kout")
        v_out = a_f32([128, 8, 128]); r_vout = P.res("vout")
        smpT = a_f32([4, 4, 128]); r_smpT = P.res("smpT")
        kq_r = [a_f32([128, 128]) for _ in range(2)]; r_kqr = [P.res("kqr") for _ in range(2)]
        kq_f = [a_f32([128, 2, 128]) for _ in range(2)]; r_kqf = [P.res("kqf") for _ in range(2)]
        kq_b = [a_bf([128, 2, 128]) for _ in range(2)]; r_kqb = [P.res("kqb") for _ in range(2)]
        rt = [a_f32([128, 2, 64]) for _ in range(4)]; r_rtl = [P.res("rt") for _ in range(4)]
        gtmp = a_f32([128, 128]); r_gtmp = P.res("gtmp")
        r_pb0 = [r_psB[0], r_psB[0]]
        r_pb1 = [r_psB[1], r_psB[1]]

        def do_rope(src, nk, cosv, sinv, dsts, Rsrc, Wdst, M):
            for j in range(nk):
                x1 = src[j][:, 0:64]
                x2 = src[j][:, 64:128]
                tt("dve", rt[0][0:M, j], x1, cosv, ALU.mult, Rsrc, [r_rtl[0]])
                tt("dve", rt[1][0:M, j], x2, sinv, ALU.mult, Rsrc, [r_rtl[1]])
                tt("pool", rt[2][0:M, j], x2, cosv, ALU.mult, Rsrc, [r_rtl[2]])
                tt("pool", rt[3][0:M, j], x1, sinv, ALU.mult, Rsrc, [r_rtl[3]])
                tt("dve", dsts[j][:, 0:64], rt[0][0:M, j], rt[1][0:M, j], ALU.subtract, [r_rtl[0], r_rtl[1]], Wdst[j])
                tt("dve", dsts[j][:, 64:128], rt[2][0:M, j], rt[3][0:M, j], ALU.add, [r_rtl[2], r_rtl[3]], Wdst[j])

        def do_rope2(stg, cosv, sinv, dst2, Rsrc, Wdst):
            x1 = fap(stg[:], 0, [[256, 2], [1, 64]])
            x2 = fap(stg[:], 64, [[256, 2], [1, 64]])
            cb = cosv.unsqueeze(1).to_broadcast([128, 2, 64])
            sb_ = sinv.unsqueeze(1).to_broadcast([128, 2, 64])
            tt("dve", rt[0], x1, cb, ALU.mult, Rsrc, [r_rtl[0]])
            tt("dve", rt[1], x2, sb_, ALU.mult, Rsrc, [r_rtl[1]])
            tt("pool", rt[2], x2, cb, ALU.mult, Rsrc, [r_rtl[2]])
            tt("pool", rt[3], x1, sb_, ALU.mult, Rsrc, [r_rtl[3]])
            tt("dve", dst2[:, :, 0:64], rt[0], rt[1], ALU.subtract, [r_rtl[0], r_rtl[1]], Wdst)
            tt("pool", dst2[:, :, 64:128], rt[2], rt[3], ALU.add, [r_rtl[2], r_rtl[3]], Wdst)

        units = []
        for u in range(8):
            kb0 = 1024 + 128 * (u - 1)
            units.append(dict(br=0, q=(0, 128 * u, 1), k=(kb0, [[1, 256]]), bias=(1 if u == 0 else 0),
                              v=[[(kb0, 1, 128)], [(kb0 + 128, 1, 128)]], rows=[(128 * u, 1, 128)]))
        for n in range(2):
            for r in range(4):
                q0 = 512 * n + r
                k0 = 1024 + 512 * (n - 1) + r
                units.append(dict(br=1, q=(0, q0, 4), k=(k0, [[512, 2], [4, 128]]), bias=(1 if n == 0 else 0),
                                  v=[[(k0, 4, 128)], [(k0 + 512, 4, 128)]], rows=[(q0, 4, 128)]))
        for u in range(8):
            q0 = 2 * u
            units.append(dict(br=2, q=(1, 128 * u, 1), k=(q0, [[1024, 2], [1, 2], [16, 64]]), bias=2,
                              v=[[(q0, 16, 64), (q0 + 1, 16, 64)], [(1024 + q0, 16, 64), (1024 + q0 + 1, 16, 64)]],
                              rows=[(q0, 16, 64), (q0 + 1, 16, 64)]))

        def A_proj_tile(h, t):
            P.ctx = "Aproj h%d t%d" % (h, t)
            slot = h % 2
            H = HB[h % 2]
            qT, kT, v_bf = H["qT"], H["kT"], H["v_bf"]
            r_qT, r_kT, r_vbf = H["r_qT"], H["r_kT"], H["r_v_bf"]
            if t == 2 and h + 1 < 6:
                h1 = h + 1
                load_w(h1 % 2, w_in, [(768 + h1 * 128, 128), (1536 + h1 * 128, 128), (h1 * 128, 128), (2304 + h1 * 128, 128)])
            if t < 16:
                own = t >= 8
                par = t % 2
                M, N = 128, (512 if own else 256)
                stg, rs = project(slot, lambda kc, t=t: xT[:, kc, t * 128:(t + 1) * 128], M, N, [r_xTt[t]])
                cosv, sinv = rope[:, 0, t], rope[:, 1, t]
                kr = kq_r[par]; rkr = r_kqr[par]
                kb_, rkb = kq_b[par], r_kqb[par]
                if own:
                    to = t - 8
                    kq2 = kq_f[par]; rkq2 = r_kqf[par]
                    do_rope2(stg, cosv, sinv, kq2, [rs, r_rope], [rkq2])
                    cp("act", kb_[:, 0:2], kq2, [rkq2], [rkb])
                    cp("pool", k_out[:, to], kq2[:, 0], [rkq2], [r_kout])
                    cp("pool", v_out[:, to], stg[:, 128:256], [rs], [r_vout])
                    nk = 2
                else:
                    do_rope([stg[:, 0:128]], 1, cosv, sinv, [kr], [rs, r_rope], [[rkr]], 128)
                    cp("act", kb_[:, 0], kr, [rkr], [rkb])
                    nk = 1
                cp("act", v_bf[:, t], stg[:, 128:256], [rs], [r_vbf])
                pb = psB[0]
                c0 = par * 256
                trs([(pb[:, c0 + j * 128:c0 + (j + 1) * 128], kb_[:, j]) for j in range(nk)], identb[:], [rkb, r_idb], [r_pb0[par]])
                cp("act", kT[:, t * 128:(t + 1) * 128], pb[:, c0:c0 + 128], [r_pb0[par]], [r_kT])
                if own:
                    cp("act", qT[:, to * 128:(to + 1) * 128], pb[:, c0 + 128:c0 + 256], [r_pb0[par]], [r_qT])
                    silu_to("pool", z[:, to, h * 128:(h + 1) * 128], stg[:, 384:512], 128, [rs], [r_z[to]], gtmp, r_gtmp)
            else:
                stg, rs = project(slot, lambda kc: xsT[:, kc, :], 4, 512, [r_xsT])
                do_rope([stg[0:4, 0:128], stg[0:4, 256:384]], 2, rope_s[:, 0], rope_s[:, 1], [smpT[:, 0], smpT[:, 2]],
                        [rs, r_rope_s], [[r_smpT], [r_smpT]], 4)
                cp("pool", smpT[:, 1], stg[0:4, 128:256], [rs], [r_smpT])
                cp("pool", smpT[:, 3], stg[0:4, 384:512], [rs], [r_smpT])
                dma("sp", [(smp_scr[:, h], smpT[:])], r_smpT, [r_smpT], [])

        def A_post(h):
            P.ctx = "Apost h%d" % h
            H = HB[h % 2]
            dma("sp", [(pk[:, h * 128:(h + 1) * 128].rearrange("(t p) d -> p t d", p=128), k_out[:])], r_kout, [r_kout], [], is_out=True)
            dma("sp", [(pv[:, h * 128:(h + 1) * 128].rearrange("(t p) d -> p t d", p=128), v_out[:])], r_vout, [r_vout], [], is_out=True)
            H["r_vscr"] = P.res("vscr")
            dma("sp", [(dap(v_scr, h * 128, [[768, 128], [128 * 768, 16], [1, 128]]), H["v_bf"][:])], H["r_v_bf"], [H["r_v_bf"]], [H["r_vscr"]])
            cp("act", H["qT3"].rearrange("p (r i) -> p r i", r=16), fap(H["qT"], 0, [[1, 16], [16, 64]]), [H["r_qT"]], [H["r_qT3"]])

        def A_unit(h, ui):
            P.ctx = "Aunit h%d u%d" % (h, ui)
            U = units[ui]
            H = HB[h % 2]
            qT, kT, qT3 = H["qT"], H["kT"], H["qT3"]
            s4_ = ui % NU
            pairs = []
            for blk in range(2):
                p0 = 0
                for (row0, step, n) in U["v"][blk]:
                    pairs.append((vg[s4_][p0:p0 + n, blk, :], dap(v_scr, row0 * 768 + h * 128, [[step * 768, n], [1, 128]])))
                    p0 += n
            dma("sp", pairs, r_vg[s4_], [H["r_vscr"]], [r_vg[s4_]])
            psS, rS = psA[2 + s4_], r_psA[2 + s4_]
            qsrc = (qT, qT3)[U["q"][0]]
            q_ap = fap(qsrc, U["q"][1], [[U["q"][2], 128]])
            mms([(psS[:, 0:256], q_ap, fap(kT, U["k"][0], U["k"][1]))], [H["r_qT"], H["r_kT"], H["r_qT3"]], [rS])
            stt("dve", sm[s4_], psS[:, 0:256], -SCALE, biasm[:, U["bias"]], ALU.mult, ALU.subtract, [rS, r_biasm], [r_sm[s4_]])
            rc, rrc = rec[s4_], r_rec[s4_]
            red("dve", rc[:, 128:129], sm[s4_], ALU.min, [r_sm[s4_]], [rrc])
            memset("pool", rc[:, 129:130], 0.0, [r_den[s4_]])
            act(pbf[s4_], sm[s4_], AF.Exp, [r_sm[s4_], rrc], [r_pbf[s4_], r_den[s4_]], bias=rc[:, 128:129], scale=-1.0, accum=rc[:, 129:130])
            pb, rpb = psB[ui % 2], r_psB[ui % 2]
            trs([(pb[:, 512 + j * 128:512 + (j + 1) * 128], pbf[s4_][:, j * 128:(j + 1) * 128]) for j in range(2)], identb[:],
                [r_pbf[s4_], r_idb], [rpb])
            cp("dve", pT[s4_], pb[:, 512:768].rearrange("p (a b) -> p a b", a=2), [rpb], [r_pT[s4_]])
            mms([(psS[:, 256:384], pT[s4_][:, 0], vg[s4_][:, 0]), (psS[:, 256:384], pT[s4_][:, 1], vg[s4_][:, 1])], [r_pT[s4_], r_vg[s4_]], [rS])
            cp("act", rc[:, 0:128], psS[:, 256:384], [rS], [rrc])
            pairs = []
            p0 = 0
            for (row0, step, n) in U["rows"]:
                pairs.append((dap(rec_scr, row0 * 2340 + U["br"] * 780 + h * 130, [[step * 2340, n], [1, 130]]), rc[p0:p0 + n, :]))
                p0 += n
            dma("sp", pairs, rrc, [rrc, r_den[s4_]], [])

        for t in range(17):
            A_proj_tile(0, t)
        for h in range(5):
            A_post(h)
            nt = 17
            ti = 0
            for ui in range(24):
                A_unit(h, ui)
                while ti < nt and ti * 24 <= (ui + 1) * nt:
                    A_proj_tile(h + 1, ti)
                    ti += 1
            while ti < nt:
                A_proj_tile(h + 1, ti)
                ti += 1
        A_post(5)
        P.ctx = "Atail"
        P.barrier()
        apos[0] = tail_mark
        dma("sp", [(sk.rearrange("t (h d) -> t h d", h=6), smp_scr[:, :, 0, :])], P.res("skd"), [], [], is_out=True)
        dma("sp", [(sv.rearrange("t (h d) -> t h d", h=6), smp_scr[:, :, 1, :])], P.res("svd"), [], [], is_out=True)
        gsm = a_f32([4, 6, 128]); r_gsm = P.res("gsm")
        dma("sp", [(gsm, smp_scr[:, :, 3, :])], r_gsm, [], [r_gsm])
        osmp = a_f32([4, 768]); r_osmp = P.res("osmp")
        memset("dve", osmp, 0.0, [r_osmp])
        SA = []
        for _i in range(1):
            d = dict(Kg=[a_f32([128, 768]) for _ in range(2)], Vg=[a_f32([128, 768]) for _ in range(2)], qb=a_f32([128, 768]), kb=a_f32([128, 768]), prod=a_f32([128, 768]),
                     prod2=a_f32([128, 768]), s0=a_f32([128, 6]), sx=a_f32([128, 6]), dsum=a_f32([128, 6]), nsum=a_f32([128, 768]),
                     prodB=a_f32([128, 768]), sxB=a_f32([128, 6]))
            d["r_Kg"] = [P.res("Kg") for _ in range(2)]
            d["r_Vg"] = [P.res("Vg") for _ in range(2)]
            for k in ("qb", "kb", "prod", "prod2", "s0", "sx", "dsum", "nsum", "prodB", "sxB"):
                d["r_" + k] = P.res(k)
            SA.append(d)
        starts = [(1920, 1), (1536, 4), (0, 16)]
        def sampA_batch(b):
            D = SA[0]
            qb, kb, prod, prod2, s0, sx, dsum, nsum = (D[k] for k in ("qb", "kb", "prod", "prod2", "s0", "sx", "dsum", "nsum"))

            def bc(kind):
                return dap(smp_scr, b * 3072 + kind * 128, [[0, 128], [512, 6], [1, 128]])
            dma("sp", [(qb.rearrange("p (h d) -> p h d", h=6), bc(2))], D["r_qb"], [], [D["r_qb"]])
            dma("sp", [(kb.rearrange("p (h d) -> p h d", h=6), bc(0))], D["r_kb"], [], [D["r_kb"]])
            dma("sp", [(nsum.rearrange("p (h d) -> p h d", h=6), bc(1))], D["r_nsum"], [], [D["r_nsum"]])
            ts("pool", nsum, nsum, 3.0, None, ALU.mult, None, [D["r_nsum"]], [D["r_nsum"]])
            tt("dve", prod, kb, qb, ALU.mult, [D["r_qb"], D["r_kb"]], [D["r_prod"]])
            red("dve", s0, prod.rearrange("p (h d) -> p h d", h=6), ALU.add, [D["r_prod"]], [D["r_s0"]])
            memset("pool", dsum, 3.0, [D["r_dsum"]])
            for r in range(3):
                r0, stp = starts[r]
                Kg, Vg, r_Kg, r_Vg = D["Kg"][r % 2], D["Vg"][r % 2], D["r_Kg"][r % 2], D["r_Vg"][r % 2]
                sfx = "B" if r % 2 else ""
                prod, prod2, sx = D["prod" + sfx], D["prod2"], D["sx" + sfx]
                rprod, rprod2, rsx = D["r_prod" + sfx], D["r_prod2"], D["r_sx" + sfx]
                dma("sp", [(Kg, dap(ck, b * 2048 * 768 + r0 * 768, [[stp * 768, 128], [1, 768]]))], r_Kg, [], [r_Kg])
                dma("sp", [(Vg, dap(cv, b * 2048 * 768 + r0 * 768, [[stp * 768, 128], [1, 768]]))], r_Vg, [], [r_Vg])
                tt("pool", prod, Kg, qb, ALU.mult, [r_Kg, D["r_qb"]], [rprod])
                red("dve", sx, prod.rearrange("p (h d) -> p h d", h=6), ALU.add, [rprod], [rsx])
                tt("dve", sx, sx, s0, ALU.subtract, [rsx, D["r_s0"]], [rsx])
                act(sx, sx, AF.Exp, [rsx], [rsx], scale=SCALE)
                tt("pool", prod2.rearrange("p (h d) -> p h d", h=6), Vg.rearrange("p (h d) -> p h d", h=6),
                   sx.unsqueeze(2).to_broadcast([128, 6, 128]), ALU.mult, [r_Vg, rsx], [rprod2])
                P.add("pe", lambda e, sx=sx, r=r: e.matmul(psA[2][:, 0:6], lhsT=onesf[:], rhs=sx, start=(r == 0), stop=(r == 2)),
                      [r_ones, rsx], [r_psA[2]], cost=0.3)
                for hh in range(2):
                    P.add("pe", lambda e, hh=hh, prod2=prod2, r=r: e.matmul(psA[hh][:, 0:384], lhsT=onesf[:], rhs=prod2[:, hh * 384:(hh + 1) * 384],
                                                                      start=(r == 0), stop=(r == 2)), [r_ones, rprod2], [r_psA[hh]], cost=0.8)
            tt("dve", dsum, dsum, psA[2][:, 0:6], ALU.add, [r_psA[2], D["r_dsum"]], [D["r_dsum"]])
            for hh in range(2):
                tt("dve", nsum[:, hh * 384:(hh + 1) * 384], nsum[:, hh * 384:(hh + 1) * 384], psA[hh][:, 0:384], ALU.add,
                   [r_psA[hh], D["r_nsum"]], [D["r_nsum"]])
            recip(dsum, dsum, [D["r_dsum"]], [D["r_dsum"]])
            tt("dve", nsum.rearrange("p (h d) -> p h d", h=6), nsum.rearrange("p (h d) -> p h d", h=6),
               dsum.unsqueeze(2).to_broadcast([128, 6, 128]), ALU.mult, [D["r_dsum"], D["r_nsum"]], [D["r_nsum"]])
            stt("dve", osmp, nsum[0:4, :], sel[:, b, 0:1], osmp, ALU.mult, ALU.add, [D["r_nsum"], r_sel, r_osmp], [r_osmp])
        def sampA_tail():
            gs_t = SA[0]["prod"][0:4, :].rearrange("p (h d) -> p h d", h=6); r_gst = SA[0]["r_prod"]
            act(gs_t, gsm, AF.Exp, [r_gsm], [r_gst], scale=-1.0)
            ts("dve", gs_t, gs_t, 1.0, None, ALU.add, None, [r_gst], [r_gst])
            recip(gs_t, gs_t, [r_gst], [r_gst])
            tt("dve", gs_t, gs_t, gsm, ALU.mult, [r_gst, r_gsm], [r_gst])
            tt("dve", zs[:, 0:768].rearrange("p (h d) -> p h d", h=6), gs_t, osmp.rearrange("p (h d) -> p h d", h=6), ALU.mult,
               [r_gst, r_osmp], [r_zs])

        for ui in range(24):
            A_unit(5, ui)
            if ui % 6 == 1:
                P.ctx = "sampA b%d" % (ui // 6)
                sampA_batch(ui // 6)
        P.ctx = "sampA tail"
        sampA_tail()
        P.ctx = "Bpre"
        load_w(0, w_in, [(3840, 128), (4608, 128), (3072, 128), (5376, 128)])
        areset()
        hg = a_f32([128, 4, 128]); r_hg = P.res("hg")
        dma("sp", [(hg[:], c_hg.rearrange("k p c -> p k c"))], r_hg, [], [r_hg])
        lbr = a_f32([128, 2, 768]); r_lbr = P.res("lbr")
        dma("sp", [(lbr[:, j], dap(lb_raw, j * 768, [[0, 128], [1, 768]])) for j in range(2)], r_lbr, [], [r_lbr])
        lbv = a_f32([128, 768]); oml = a_f32([128, 768]); r_lb = P.res("lb")
        tt("dve", lbv, lbr[:, 1], lbr[:, 0], ALU.subtract, [r_lbr], [r_lb])
        act(lbv, lbv, AF.Exp, [r_lb], [r_lb])
        ts("dve", lbv, lbv, 1.0, None, ALU.add, None, [r_lb], [r_lb])
        recip(lbv, lbv, [r_lb], [r_lb])
        ts("dve", oml, lbv, -1.0, 1.0, ALU.mult, ALU.add, [r_lb], [r_lb])
        ngb = a_f32([128, 768]); r_ngb = P.res("ngb")
        dma("sp", [(ngb, dap(norm_g, 0, [[0, 128], [1, 768]]))], r_ngb, [], [r_ngb])
        stS = a_f32([128, 4, 6, 128]); r_stS = P.res("stS")
        dma("sp", [(stS[:, b], st_in[b].rearrange("h k v -> k h v")) for b in range(4)], r_stS, [], [r_stS])
        smpB = a_f32([4, 4, 128]); r_smpB = P.res("smpB")
        NB = 3
        Bset = []
        for _i in range(NB):
            d = dict(ft=a_f32([128, 128]), logf=a_f32([128, 128]), kk=a_f32([128, 128]), ex=a_f32([128, 4, 128]),
                     prods=a_bf([128, 4, 128]), i_bf=a_bf([128, 128]), trT=a_bf([128, 3, 128]), attm=a_bf([128, 128]),
                     dec=a_f32([128, 2]), osb=a_f32([128, 128]), sq=a_f32([128, 128]), ssum=a_f32([128, 1]),
                     gtmp2=a_f32([128, 128]), sg=a_bf([128, 128]))
            for k in list(d.keys()):
                d["r_" + k] = P.res(k)
            Bset.append(d)
        Sf = a_f32([128, 128]); r_Sf = P.res("Sf")
        Sb = a_bf([128, 128]); r_Sb = P.res("Sb")
        colsT = a_f32([128, 3, 4]); r_colsT = P.res("colsT")
        qmask = [a_f32([128, 4]) for _ in range(2)]; r_qm = [P.res("qm") for _ in range(2)]
        ibc = [a_f32([128, 128]) for _ in range(2)]; r_ibc = [P.res("ibc") for _ in range(2)]
        Snew = [a_f32([128, 128]) for _ in range(2)]; r_Snew = [P.res("Snew") for _ in range(2)]
        smp_o = a_f32([4, 128]); r_smpo = P.res("smpo")
        s4 = a_f32([4, 4, 128]); r_s4 = P.res("s4")

        def gate_f(dst_f, src, M, lb_ap, oml_ap, R, Wr):
            act(dst_f, src, AF.Exp, R, [Wr], scale=-1.0)
            act(dst_f, dst_f, AF.Ln, [Wr], [Wr], bias=1.0)
            act(dst_f, dst_f, AF.Exp, [Wr], [Wr], scale=-1.0)
            tt("pool", dst_f, dst_f, oml_ap, ALU.mult, [Wr, r_lb], [Wr])
            tt("dve", dst_f, dst_f, lb_ap, ALU.add, [Wr, r_lb], [Wr])

        def rms_gate(o_ap, g_ap, M, h, out_bf, R_o, R_g, W_out, B):
            sq, ssum, sg, gtmp2 = B["sq"], B["ssum"], B["sg"], B["gtmp2"]
            r_sq, r_ss, r_sg, r_g2 = B["r_sq"], B["r_ssum"], B["r_sg"], B["r_gtmp2"]
            tt("dve", sq[0:M], o_ap, o_ap, ALU.mult, R_o, [r_sq])
            red("dve", ssum[0:M], sq[0:M], ALU.add, [r_sq], [r_ss])
            act(ssum[0:M], ssum[0:M], AF.Ln, [r_ss], [r_ss], bias=1e-6, scale=1.0 / 128.0)
            act(ssum[0:M], ssum[0:M], AF.Exp, [r_ss], [r_ss], scale=-0.5)
            stt("dve", sq[0:M], o_ap, ssum[0:M, 0:1], ngb[0:M, h * 128:(h + 1) * 128], ALU.mult, ALU.mult, R_o + [r_ss, r_ngb], [r_sq])
            silu_to("pool", sg[0:M], g_ap, M, R_g, [r_sg], gtmp2[0:M], r_g2)
            tt("dve", out_bf, sq[0:M], sg[0:M], ALU.mult, [r_sq, r_sg], W_out)

        Bps = []
        for _i in range(2):
            X, Y = psA[2 + 2 * _i], psA[3 + 2 * _i]
            rX, rY = r_psA[2 + 2 * _i], r_psA[3 + 2 * _i]
            Bps.append(dict(X=X, Y=Y, r_cums=rX, r_decp=rX, r_att=rY, r_o=rY, r_dS=rY))

        def load_wB(h):
            load_w(h % 2, w_in, [(3840 + h * 128, 128), (4608 + h * 128, 128), (3072 + h * 128, 128), (5376 + h * 128, 128)])

        for h in range(6):
            slot = h % 2
            lb_h, oml_h = lbv[:, h * 128:(h + 1) * 128], oml[:, h * 128:(h + 1) * 128]
            memset("dve", Sf, 0.0, [r_Sf])
            memset("pool", Sb, 0.0, [r_Sb])
            for t in range(16):
                own = t >= 8
                P.ctx = "B h%d t%d" % (h, t)
                if t == 2 and h + 1 < 6:
                    load_wB(h + 1)
                B = Bset[t % NB]
                Q = Bps[t % 2]
                ft, logf, kk, ex, prods, i_bf, trT, attm, dec, osb = (B[k] for k in ("ft", "logf", "kk", "ex", "prods", "i_bf", "trT", "attm", "dec", "osb"))
                stg, rs = project(slot, lambda kc, t=t: xT[:, kc, t * 128:(t + 1) * 128], 128, (512 if own else 256), [r_xTt[t]], ev="dve")
                gate_f(ft, stg[:, 0:128], 128, lb_h, oml_h, [rs], B["r_ft"])
                act(logf, ft, AF.Ln, [B["r_ft"]], [B["r_logf"]])
                ts("pool", kk, ft, -1.0, 1.0, ALU.mult, ALU.add, [B["r_ft"]], [B["r_kk"]])
                cp("pool", i_bf, stg[:, 128:256], [rs], [B["r_i_bf"]])
                pc = Q["X"]
                if own:
                    P.add("pe", lambda e, pc=pc, logf=logf: [e.matmul(pc[:, j * 128:(j + 1) * 128], lhsT=hg[:, j], rhs=logf, start=True, stop=True)
                                                             for j in range(2)][-1], [r_hg, B["r_logf"]], [Q["r_cums"]], cost=0.55)
                    mmi([(pc[:, 384:385], logf, onesf[:, 0:1]), (pc[:, 385:386], logf, hg[:, 2, 63:64])], [B["r_logf"], r_ones, r_hg], [Q["r_decp"]])
                    act(ex[:, 0:2], pc[:, 0:256].rearrange("p (a b) -> p a b", a=2), AF.Exp, [Q["r_cums"]], [B["r_ex"]])
                    act(ex[:, 3], pc[:, 0:128], AF.Exp, [Q["r_cums"]], [B["r_ex"]], scale=-1.0)
                    act(dec, pc[:, 384:386], AF.Exp, [Q["r_decp"]], [B["r_dec"]])
                else:
                    mmi([(pc[:, 128:256], hg[:, 1], logf), (pc[:, 384:385], logf, onesf[:, 0:1])], [r_hg, B["r_logf"], r_ones], [Q["r_cums"]])
                    act(ex[:, 1], pc[:, 128:256], AF.Exp, [Q["r_cums"]], [B["r_ex"]])
                    act(dec[:, 0:1], pc[:, 384:385], AF.Exp, [Q["r_decp"]], [B["r_dec"]])
                tt("dve", prods[:, 3], kk, ex[:, 1], ALU.mult, [B["r_kk"], B["r_ex"]], [B["r_prods"]])
                Y = Q["Y"]
                if own:
                    q_ap = stg[:, 256:384]
                    tt("dve", prods[:, 0], q_ap, ex[:, 0], ALU.mult, [rs, B["r_ex"]], [B["r_prods"]])
                    tt("pool", prods[:, 1], kk, ex[:, 3], ALU.mult, [B["r_kk"], B["r_ex"]], [B["r_prods"]])
                    pb, rpb = psB[t % 2], r_psB[t % 2]
                    trs([(pb[:, j * 128:(j + 1) * 128], prods[:, j]) for j in range(2)], identb[:], [B["r_prods"], r_idb], [rpb])
                    cp("dve", trT[:, 0:2], pb[:, 0:256].rearrange("p (a b) -> p a b", a=2), [rpb], [B["r_trT"]])
                    mms([(Y[:, 0:128], trT[:, 1], trT[:, 0])], [B["r_trT"]], [Q["r_att"]])
                    tt("dve", attm, Y[:, 0:128], hg[:, 3], ALU.mult, [Q["r_att"], r_hg], [B["r_attm"]])
                    ts("pool", Sb, Sf, dec[:, 1:2], None, ALU.mult, None, [r_Sf, B["r_dec"]], [r_Sb])
                    mms([(Y[:, 128:256], attm, i_bf), (Y[:, 128:256], trT[:, 0], Sb)], [B["r_attm"], B["r_i_bf"], B["r_trT"], r_Sb], [Q["r_o"]])
                    cp("act", osb, Y[:, 128:256], [Q["r_o"]], [B["r_osb"]])
                mms([(Y[:, 256:384], prods[:, 3], i_bf)], [B["r_prods"], B["r_i_bf"]], [Q["r_dS"]])
                stt("dve", Sf, Sf, dec[:, 0:1], Y[:, 256:384], ALU.mult, ALU.add, [r_Sf, B["r_dec"], Q["r_dS"]], [r_Sf])
                if own:
                    to = t - 8
                    rms_gate(osb, stg[:, 384:512], 128, h, z[:, to, 768 + h * 128:768 + (h + 1) * 128], [B["r_osb"]], [rs], [r_z[to]], B)
            dma("sp", [(pstate[h], Sf)], r_Sf, [r_Sf], [], is_out=True)
            B = Bset[0]
            stg, rs = project(slot, lambda kc: xsT[:, kc, :], 4, 512, [r_xsT])
            cp("pool", smpB, stg[0:4, :].rearrange("p (a b) -> p a b", a=4), [rs], [r_smpB])
            gate_f(s4[:, 0], smpB[:, 0], 4, lbv[0:4, h * 128:(h + 1) * 128], oml[0:4, h * 128:(h + 1) * 128], [r_smpB], r_s4)
            ts("dve", s4[:, 1], s4[:, 0], -1.0, 1.0, ALU.mult, ALU.add, [r_s4], [r_s4])
            cp("dve", s4[:, 2], smpB[:, 2], [r_smpB], [r_s4])
            P.add("pe", lambda e: [e.transpose(out=psA[2][:, j * 4:(j + 1) * 4], in_=s4[:, j], identity=identf[0:4, 0:4]) for j in range(3)][-1],
                  [r_s4, r_idf], [Bps[0]["r_cums"]], cost=0.4)
            cp("act", colsT, psA[2][:, 0:12].rearrange("p (a b) -> p a b", a=3), [Bps[0]["r_cums"]], [r_colsT])
            for b in range(4):
                sn, rsn = Snew[b % 2], r_Snew[b % 2]
                Yb = Bps[b % 2]
                mms([(Yb["Y"][:, 0:128], sel[:, b, :], smpB[:, 1])], [r_sel, r_smpB], [Yb["r_att"]])
                ts("dve", ibc[b % 2], Yb["Y"][:, 0:128], colsT[:, 1, b:b + 1], None, ALU.mult, None, [Yb["r_att"], r_colsT], [r_ibc[b % 2]])
                stt("dve", sn, stS[:, b, h], colsT[:, 0, b:b + 1], ibc[b % 2], ALU.mult, ALU.add, [r_stS, r_colsT, r_ibc[b % 2]], [rsn])
                dma("sp", [(sstate[b, h], sn)], rsn, [rsn], [], is_out=True)
                tt("dve", qmask[b % 2], selc[:, b * 4:(b + 1) * 4], colsT[:, 2, b:b + 1].to_broadcast([128, 4]), ALU.mult, [r_selc, r_colsT], [r_qm[b % 2]])
                P.add("pe", lambda e, b=b, sn=sn: e.matmul(psA[4][0:4, 256:384], lhsT=qmask[b % 2], rhs=sn, start=(b == 0), stop=(b == 3)),
                      [r_qm[b % 2], rsn], [r_psA[4]], cost=0.3)
            cp("act", smp_o, psA[4][0:4, 256:384], [r_psA[4]], [r_smpo])
            rms_gate(smp_o, smpB[:, 3], 4, h, zs[:, 768 + h * 128:768 + (h + 1) * 128], [r_smpo], [r_smpB], [r_zs], B)

        P.ctx = "Mpre"
        load_w(0, w_mem, [(0, 512)])
        load_w(1, w_mem, [(512, 512)])
        areset()
        P.ctx = "M"
        smpM = a_f32([4, 2, 4, 128]); r_smpM = P.res("smpM")
        mark_M = apos[0]
        def P_res_tmp(G):
            if "rt" not in G:
                G["rt"] = P.res("mgt")
            return G["rt"]

        recsL = [a_f32([128, 3, 6, 130]) for _ in range(2)]; r_recsL = [P.res("recs") for _ in range(2)]
        MG = []
        for _i in range(2):
            d = dict(Mx=a_f32([128, 6]), wv=a_f32([128, 3, 6]), wd=a_f32([128, 3, 6]), Dn=a_f32([128, 6]),
                     oacc=a_f32([128, 6, 128]), otmp=a_f32([128, 6, 128]))
            d["r"] = P.res("mg")
            d["r2"] = P.res("mg2")
            MG.append(d)
        def merge_tile(to):
            recs, r_recs = recsL[to % 2], r_recsL[to % 2]
            G = MG[to % 2]
            Mx, wv, wd, Dn, oacc, otmp, r_mg, r_mg2 = G["Mx"], G["wv"], G["wd"], G["Dn"], G["oacc"], G["otmp"], G["r"], G["r2"]
            dma("sp", [(recs[:], rec_scr[to * 128:(to + 1) * 128])], r_recs, [], [r_recs])
            mvw = recs[:, :, :, 128]
            dvw = recs[:, :, :, 129]
            tt("dve", Mx, mvw[:, 0], mvw[:, 1], ALU.min, [r_recs], [r_mg])
            tt("dve", Mx, Mx, mvw[:, 2], ALU.min, [r_recs, r_mg], [r_mg])
            tt("dve", wv, mvw, Mx.unsqueeze(1).to_broadcast([128, 3, 6]), ALU.subtract, [r_recs, r_mg], [r_mg])
            act(wv, wv, AF.Exp, [r_mg], [r_mg], scale=-1.0)
            tt("dve", wd, wv, dvw, ALU.mult, [r_recs, r_mg], [r_mg])
            tt("dve", Dn, wd[:, 0], wd[:, 1], ALU.add, [r_mg], [r_mg])
            tt("dve", Dn, Dn, wd[:, 2], ALU.add, [r_mg], [r_mg])
            recip(Dn, Dn, [r_mg], [r_mg])
            tt("dve", wv, wv, Dn.unsqueeze(1).to_broadcast([128, 3, 6]), ALU.mult, [r_mg], [r_mg])
            tt("dve", oacc, recs[:, 0, :, 0:128], wv[:, 0].unsqueeze(2).to_broadcast([128, 6, 128]), ALU.mult, [r_recs, r_mg], [r_mg2])
            tt("pool", otmp, recs[:, 1, :, 0:128], wv[:, 1].unsqueeze(2).to_broadcast([128, 6, 128]), ALU.mult, [r_recs, r_mg], [P_res_tmp(G)])
            tt("dve", oacc, oacc, otmp, ALU.add, [r_mg2, G["rt"]], [r_mg2])
            tt("pool", otmp, recs[:, 2, :, 0:128], wv[:, 2].unsqueeze(2).to_broadcast([128, 6, 128]), ALU.mult, [r_recs, r_mg], [G["rt"]])
            tt("dve", oacc, oacc, otmp, ALU.add, [r_mg2, G["rt"]], [r_mg2])
            zv = z[:, to, 0:768].rearrange("p (h d) -> p h d", h=6)
            tt("dve", zv, zv, oacc, ALU.mult, [r_mg2, r_z[to]], [r_z[to]])

        memT = a_bf([128, 16, 256]); r_memT = P.res("memT")
        mb = [a_bf([128, 2048]) for _ in range(2)]; r_mb = [P.res("mb") for _ in range(2)]
        for t in range(2):
            dma("pool", [(mb[t], mem[t * 128:(t + 1) * 128, :])], r_mb[t], [], [r_mb[t]])
            for hb in range(2):
                trs([(psB[hb][:, j * 128:(j + 1) * 128], mb[t][:, (hb * 8 + j) * 128:(hb * 8 + j + 1) * 128]) for j in range(8)],
                    identb[:], [r_mb[t], r_idb], [r_psB[hb]])
                cp("act" if hb == 0 else "dve", memT[:, hb * 8:(hb + 1) * 8, t * 128:(t + 1) * 128],
                   psB[hb][:].rearrange("p (j c) -> p j c", j=8), [r_psB[hb]], [r_memT])
        mkv_b = a_bf([128, 2, 2, 512]); r_mkvb = P.res("mkvb")
        mkT = a_bf([128, 4, 256]); r_mkT = P.res("mkT")
        for kv in range(2):
            slot = kv
            for t in range(2):
                stg, rs = project(slot, lambda kc, t=t: memT[:, kc, t * 128:(t + 1) * 128], 128, 512, [r_memT])
                dma("sp", [((pmk if kv == 0 else pmv)[t * 128:(t + 1) * 128, :], stg[:, :])], rs, [rs], [], is_out=True)
                cp("pool", mkv_b[:, kv, t], stg[:, :], [rs], [r_mkvb])
                if kv == 0:
                    trs([(psB[0][:, j * 128:(j + 1) * 128], mkv_b[:, 0, t, j * 128:(j + 1) * 128]) for j in range(4)], identb[:],
                        [r_mkvb, r_idb], [r_psB[0]])
                    cp("act", mkT[:, :, t * 128:(t + 1) * 128], psB[0][:, 0:512].rearrange("p (a b) -> p a b", a=4), [r_psB[0]], [r_mkT])
        NM = 3
        Mset = []
        for _i in range(NM):
            d = dict(qmb=a_bf([128, 128]), qmT=a_bf([128, 128]), mxM=a_f32([128, 2]), pM=a_bf([128, 256]), pMT=a_bf([128, 2, 128]),
                     oM=a_f32([128, 128]), sgM=a_bf([128, 128]), gtmp3=a_f32([128, 128]))
            for k in list(d.keys()):
                d["r_" + k] = P.res(k)
            d["r_denM"] = P.res("denM")
            Mset.append(d)
        mi = 0
        for p in range(2):
            slot = p
            load_w(slot, w_in, [(6144 + p * 256, 256), (6656 + p * 256, 256)])
            if p == 1:
                dma("pool", [(xT[:, :, c * 512:(c + 1) * 512], dap(w_out, c * 512, [[2048, 128], [128 * 2048, 16], [1, 512]])) for c in (0, 1)],
                    r_xTc, [], r_xTt[0:8])
            for t in range(9):
                if t == 8:
                    stg, rs = project(slot, lambda kc: xsT[:, kc, :], 4, 512, [r_xsT])
                    cp("pool", smpM[:, p], stg[0:4, :].rearrange("p (a b) -> p a b", a=4), [rs], [r_smpM])
                    continue
                stg, rs = project(slot, lambda kc, t=t: xT[:, kc, (8 + t) * 128:(9 + t) * 128], 128, 512, [r_xTt[8 + t]])
                for j in range(2):
                    hm = 2 * p + j
                    D = Mset[mi % NM]
                    par = mi % 2
                    mi += 1
                    qmb, qmT, mxM, pM, pMT, oM, sgM, gtmp3 = (D[k] for k in ("qmb", "qmT", "mxM", "pM", "pMT", "oM", "sgM", "gtmp3"))
                    pS, rpS = psA[2 + par], r_psA[2 + par]
                    pO, rpO = psA[4 + par], r_psA[4 + par]
                    pB, rpB = psB[par], r_psB[par]
                    cp("pool", qmb, stg[:, j * 128:(j + 1) * 128], [rs], [D["r_qmb"]])
                    trs([(pB[:, 0:128], qmb)], identb[:], [D["r_qmb"], r_idb], [rpB])
                    cp("act", qmT, pB[:, 0:128], [rpB], [D["r_qmT"]])
                    mms([(pS[:, 0:256], qmT, mkT[:, hm])], [D["r_qmT"], r_mkT], [rpS])
                    red("dve", mxM[:, 0:1], pS[:, 0:256], ALU.max, [rpS], [D["r_mxM"]])
                    ts("dve", mxM[:, 0:1], mxM[:, 0:1], -SCALE, None, ALU.mult, None, [D["r_mxM"]], [D["r_mxM"]])
                    memset("pool", mxM[:, 1:2], 0.0, [D["r_denM"]])
                    act(pM, pS[:, 0:256], AF.Exp, [rpS, D["r_mxM"]], [D["r_pM"], D["r_denM"]], bias=mxM[:, 0:1], scale=SCALE, accum=mxM[:, 1:2])
                    trs([(pB[:, 256 + jj * 128:256 + (jj + 1) * 128], pM[:, jj * 128:(jj + 1) * 128]) for jj in range(2)], identb[:],
                        [D["r_pM"], r_idb], [rpB])
                    cp("dve", pMT, pB[:, 256:512].rearrange("p (a b) -> p a b", a=2), [rpB], [D["r_pMT"]])
                    mms([(pO[:, 0:128], pMT[:, 0], mkv_b[:, 1, 0, hm * 128:(hm + 1) * 128]),
                         (pO[:, 0:128], pMT[:, 1], mkv_b[:, 1, 1, hm * 128:(hm + 1) * 128])], [D["r_pMT"], r_mkvb], [rpO])
                    recip(mxM[:, 1:2], mxM[:, 1:2], [D["r_denM"]], [D["r_denM"]])
                    ts("dve", oM, pO[:, 0:128], mxM[:, 1:2], None, ALU.mult, None, [rpO, D["r_denM"]], [D["r_oM"]])
                    silu_to("pool", sgM, stg[:, 256 + j * 128:256 + (j + 1) * 128], 128, [rs], [D["r_sgM"]], gtmp3, D["r_gtmp3"])
                    tt("dve", z[:, t, 1536 + hm * 128:1536 + (hm + 1) * 128], oM, sgM, ALU.mult, [D["r_oM"], D["r_sgM"]], [r_z[t]])
                if p == 0:
                    merge_tile(t)
        def sampM_alloc():
            return (a_f32([128, 2, 512]), a_f32([128, 2, 512]), a_f32([128, 512]), a_f32([128, 2, 512]), a_f32([128, 2, 4]), a_f32([128, 8]),
                    a_f32([128, 4]), a_f32([128, 512]), a_f32([4, 512]), a_f32([4, 2, 2, 128]))
        r_Km = P.res("Km"); r_Vm = P.res("Vm"); r_qbm = P.res("qbm"); r_prm = P.res("prm"); r_sxm = P.res("sxm"); r_accm = P.res("accm"); r_osm = P.res("osm")
        SMB = {}
        def sampM_batch(b):
            Km, Vm, qbm, prm, sxm, srf, dsm, nsm, osm, gm_t = SMB["bufs"]
            dma("sp", [(Km[:], cmk[b].rearrange("(t p) c -> p t c", p=128))], r_Km, [], [r_Km])
            dma("sp", [(Vm[:], cmv[b].rearrange("(t p) c -> p t c", p=128))], r_Vm, [], [r_Vm])
            for p in range(2):
                mms([(psA[p][:, 0:256], sel[:, b, :], smpM[:, p, 0:2, :])], [r_sel, r_smpM], [r_psA[p]])
                cp("act", qbm[:, p * 256:(p + 1) * 256], psA[p][:, 0:256], [r_psA[p]], [r_qbm])
            tt("dve", prm, Km, qbm.unsqueeze(1).to_broadcast([128, 2, 512]), ALU.mult, [r_Km, r_qbm], [r_prm])
            red("dve", sxm, prm.rearrange("p t (h d) -> p t h d", h=4), ALU.add, [r_prm], [r_sxm])
            mms([(psA[2][:, 0:8], onesf[:], sxm.rearrange("p a b -> p (a b)"))], [r_ones, r_sxm], [r_psA[2]])
            ts("dve", srf, psA[2][:, 0:8], 1.0 / 128.0, None, ALU.mult, None, [r_psA[2]], [r_sxm])
            tt("dve", sxm, sxm, srf[:, 0:4].unsqueeze(1).to_broadcast([128, 2, 4]), ALU.subtract, [r_sxm], [r_sxm])
            act(sxm, sxm, AF.Exp, [r_sxm], [r_sxm], scale=SCALE)
            tt("dve", prm.rearrange("p t (h d) -> p t h d", h=4), Vm.rearrange("p t (h d) -> p t h d", h=4),
               sxm.unsqueeze(3).to_broadcast([128, 2, 4, 128]), ALU.mult, [r_Vm, r_sxm], [r_prm])
            mms([(psA[2][:, 0:4], onesf[:], sxm[:, 0]), (psA[2][:, 0:4], onesf[:], sxm[:, 1])], [r_ones, r_sxm], [r_psA[2]])
            cp("dve", dsm, psA[2][:, 0:4], [r_psA[2]], [r_accm])
            mms([(psA[3][:, 0:512], onesf[:], prm[:, 0]), (psA[3][:, 0:512], onesf[:], prm[:, 1])], [r_ones, r_prm], [r_psA[3]])
            recip(dsm, dsm, [r_accm], [r_accm])
            tt("dve", nsm.rearrange("p (h d) -> p h d", h=4), psA[3][:, 0:512].rearrange("p (h d) -> p h d", h=4),
               dsm.unsqueeze(2).to_broadcast([128, 4, 128]), ALU.mult, [r_psA[3], r_accm], [r_accm])
            stt("dve", osm, nsm[0:4, :], sel[:, b, 0:1], osm, ALU.mult, ALU.add, [r_accm, r_sel, r_osm], [r_osm])
        def sampM_tail():
            Km, Vm, qbm, prm, sxm, srf, dsm, nsm, osm, gm_t = SMB["bufs"]
            r_gmt = P.res("gmt")
            act(gm_t, smpM[:, :, 2:4, :], AF.Exp, [r_smpM], [r_gmt], scale=-1.0)
            ts("dve", gm_t, gm_t, 1.0, None, ALU.add, None, [r_gmt], [r_gmt])
            recip(gm_t, gm_t, [r_gmt], [r_gmt])
            tt("dve", gm_t, gm_t, smpM[:, :, 2:4, :], ALU.mult, [r_gmt, r_smpM], [r_gmt])
            tt("dve", zs[:, 1536:2048].rearrange("p (a b d) -> p a b d", a=2, b=2), gm_t,
               osm.rearrange("p (a b d) -> p a b d", a=2, b=2), ALU.mult, [r_gmt, r_osm], [r_zs])

        areset()
        apos[0] = mark_M
        SMB["bufs"] = sampM_alloc()
        memset("dve", SMB["bufs"][8], 0.0, [r_osm])
        wo = xT
        dma("pool", [(wo[:, :, c * 512:(c + 1) * 512], dap(w_out, c * 512, [[2048, 128], [128 * 2048, 16], [1, 512]])) for c in (2, 3)],
            r_xTo, [], r_xTt[8:16])
        zT = wbuf[:].rearrange("p s k c -> p (s k c)").rearrange("p (k t) -> p k t", k=16)
        r_zT = P.res("zT")
        zsT = a_bf([128, 16, 4]); r_zsT = P.res("zsT")
        for t in range(8):
            for hb in range(2):
                trs([(psB[hb][:, j * 128:(j + 1) * 128], z[:, t, (hb * 8 + j) * 128:(hb * 8 + j + 1) * 128]) for j in range(8)],
                    identb[:], [r_z[t], r_idb], [r_psB[hb]])
                cp("act" if hb == 0 else "dve", zT[:, hb * 8:(hb + 1) * 8, t * 128:(t + 1) * 128],
                   psB[hb][:].rearrange("p (j c) -> p j c", j=8), [r_psB[hb]], [r_zT] + r_w)
        gbc = a_f32([128, 2048]); bbc = a_f32([128, 2048]); r_gb = P.res("gb")
        dma("sp", [(gbc, dap(ln_g, 0, [[0, 128], [1, 2048]])), (bbc, dap(ln_b, 0, [[0, 128], [1, 2048]]))], r_gb, [], [r_gb])
        rr = [a_f32([128, 2048]) for _ in range(2)]; r_rr = [P.res("rr") for _ in range(2)]
        sqo = a_f32([128, 2048]); r_sqo = P.res("sqo")
        stat = a_f32([128, 4]); r_stat = P.res("stat")
        for t in range(9):
            M = 128 if t < 8 else 4
            s = t % 2
            rv, rrv = rr[s], r_rr[s]
            xr, rxr = rv, rrv
            P.ctx = "O t%d" % t
            if t < 8:
                dma("sp", [(xr, x_all[1024 + t * 128:1024 + (t + 1) * 128, :])], rxr, [], [rxr])
                if t % 2 == 0:
                    sampM_batch(t // 2)
                    P.ctx = "O t%d" % t
            else:
                sampM_tail()
                trs([(psB[0][:, j * 4:(j + 1) * 4], zs[:, j * 128:(j + 1) * 128]) for j in range(16)], identb[0:4, 0:4], [r_zs, r_idb], [r_psB[0]])
                cp("act", zsT[:], psB[0][:, 0:64].rearrange("p (j c) -> p j c", j=16), [r_psB[0]], [r_zsT])
                dma("sp", [(xr[0:4], x_s)], rxr, [], [rxr])
            for c in range(4):
                ps, rps = psA[c], r_psA[c]
                if t < 8:
                    lst = [(ps[0:M, :], zT[:, kc, t * 128:(t + 1) * 128], wo[:, kc, c * 512:(c + 1) * 512]) for kc in range(16)]
                    mms(lst, [r_zT] + (r_xTt[0:8] if c < 2 else r_xTt[8:16]), [rps])
                else:
                    lst = [(ps[0:M, :], zsT[:, kc, :], wo[:, kc, c * 512:(c + 1) * 512]) for kc in range(16)]
                    mms(lst, [r_zsT] + (r_xTt[0:8] if c < 2 else r_xTt[8:16]), [rps])
                stt("dve", rv[0:M, c * 512:(c + 1) * 512], xr[0:M, c * 512:(c + 1) * 512], ALPHA, ps[0:M, :], ALU.mult, ALU.add,
                    [rxr, rps], [rrv])
            red("dve", stat[0:M, 0:1], rv[0:M], ALU.add, [rrv], [r_stat])
            ts("dve", stat[0:M, 0:1], stat[0:M, 0:1], 1.0 / 2048.0, None, ALU.mult, None, [r_stat], [r_stat])
            ts("dve", rv[0:M], rv[0:M], stat[0:M, 0:1], None, ALU.subtract, None, [rrv, r_stat], [rrv])
            tt("pool", sqo[0:M], rv[0:M], rv[0:M], ALU.mult, [rrv], [r_sqo])
            red("dve", stat[0:M, 1:2], sqo[0:M], ALU.add, [r_sqo], [r_stat])
            act(stat[0:M, 1:2], stat[0:M, 1:2], AF.Ln, [r_stat], [r_stat], bias=1e-5, scale=1.0 / 2048.0)
            act(stat[0:M, 1:2], stat[0:M, 1:2], AF.Exp, [r_stat], [r_stat], scale=-0.5)
            stt("dve", rv[0:M], rv[0:M], stat[0:M, 1:2], gbc[0:M], ALU.mult, ALU.mult, [rrv, r_stat, r_gb], [rrv])
            tt("pool", rv[0:M], rv[0:M], bbc[0:M], ALU.add, [rrv, r_gb], [rrv])
            if t < 8:
                dma("sp", [(y[t * 128:(t + 1) * 128, :], rv)], rrv, [rrv], [], is_out=True)
            else:
                dma("sp", [(ys, rv[0:4])], rrv, [rrv], [], is_out=True)
        cnt = P.emit()
        _bi = [o.idx for o in P.bar_ops] + [len(P.ops)]
        _prev = 0
        for _k, _b in enumerate(_bi):
            _tot = {e: 0.0 for e in P.ENGS}
            for o in P.ops[_prev:_b]:
                _tot[o.eng] += (0.06 if o.is_dma and o.eng != "pool" else (0.6 if o.is_dma else o.cost))
            print('phase', _k, 'ops', _b - _prev, {e: round(v) for e, v in _tot.items()})
            _prev = _b
        import os
        if os.environ.get("CRIT"):
            k = int(os.environ["CRIT"])
            o = P.bar_ops[k] if k < len(P.bar_ops) else max(P.ops, key=lambda x: x.fin)
            agg = {}
            chain = []
            while o is not None and (k == 0 or o.idx > P.bar_ops[k - 1].idx):
                key = (o.eng, "dma" if o.is_dma else "op")
                agg[key] = agg.get(key, 0.0) + (o.fin - o.start)
                chain.append(o)
                o = o.crit
            print("CRIT chain len", len(chain), {kk: round(v) for kk, v in agg.items()})
            for o in chain[-120:][::-1][:120]:
                print("   %8.1f %8.1f %s %s idx=%d" % (o.start, o.fin, o.eng, "dma" if o.is_dma else "op", o.idx))
        if os.environ.get("GAPS"):
            lo, hi = [float(x) for x in os.environ["GAPS"].split(",")]
            pe_ops = sorted([o for o in P.ops if o.eng == "pe"], key=lambda o: o.start)
            prev = None
            for o in pe_ops:
                if prev is not None and lo <= o.start <= hi and o.start - prev.fin > 1.0:
                    c = o.crit
                    print("  PE gap %.1f at %.1f before [%s] crit=(%s %s %s fin %.1f)" % (o.start - prev.fin, o.start, o.name, c.eng if c else None,
                          "dma" if (c is not None and c.is_dma) else "op", c.name if c else None, c.fin if c else 0))
                prev = o
        print('SCHED ops', len(P.ops), 'sim_end_us %.1f' % P.sim_end, cnt, 'barriers', ['%.0f' % o.fin for o in P.bar_ops])
    return nc


_NC = None


def _consts(half):
    ident = np.eye(128, dtype=np.float32)
    inv = 1.0 / (10000.0 ** (np.arange(64, dtype=np.float32) / 64.0))
    L = np.arange(2048)
    pos = (L if half == 1 else np.maximum(L - 1024, 0)).astype(np.float32)
    ang = pos[:, None] * inv[None, :].astype(np.float32)
    rope = np.stack([np.cos(ang), np.sin(ang)]).astype(np.float32)
    angs = np.full((4, 1), 8192.0, np.float32) * inv[None, :]
    rope_s = np.stack([np.cos(angs), np.sin(angs)]).astype(np.float32)
    qi = np.arange(128)[:, None]
    ki = np.arange(128)[None, :]
    band_prev = np.where(ki >= qi, 0.0, NEG)
    band_cur = np.where(ki <= qi, 0.0, NEG)
    ctx_ok = 0.0 if half == 1 else NEG
    b0 = np.concatenate([band_prev, band_cur], 1)
    b1 = np.concatenate([band_prev + ctx_ok, band_cur], 1)
    cq = qi // 64
    ck_ = ki // 64
    iq = qi % 64
    ik = ki % 64
    cls_prev = np.where(cq == ck_, 0.0, NEG) + ctx_ok
    cls_cur = np.where((cq == ck_) & (ik <= iq), 0.0, NEG)
    b2 = np.concatenate([cls_prev, cls_cur], 1)
    bias = np.maximum(np.stack([b0, b1, b2]), NEG).astype(np.float32)
    s = np.arange(128)[:, None]
    t = np.arange(128)[None, :]
    tri = (s <= t).astype(np.float32)
    mid = tri - (s <= 63).astype(np.float32)
    upper = (s > t).astype(np.float32)
    hg = np.stack([mid, upper, tri, tri]).astype(np.float32)
    sel = np.zeros((4, 4, 128), np.float32)
    selc = np.zeros((128, 16), np.float32)
    for b in range(4):
        sel[b, b, :] = 1.0
        selc[:, b * 4 + b] = 1.0
    return dict(c_ident=ident, c_rope=rope, c_rope_s=rope_s, c_bias=bias, c_hg=hg, c_sel=sel, c_selc=selc)


def kernel(x_prompt, x_sample, cache_win_k, cache_win_v, state_hgrn, cache_mem_k, cache_mem_v, mem_prompt,
           w_in, w_mem_kv, hgrn_lb_raw, hgrn_norm_g, w_out, ln_g, ln_b):
    global _NC
    if _NC is None:
        _NC = build_nc()
    f = lambda a: np.ascontiguousarray(np.asarray(a, dtype=np.float32))
    x_prompt, x_sample = f(x_prompt), f(x_sample)
    in_maps = []
    for c in range(8):
        b, half = c // 2, c % 2
        xa = np.zeros((2048, 2048), np.float32)
        xa[1024:] = x_prompt[b, half * 1024:(half + 1) * 1024]
        if half == 1:
            xa[:1024] = x_prompt[b, :1024]
        m = dict(
            x_all=xa, mem=f(mem_prompt[b]), x_s=f(x_sample[4 * c:4 * c + 4, 0]),
            ck=f(np.asarray(cache_win_k)[0, 4 * c:4 * c + 4]).reshape(4, 2048, 768),
            cv=f(np.asarray(cache_win_v)[0, 4 * c:4 * c + 4]).reshape(4, 2048, 768),
            st_in=f(np.asarray(state_hgrn)[0, 4 * c:4 * c + 4]),
            cmk=f(np.asarray(cache_mem_k)[0, 4 * c:4 * c + 4]).reshape(4, 256, 512),
            cmv=f(np.asarray(cache_mem_v)[0, 4 * c:4 * c + 4]).reshape(4, 256, 512),
            w_in=f(np.asarray(w_in)[0]), w_mem=f(np.asarray(w_mem_kv)[0]), w_out=f(np.asarray(w_out)[0]),
            lb_raw=f(hgrn_lb_raw), norm_g=f(hgrn_norm_g), ln_g=f(ln_g), ln_b=f(ln_b),
        )
        m.update(_consts(half))
        in_maps.append(m)
    res = run_bass_kernel_spmd(_NC, in_maps, core_ids=list(range(8)))
    R = res.results
    y_p = np.zeros((4, 2048, 2048), np.float32)
    y_s = np.zeros((32, 1, 2048), np.float32)
    pk = np.zeros((1, 4, 2048, 6, 128), np.float32)
    pv = np.zeros((1, 4, 2048, 6, 128), np.float32)
    pst = np.zeros((1, 4, 6, 128, 128), np.float32)
    pmk = np.zeros((1, 4, 256, 4, 128), np.float32)
    pmv = np.zeros((1, 4, 256, 4, 128), np.float32)
    sk = np.zeros((1, 32, 1, 6, 128), np.float32)
    sv = np.zeros((1, 32, 1, 6, 128), np.float32)
    sst = np.zeros((1, 32, 6, 128, 128), np.float32)
    for c in range(8):
        b, half = c // 2, c % 2
        r = R[c]
        sl = slice(half * 1024, (half + 1) * 1024)
        y_p[b, sl] = r["y"]
        y_s[4 * c:4 * c + 4, 0] = r["ys"]
        pk[0, b, sl] = np.asarray(r["pk"]).reshape(1024, 6, 128)
        pv[0, b, sl] = np.asarray(r["pv"]).reshape(1024, 6, 128)
        if half == 1:
            pst[0, b] = r["pstate"]
        else:
            pmk[0, b] = np.asarray(r["pmk"]).reshape(256, 4, 128)
            pmv[0, b] = np.asarray(r["pmv"]).reshape(256, 4, 128)
        sk[0, 4 * c:4 * c + 4, 0] = np.asarray(r["sk"]).reshape(4, 6, 128)
        sv[0, 4 * c:4 * c + 4, 0] = np.asarray(r["sv"]).reshape(4, 6, 128)
        sst[0, 4 * c:4 * c + 4] = r["sstate"]
    return (y_p, y_s, pk, pv, pst, pmk, pmv, sk, sv, sst)
```
